# Optimizing a Trainium2 kernel written in Bass

```python
import math
import jax, jax.numpy as jnp
from jax import lax
import numpy as np

D_MODEL = 1024
BATCH = 8
SEQ = 2048
DEPTH = 2
DEC_BATCH = 128
DEC_SEQ = 1
PAST_LEN = 16384
PAGE_SIZE = 128

F32 = jnp.float32
EPS = 1e-6
CHUNK = 64
DN_HEADS = 4
DN_DK = 128
DN_DV = 128
CONV_W = 4
GLA_HEADS = 4
GLA_DK = 64
GLA_DV = 128
GLA_RANK = 16
GLA_TAU = 16.0
RW_HEADS = 8
RW_N = 64
RW_DECAY_RANK = 64
RW_A_RANK = 64
RW_LN_EPS = 64e-5

DN_QK = DN_HEADS * DN_DK
DN_V = DN_HEADS * DN_DV
DN_CONV_DIM = 2 * DN_QK + DN_V
GLA_QK = GLA_HEADS * GLA_DK
GLA_V = GLA_HEADS * GLA_DV
RW_C = RW_HEADS * RW_N
RW_SHIFT_DIM = 3 * RW_C + RW_DECAY_RANK + RW_A_RANK
D_MIX = DN_V + GLA_V + RW_C
DN_PROJ = DN_CONV_DIM + DN_V + 2 * DN_HEADS
GLA_PROJ = 2 * GLA_QK + 2 * GLA_V + GLA_RANK
RW_PROJ = RW_SHIFT_DIM + RW_C
D_PROJ = DN_PROJ + GLA_PROJ + RW_PROJ

kernel_name = 'hybrid_deltanet_gla_rwkv7_decoder_step'


def rmsnorm(x, g, eps=EPS):
    xf = x.astype(F32)
    return xf * lax.rsqrt(jnp.mean(xf * xf, axis=-1, keepdims=True) + eps) * g.astype(F32)


def l2norm(x, eps=EPS):
    return x * lax.rsqrt(jnp.sum(x * x, axis=-1, keepdims=True) + eps)


def _chunkify(x, C):
    B, T = x.shape[:2]
    n = -(-T // C)
    x = jnp.pad(x, [(0, 0), (0, n * C - T)] + [(0, 0)] * (x.ndim - 2))
    x = x.reshape((B, n, C) + x.shape[2:])
    return jnp.swapaxes(jnp.moveaxis(x, 1, 0), 2, 3)


def _unchunk(o, T):
    o = jnp.moveaxis(jnp.swapaxes(o, 2, 3), 0, 1)
    B, n, C = o.shape[:3]
    return o.reshape((B, n * C) + o.shape[3:])[:, :T]


def gated_delta_rule(q, k, v, beta, g, S0):
    T = q.shape[1]
    C = min(CHUNK, T)
    causal = jnp.tril(jnp.ones((C, C), bool))
    strict = jnp.tril(jnp.ones((C, C), bool), -1)
    eye = jnp.eye(C, dtype=F32)

    def step(S, inp):
        q_, k_, v_, b_, g_ = inp
        G = jnp.cumsum(g_, axis=-1)
        decay = jnp.exp(jnp.where(causal, G[..., :, None] - G[..., None, :], -jnp.inf))
        kb = k_ * b_[..., None]
        L = jnp.where(strict, jnp.einsum('bhik,bhjk->bhij', kb, k_) * decay, 0.0)
        rhs = jnp.concatenate([v_ * b_[..., None], kb * jnp.exp(G)[..., None]], axis=-1)
        sol = lax.linalg.triangular_solve(L + eye, rhs, left_side=True, lower=True, unit_diagonal=True)
        u, w = sol[..., :DN_DV], sol[..., DN_DV:]
        v_new = u - jnp.einsum('bhck,bhkv->bhcv', w, S)
        A = jnp.einsum('bhik,bhjk->bhij', q_, k_) * decay
        o = (jnp.einsum('bhck,bhkv->bhcv', q_ * jnp.exp(G)[..., None], S)
             + jnp.einsum('bhij,bhjv->bhiv', A, v_new))
        G_last = G[..., -1]
        S = (S * jnp.exp(G_last)[..., None, None]
             + jnp.einsum('bhck,bhcv->bhkv', k_ * jnp.exp(G_last[..., None] - G)[..., None], v_new))
        return S, o

    xs = tuple(_chunkify(a, C) for a in (q, k, v, beta, g))
    S, o = lax.scan(step, S0.astype(F32), xs)
    return _unchunk(o, T), S


def gla_chunked(q, k, v, lf, S0):
    T = q.shape[1]
    C = min(CHUNK, T)
    causal = jnp.tril(jnp.ones((C, C), bool))

    def step(S, inp):
        q_, k_, v_, f_ = inp
        G = jnp.cumsum(f_, axis=2)
        diff = G[:, :, :, None, :] - G[:, :, None, :, :]
        rel = jnp.exp(jnp.where(causal[:, :, None], diff, -jnp.inf))
        A = jnp.einsum('bhik,bhjk,bhijk->bhij', q_, k_, rel)
        o = (jnp.einsum('bhik,bhkv->bhiv', q_ * jnp.exp(G), S)
             + jnp.einsum('bhij,bhjv->bhiv', A, v_))
        G_last = G[:, :, -1:]
        S = (S * jnp.exp(G_last[:, :, 0])[..., None]
             + jnp.einsum('bhjk,bhjv->bhkv', k_ * jnp.exp(G_last - G), v_))
        return S, o

    xs = tuple(_chunkify(a, C) for a in (q, k, v, lf))
    S, o = lax.scan(step, S0.astype(F32), xs)
    return _unchunk(o, T), S


def rwkv7_scan(r, w, k, v, kk, a, S0):
    def step(S, inp):
        r_, w_, k_, v_, kk_, a_ = inp
        Skk = jnp.einsum('bhvk,bhk->bhv', S, kk_)
        S = (S * w_[:, :, None, :] - Skk[..., None] * (kk_ * a_)[:, :, None, :]
             + v_[..., None] * k_[:, :, None, :])
        return S, jnp.einsum('bhvk,bhk->bhv', S, r_)

    xs = tuple(jnp.moveaxis(t, 1, 0) for t in (r, w, k, v, kk, a))
    S, o = lax.scan(step, S0.astype(F32), xs)
    return jnp.moveaxis(o, 0, 1), S


def hybrid_layer(x, c, conv_st, dn_st, gla_st, rs_st, rw_st,
                 norm_g, ada_w, ada_b, w_in, dn_conv_w, dn_a_log, dn_dt_bias, dn_norm_g,
                 gla_wf, gla_bf, gla_norm_g, rw_mu, rw_w0, rw_w2, rw_a0, rw_a2,
                 rw_k_k, rw_k_a, rw_r_k, rw_ln_w, rw_ln_b, w_out):
    B, T, _ = x.shape
    mod = jax.nn.silu(c.astype(F32)) @ ada_w + ada_b
    shift, scale, gate = jnp.split(mod, 3, axis=-1)
    h = rmsnorm(x, norm_g) * (1.0 + scale[:, None]) + shift[:, None]
    proj = jnp.einsum('btd,de->bte', h, w_in)
    dn_p = proj[..., :DN_PROJ]
    gla_p = proj[..., DN_PROJ:DN_PROJ + GLA_PROJ]
    rw_p = proj[..., DN_PROJ + GLA_PROJ:]

    qkv = dn_p[..., :DN_CONV_DIM]
    z_dn = dn_p[..., DN_CONV_DIM:DN_CONV_DIM + DN_V].reshape(B, T, DN_HEADS, DN_DV)
    b_raw = dn_p[..., DN_CONV_DIM + DN_V:DN_CONV_DIM + DN_V + DN_HEADS]
    a_raw = dn_p[..., DN_CONV_DIM + DN_V + DN_HEADS:]
    full = jnp.concatenate([conv_st.astype(F32), qkv], axis=1)
    conv = full[:, :T] * dn_conv_w[0]
    for j in range(1, CONV_W):
        conv = conv + full[:, j:j + T] * dn_conv_w[j]
    conv = jax.nn.silu(conv)
    conv_new = full[:, T:]
    dq = l2norm(conv[..., :DN_QK].reshape(B, T, DN_HEADS, DN_DK)) * (DN_DK ** -0.5)
    dk = l2norm(conv[..., DN_QK:2 * DN_QK].reshape(B, T, DN_HEADS, DN_DK))
    dv = conv[..., 2 * DN_QK:].reshape(B, T, DN_HEADS, DN_DV)
    beta = jax.nn.sigmoid(b_raw)
    g = -jnp.exp(dn_a_log.astype(F32)) * jax.nn.softplus(a_raw + dn_dt_bias)
    o_dn, dn_new = gated_delta_rule(dq, dk, dv, beta, g, dn_st)
    o_dn = (rmsnorm(o_dn, dn_norm_g) * jax.nn.silu(z_dn)).reshape(B, T, DN_V)

    gq = gla_p[..., :GLA_QK].reshape(B, T, GLA_HEADS, GLA_DK) * (GLA_DK ** -0.5)
    gk = gla_p[..., GLA_QK:2 * GLA_QK].reshape(B, T, GLA_HEADS, GLA_DK)
    gv = gla_p[..., 2 * GLA_QK:2 * GLA_QK + GLA_V].reshape(B, T, GLA_HEADS, GLA_DV)
    gz = gla_p[..., 2 * GLA_QK + GLA_V:2 * GLA_QK + 2 * GLA_V].reshape(B, T, GLA_HEADS, GLA_DV)
    glo = gla_p[..., 2 * GLA_QK + 2 * GLA_V:]
    lf = (jax.nn.log_sigmoid(glo @ gla_wf + gla_bf) / GLA_TAU).reshape(B, T, GLA_HEADS, GLA_DK)
    o_gla, gla_new = gla_chunked(gq, gk, gv, lf, gla_st)
    o_gla = (rmsnorm(o_gla, gla_norm_g) * jax.nn.silu(gz)).reshape(B, T, GLA_V)

    xs = rw_p[..., :RW_SHIFT_DIM]
    rz = rw_p[..., RW_SHIFT_DIM:]
    prev = jnp.concatenate([rs_st[:, None].astype(F32), xs[:, :-1]], axis=1)
    rs_new = xs[:, -1]
    xm = xs + (prev - xs) * rw_mu
    r = xm[..., :RW_C]
    k = xm[..., RW_C:2 * RW_C]
    v = xm[..., 2 * RW_C:3 * RW_C]
    wlo = xm[..., 3 * RW_C:3 * RW_C + RW_DECAY_RANK]
    alo = xm[..., 3 * RW_C + RW_DECAY_RANK:]
    w_log = -jax.nn.softplus(-(rw_w0 + jnp.tanh(wlo) @ rw_w2)) - 0.5
    decay = jnp.exp(-jnp.exp(w_log))
    a = jax.nn.sigmoid(rw_a0 + alo @ rw_a2)
    hs = (B, T, RW_HEADS, RW_N)
    kk = l2norm((k * rw_k_k).reshape(hs))
    k = k * (1.0 + (a - 1.0) * rw_k_a)
    r, k, v, decay, a = (t.reshape(hs) for t in (r, k, v, decay, a))
    o_rw, rw_new = rwkv7_scan(r, decay, k, v, kk, a, rw_st)
    mu = jnp.mean(o_rw, axis=-1, keepdims=True)
    var = jnp.mean(jnp.square(o_rw - mu), axis=-1, keepdims=True)
    o_rw = ((o_rw - mu) * lax.rsqrt(var + RW_LN_EPS)).reshape(B, T, RW_C) * rw_ln_w + rw_ln_b
    bonus = (jnp.sum(r * k * rw_r_k, axis=-1, keepdims=True) * v).reshape(B, T, RW_C)
    o_rw = (o_rw + bonus) * jax.nn.silu(rz)

    o = jnp.concatenate([o_dn, o_gla, o_rw], axis=-1)
    x = x + gate[:, None] * jnp.einsum('bte,ed->btd', o, w_out)
    return x, (conv_new, dn_new, gla_new, rs_new, rw_new)


def setup_inputs(seed: int = 0) -> dict:
    key = jax.random.key(seed)
    ks = jax.random.split(key, 32)

    def nrm(i, shape, s):
        return s * jax.random.normal(ks[i], shape, F32)

    def uni(i, shape, lo, hi):
        return jax.random.uniform(ks[i], shape, F32, lo, hi)

    L, D = DEPTH, D_MODEL
    dt = jnp.exp(uni(15, (L, DN_HEADS), math.log(1e-3), math.log(1e-1)))
    return {
        'x_prompt': nrm(0, (BATCH, SEQ, D), 1.0),
        'x_sample': nrm(1, (DEC_BATCH, DEC_SEQ, D), 1.0),
        'c_prompt': nrm(2, (BATCH, D), 1.0),
        'c_sample': nrm(3, (DEC_BATCH, D), 1.0),
        'state_dn_conv': nrm(4, (L, DEC_BATCH, CONV_W - 1, DN_CONV_DIM), 1.0),
        'state_dn': nrm(5, (L, DEC_BATCH, DN_HEADS, DN_DK, DN_DV), 0.05),
        'state_gla': nrm(6, (L, DEC_BATCH, GLA_HEADS, GLA_DK, GLA_DV), 0.1),
        'state_rwkv_shift': nrm(7, (L, DEC_BATCH, RW_SHIFT_DIM), 1.0),
        'state_rwkv': nrm(8, (L, DEC_BATCH, RW_HEADS, RW_N, RW_N), 0.05),
        'norm_g': 1.0 + nrm(9, (L, D), 0.05),
        'ada_w': nrm(10, (L, D, 3 * D), 0.5 * D ** -0.5),
        'ada_b': nrm(11, (L, 3 * D), 0.02),
        'w_in': nrm(12, (L, D, D_PROJ), D ** -0.5),
        'dn_conv_w': nrm(13, (L, CONV_W, DN_CONV_DIM), CONV_W ** -0.5),
        'dn_a_log': jnp.log(uni(14, (L, DN_HEADS), 1.0, 16.0)),
        'dn_dt_bias': dt + jnp.log(-jnp.expm1(-dt)),
        'dn_norm_g': 1.0 + nrm(16, (L, DN_DV), 0.05),
        'gla_wf': nrm(17, (L, GLA_RANK, GLA_QK), GLA_RANK ** -0.5),
        'gla_bf': 1.0 + nrm(18, (L, GLA_QK), 0.5),
        'gla_norm_g': 1.0 + nrm(19, (L, GLA_DV), 0.05),
        'rw_mu': uni(20, (L, RW_SHIFT_DIM), 0.0, 1.0),
        'rw_w0': uni(21, (L, RW_C), -6.0, -1.0),
        'rw_w2': nrm(22, (L, RW_DECAY_RANK, RW_C), 0.1 * RW_DECAY_RANK ** -0.5),
        'rw_a0': nrm(23, (L, RW_C), 0.1),
        'rw_a2': nrm(24, (L, RW_A_RANK, RW_C), RW_A_RANK ** -0.5),
        'rw_k_k': 0.85 + nrm(25, (L, RW_C), 0.05),
        'rw_k_a': 1.0 + nrm(26, (L, RW_C), 0.05),
        'rw_r_k': nrm(27, (L, RW_HEADS, RW_N), 0.1),
        'rw_ln_w': 1.0 + nrm(28, (L, RW_C), 0.05),
        'rw_ln_b': nrm(29, (L, RW_C), 0.02),
        'w_out': nrm(30, (L, D_MIX, D), D_MIX ** -0.5),
        'final_norm_g': 1.0 + nrm(31, (D,), 0.05),
    }


def reference(x_prompt, x_sample, c_prompt, c_sample, state_dn_conv, state_dn, state_gla,
              state_rwkv_shift, state_rwkv, norm_g, ada_w, ada_b, w_in, dn_conv_w, dn_a_log,
              dn_dt_bias, dn_norm_g, gla_wf, gla_bf, gla_norm_g, rw_mu, rw_w0, rw_w2, rw_a0,
              rw_a2, rw_k_k, rw_k_a, rw_r_k, rw_ln_w, rw_ln_b, w_out, final_norm_g):
    def trunk(x, c, conv_s, dn_s, gla_s, rs_s, rw_s):
        hcur = x.astype(F32)
        outs = []
        for l in range(DEPTH):
            hcur, st = hybrid_layer(
                hcur, c, conv_s[l], dn_s[l], gla_s[l], rs_s[l], rw_s[l],
                norm_g[l], ada_w[l], ada_b[l], w_in[l], dn_conv_w[l], dn_a_log[l], dn_dt_bias[l],
                dn_norm_g[l], gla_wf[l], gla_bf[l], gla_norm_g[l], rw_mu[l], rw_w0[l], rw_w2[l],
                rw_a0[l], rw_a2[l], rw_k_k[l], rw_k_a[l], rw_r_k[l], rw_ln_w[l], rw_ln_b[l], w_out[l])
            outs.append(st)
        y = rmsnorm(hcur, final_norm_g).astype(x.dtype)
        n_conv = jnp.stack([s[0] for s in outs])
        n_dn = jnp.stack([s[1] for s in outs])
        n_gla = jnp.stack([s[2] for s in outs])
        n_rs = jnp.stack([s[3] for s in outs])
        n_rw = jnp.stack([s[4] for s in outs])
        return y, n_conv, n_dn, n_gla, n_rs, n_rw

    Bp = x_prompt.shape[0]
    y_prompt, p_conv, p_dn, p_gla, p_rs, p_rw = trunk(
        x_prompt, c_prompt,
        jnp.zeros((DEPTH, Bp, CONV_W - 1, DN_CONV_DIM), F32),
        jnp.zeros((DEPTH, Bp, DN_HEADS, DN_DK, DN_DV), F32),
        jnp.zeros((DEPTH, Bp, GLA_HEADS, GLA_DK, GLA_DV), F32),
        jnp.zeros((DEPTH, Bp, RW_SHIFT_DIM), F32),
        jnp.zeros((DEPTH, Bp, RW_HEADS, RW_N, RW_N), F32))
    y_sample, s_conv, s_dn, s_gla, s_rs, s_rw = trunk(
        x_sample, c_sample, state_dn_conv, state_dn, state_gla, state_rwkv_shift, state_rwkv)
    return (y_prompt, y_sample, p_conv, p_dn, p_gla, p_rs, p_rw, s_conv, s_dn, s_gla, s_rs, s_rw)
```

```python
import numpy as np
import concourse.bass as bass
import concourse.mybir as mybir
from concourse.bass_utils import run_bass_kernel_spmd

F32 = mybir.dt.float32
BF16 = mybir.dt.bfloat16
AF = mybir.ActivationFunctionType
ALU = mybir.AluOpType
AX = mybir.AxisListType

D = 1024
NSAMP = 16
NCORES = 8
EPS = 1e-6
ENGS = ("pe", "dve", "act", "pool", "sp")
N_DMA_SEMS = 32
SEM_LIMIT = 30000
USE_R = True
BATCH_S = True
SAME_ENGINE_SYNC = True

(C_ID, C_ONES, C_TRIU, C_SH1, C_SH2, C_SH3, C_BD1, C_BD2, C_BD3, C_SELB128, C_A1, C_SELB1, C_E0, C_SELP,
 C_TRIUS, C_NTRIL64, C_NTRIU64, C_LOWLEFT, C_MLOW, C_MUP) = range(20)
C_I16 = 20
NCM = 22
NCMR = 14
F32R = mybir.dt.float32r


def make_consts(NS=NSAMP):
    cm = np.zeros((128, NCM, 128), np.float32)
    i = np.arange(128)[:, None]
    j = np.arange(128)[None, :]
    same = (i // 64) == (j // 64)
    cm[:, C_ID] = (i == j)
    cm[:, C_ONES] = 1.0
    cm[:, C_TRIU] = (i <= j)
    cm[:, C_TRIUS] = (i < j)
    cm[:, C_NTRIL64] = -1.0 * ((i > j) & same)
    cm[:, C_NTRIU64] = -1.0 * ((i < j) & same)
    cm[:, C_LOWLEFT] = ((i >= 64) & (j < 64))
    cm[:, C_MLOW] = np.where(i >= j, 0.0, -1e30)
    cm[:, C_MUP] = np.where(j >= i, 0.0, -1e30)
    cm[:, C_SH1] = (i == j - 1)
    cm[:, C_SH2] = (i == j - 2)
    cm[:, C_SH3] = (i == j - 3)
    for d, c in ((1, C_BD1), (2, C_BD2), (3, C_BD3)):
        cm[:, c] = ((i == 3 + j - d) & (i < 3) & (j < d))
    cm[:, C_SELB128] = ((i == 125 + j) & (j < 3))
    cm[:, C_A1] = ((i == j + 1) & (i < 3) & (j < 2))
    cm[0, C_SELB1, 2] = 1.0
    cm[0, C_E0, 0] = 1.0
    cm[NS, C_SELP, :] = 1.0
    cm[:, C_I16:C_I16 + 2, :] = np.eye(16, dtype=np.float32).reshape(1, 2, 128)
    return cm


class Buf:
    __slots__ = ("last_w", "reads", "excl")

    def __init__(self, excl):
        self.last_w = None
        self.reads = []
        self.excl = excl


class Sched:
    def __init__(self, nc):
        self.nc = nc
        self.streams = {e: [] for e in ENGS}
        self.gen = {e: 0 for e in ENGS}
        self.sems = {}
        for e in ENGS:
            self.sems[(e, 0)] = nc.alloc_semaphore(name=f"s_{e}0")
        self.cnt = {e: 0 for e in ENGS}
        self.dsems = [nc.alloc_semaphore(name=f"s_dma{i}") for i in range(N_DMA_SEMS)]
        self.dcnt = [0] * N_DMA_SEMS
        self.dnext = 0
        self.dnext2 = 0
        self.dnext3 = 0
        self.seen = {e: {} for e in ENGS}
        self.bufs = {}
        self.n_instr = 0
        self.n_wait = 0

    def buf(self, key):
        b = self.bufs.get(key)
        if b is None:
            b = self.bufs[key] = Buf(key.startswith("ps"))
        return b

    def _sem(self, key):
        return self.dsems[key] if isinstance(key, int) else self.sems[key]

    def _deps(self, eng, rb, wb):
        need = {}

        def add(ev):
            if ev is None:
                return
            k, v = ev
            if not isinstance(k, int) and k[0] == eng and (eng == "pe" or not SAME_ENGINE_SYNC):
                return
            if need.get(k, 0) < v:
                need[k] = v

        for b in rb:
            add(b.last_w)
            if b.excl:
                for r in b.reads:
                    add(r)
        for b in wb:
            add(b.last_w)
            for r in b.reads:
                add(r)
        out = []
        seen = self.seen[eng]
        for k, v in need.items():
            if seen.get(k, 0) < v:
                seen[k] = v
                out.append((self._sem(k), v))
        return out

    def _commit(self, ev, rb, wb):
        for b in rb:
            if b.excl:
                b.last_w = ev
                b.reads = []
            else:
                b.reads.append(ev)
                if len(b.reads) > 48:
                    best = {}
                    for k, v in b.reads:
                        if best.get(k, 0) < v:
                            best[k] = v
                    b.reads = list(best.items())
        for b in wb:
            b.last_w = ev
            b.reads = []

    def op(self, eng, fn, reads, writes, desc=None):
        if not hasattr(self, "meta"):
            self.meta = {e: [] for e in ENGS}
        rb = [self.buf(k) for k in reads]
        wb = [self.buf(k) for k in writes]
        wl = self._deps(eng, rb, wb)
        if self.cnt[eng] >= SEM_LIMIT:
            self.gen[eng] += 1
            self.cnt[eng] = 0
            self.sems[(eng, self.gen[eng])] = self.nc.alloc_semaphore(name=f"s_{eng}{self.gen[eng]}")
        self.cnt[eng] += 1
        key = (eng, self.gen[eng])
        ev = (key, self.cnt[eng])
        sem = self.sems[key]
        self.n_wait += len(wl)
        self.n_instr += 1
        self.meta[eng].append((len(wl), desc))

        def emit(e, fn=fn, wl=wl, sem=sem):
            for s, v in wl:
                e.wait_ge(s, v)
            fn(e).then_inc(sem, 1)

        self.streams[eng].append(emit)
        self._commit(ev, rb, wb)

    def dma(self, q, out, in_, reads, writes):
        rb = [self.buf(k) for k in reads]
        wb = [self.buf(k) for k in writes]
        wl = self._deps(q, rb, wb)
        if q == "sp":
            j = self.dnext
            self.dnext = (self.dnext + 1) % 16
        elif q == "pool":
            j = 16 + self.dnext2
            self.dnext2 = (self.dnext2 + 1) % 8
        else:
            j = 24 + self.dnext3
            self.dnext3 = (self.dnext3 + 1) % 8
        prev = self.dcnt[j]
        if prev > 0 and self.seen[q].get(j, 0) < prev:
            self.seen[q][j] = prev
            wl.append((self.dsems[j], prev))
        self.dcnt[j] += 16
        ev = (j, self.dcnt[j])
        dsem = self.dsems[j]
        self.n_wait += len(wl)
        self.n_instr += 1

        def emit(e, wl=wl, dsem=dsem, out=out, in_=in_):
            for s, v in wl:
                e.wait_ge(s, v)
            e.dma_start(out=out, in_=in_).then_inc(dsem, 16)

        self.streams[q].append(emit)
        self._commit(ev, rb, wb)

    def finish(self):
        wl = [(self.dsems[j], self.dcnt[j]) for j in range(N_DMA_SEMS) if self.dcnt[j] > 0]
        for e in ENGS:
            if e not in ("sp",) and self.cnt[e] > 0:
                wl.append((self.sems[(e, self.gen[e])], self.cnt[e]))

        def emit(e, wl=wl):
            for s, v in wl:
                e.wait_ge(s, v)

        self.streams["sp"].append(emit)
        streams = self.streams
        with self.nc.Block() as block:
            @block.tensor
            def _(e):
                for f in streams["pe"]:
                    f(e)

            @block.vector
            def _(e):
                for f in streams["dve"]:
                    f(e)

            @block.scalar
            def _(e):
                for f in streams["act"]:
                    f(e)

            @block.gpsimd
            def _(e):
                for f in streams["pool"]:
                    f(e)

            @block.sync
            def _(e):
                for f in streams["sp"]:
                    f(e)


def _isap(v):
    return hasattr(v, "tensor") and hasattr(v, "ap")


class K:
    def __init__(self, nc):
        self.nc = nc
        self.S = Sched(nc)
        self.banks = [nc.alloc_psum_tensor(f"ps{i}", [128, 512], F32) for i in range(8)]
        self.nb = 0
        self.nrot = 7
        self.ncp = 0
        self.R = set()

    def bank(self):
        b = self.banks[self.nb % self.nrot]
        self.nb += 1
        return b

    def e(self, eng, method, _keys=None, **kw):
        reads, writes = [], []
        for name, val in list(kw.items()):
            if _isap(val):
                if _keys and name in _keys:
                    kk = _keys[name]
                    (writes if name in ("out", "accum_out", "ap") else reads).extend(kk if isinstance(kk, list) else [kk])
                elif name in ("out", "accum_out", "ap"):
                    writes.append(val.name)
                    if val.name in self.R and val.dtype == F32 and name != "ap":
                        kw[name] = val.bitcast(F32R)
                else:
                    reads.append(val.name)
        desc = method + " " + " ".join(f"{n_}={tuple(v_.shape)}:{v_.dtype}@{v_.name}" for n_, v_ in kw.items() if _isap(v_))
        self.S.op(eng, lambda e: getattr(e, method)(**kw), reads, writes, desc)

    def dma(self, out, in_, q="sp", wkey=None):
        self.S.dma(q, out, in_, [in_.name], [wkey if wkey is not None else out.name])

    def mm(self, out, lhsT, rhs, start=True, stop=True, skip=False, rkey=None):
        if rkey is not None:
            self.e("pe", "matmul", _keys={"rhs": rkey}, out=out, lhsT=lhsT, rhs=rhs, start=start, stop=stop)
            return
        if (lhsT.name in self.R and rhs.name in self.R and lhsT.dtype == F32 and rhs.dtype == F32
                and lhsT.shape[0] >= 2 and lhsT.shape[-1] >= 2 and rhs.shape[-1] >= 2):
            lhsT = lhsT.bitcast(F32R)
            rhs = rhs.bitcast(F32R)
        if skip:
            self.e("pe", "matmul", out=out, lhsT=lhsT, rhs=rhs, start=start, stop=stop, skip_group_check=True)
        else:
            self.e("pe", "matmul", out=out, lhsT=lhsT, rhs=rhs, start=start, stop=stop)

    def tr(self, out, in_, ident):
        self.e("pe", "transpose", out=out, in_=in_, identity=ident)

    def tt(self, out, a, b, op, eng="dve"):
        self.e(eng, "tensor_tensor", out=out, in0=a, in1=b, op=op)

    def ts(self, out, a, s1, op0, s2=None, op1=None, eng="dve"):
        if op1 is None:
            self.e(eng, "tensor_scalar", out=out, in0=a, scalar1=s1, scalar2=None, op0=op0)
        else:
            self.e(eng, "tensor_scalar", out=out, in0=a, scalar1=s1, scalar2=s2, op0=op0, op1=op1)

    def stt(self, out, a, s, b, op0, op1, eng="dve"):
        self.e(eng, "scalar_tensor_tensor", out=out, in0=a, scalar=s, in1=b, op0=op0, op1=op1)

    def red(self, out, in_, op=ALU.add):
        self.e("dve", "tensor_reduce", out=out, in_=in_, axis=AX.X, op=op)

    def recip(self, out, in_):
        self.e("dve", "reciprocal", out=out, in_=in_)

    def act(self, out, in_, func, bias=None, scale=None):
        kw = dict(out=out, in_=in_, func=func)
        if bias is not None:
            kw["bias"] = bias
        if scale is not None:
            kw["scale"] = scale
        self.e("act", "activation", **kw)

    def cp(self, out, in_, eng=None, okey=None, ikey=None):
        if eng is None:
            eng = ("dve", "act")[self.ncp % 2]
            self.ncp += 1
        kk = {"out": okey} if okey is not None else None
        if ikey is not None:
            kk = dict(kk or {}, in_=ikey)
        if eng == "act":
            self.e("act", "copy", _keys=kk, out=out, in_=in_)
        else:
            self.e(eng, "tensor_copy", _keys=kk, out=out, in_=in_)

    def memset(self, ap, val, eng="dve"):
        self.e(eng, "memset", ap=ap, constant=val)


class Ref:
    def __init__(self, t):
        self.t = t

    def __getitem__(self, key):
        return self.t[key]


def v3(ap, a):
    return ap.rearrange("p (a b) -> p a b", a=a)


def bc(ap, shape):
    return ap.to_broadcast(list(shape))


DN_C0, DN_W = 0, 2056
GLA_C0, GLA_W = 2056, 1552
RW_C0, RW_W = 3608, 2176
NG = 27


def build(SEQ=2048, NS=NSAMP, LAYERS=2, PHASES=("dn", "gla", "rw", "out")):
    nc = bass.Bass("TRN2", target_bir_lowering=False)
    NT = SEQ // 128
    NTOK = SEQ + NS

    def din(name, shape):
        return nc.dram_tensor(name, list(shape), F32, kind="ExternalInput").ap()

    def dout(name, shape):
        return nc.dram_tensor(name, list(shape), F32, kind="ExternalOutput").ap()

    xp = din("xp", [SEQ, D]); xs = din("xs", [NS, D]); call = din("call", [NS + 1, D])
    st_conv = din("st_conv", [2, NS, 3, 1536]); st_dn = din("st_dn", [2, NS, 4, 128, 128])
    st_gla = din("st_gla", [2, NS, 4, 64, 128]); st_rs = din("st_rs", [2, NS, 1664])
    st_rw = din("st_rw", [2, NS, 8, 64, 64])
    cmd = din("cm", [128, NCM, 128])
    norm_g = din("norm_g", [2, D]); ada_w = din("ada_w", [2, D, 3 * D]); ada_b = din("ada_b", [2, 3 * D])
    w_in = din("w_in", [2, D, 5784]); dn_conv_w = din("dn_conv_w", [2, 4, 1536])
    dn_a_log = din("dn_a_log", [2, 4]); dn_dt_bias = din("dn_dt_bias", [2, 4]); dn_norm_g = din("dn_norm_g", [2, 128])
    gla_wf = din("gla_wf", [2, 16, 256]); gla_bf = din("gla_bf", [2, 256]); gla_norm_g = din("gla_norm_g", [2, 128])
    rw_mu = din("rw_mu", [2, 1664]); rw_w0 = din("rw_w0", [2, 512]); rw_w2 = din("rw_w2", [2, 64, 512])
    rw_a0 = din("rw_a0", [2, 512]); rw_a2 = din("rw_a2", [2, 64, 512]); rw_k_k = din("rw_k_k", [2, 512])
    rw_k_a = din("rw_k_a", [2, 512]); rw_r_k = din("rw_r_k", [2, 512]); rw_ln_w = din("rw_ln_w", [2, 512])
    rw_ln_b = din("rw_ln_b", [2, 512]); w_out = din("w_out", [2, 1536, D]); final_norm_g = din("final_norm_g", [1, D])

    y_p = dout("y_p", [SEQ, D]); y_s = dout("y_s", [NS, D])
    p_conv = dout("p_conv", [2, 3, 1536]); p_dn = dout("p_dn", [2, 4, 128, 128]); p_gla = dout("p_gla", [2, 4, 64, 128])
    p_rs = dout("p_rs", [2, 1664]); p_rw = dout("p_rw", [2, 8, 64, 64])
    s_conv = dout("s_conv", [2, NS, 3, 1536]); s_dn = dout("s_dn", [2, NS, 4, 128, 128])
    s_gla = dout("s_gla", [2, NS, 4, 64, 128]); s_rs = dout("s_rs", [2, NS, 1664]); s_rw = dout("s_rw", [2, NS, 8, 64, 64])

    xmid = nc.dram_tensor("xmid", [NTOK, D], F32).ap()
    omix = nc.dram_tensor("omix", [NTOK, 1536], F32).ap()
    projs = nc.dram_tensor("projs", [NS, RW_W], F32).ap()
    modall = nc.dram_tensor("modall", [NS + 1, 3 * D], F32).ap()
    hts = nc.dram_tensor("hts", [NT + 1, 128, 8 * 128], BF16).ap()

    k = K(nc)
    sb = nc.alloc_sbuf_tensor

    CM = sb("CM", [128, NCM, 128], F32)
    identb = sb("identb", [128, 128], BF16)
    W = sb("W", [128, 8 * RW_W], BF16)
    stage = [sb(f"stage{i}", [128, 1088], F32) for i in range(2)]
    modseg = sb("modseg", [128, 2 * D], F32)
    PC = sb("PC", [128, 6912], F32)
    x_t = sb("x_t", [128, D], F32)
    hb = sb("hb", [128, D], BF16)
    HB = [sb("hT", [128, 8, 128], BF16), sb("hT1", [128, 8, 128], BF16)]
    PB = [sb("proj", [128, RW_W], F32), sb("proj1", [128, RW_W], F32)]
    hT = Ref(HB[0])
    proj = Ref(PB[0])
    k.pre_tail = None
    k.post_tail = None
    k.next_load = None
    k.preloaded = None
    cq = sb("cq", [128, 1664], F32)
    cT = sb("cT", [128, 8, NS + 1], F32)
    TLS = sb("TLS", [3, 1664], F32)
    TL = TLS[0:3, 0:1536]
    srow = TLS[0:1, 0:1664]
    S_dn = sb("S_dn", [128, 4, 128], F32)
    S_gla = sb("S_gla", [128, 2, 128], F32)
    M_rw = sb("M_rw", [128, 4, 64], F32)
    sm = sb("sm", [128, 128], F32)
    sm2 = sb("sm2", [128, 64], F32)
    wsm = sb("wsm", [64, 1024], F32)
    ob = modseg[:, 1024:1792].bitcast(BF16)
    XB = [x_t, sb("x_t2", [128, D], F32)]
    oT = sb("oT", [128, 12, 128], BF16)
    G = [sb(f"g{i}", [128, 512], F32) for i in range(NG)]
    PSO = k.banks[7]

    CMR = sb("CMR", [128, NCMR, 128], F32)
    if USE_R:
        k.R.update(["CMR", "cq", "proj", "proj1", "S_dn", "S_gla", "M_rw"] + [f"g{i}" for i in range(26)])

    def cmat(i, r, c):
        return CMR[0:r, i, 0:c] if USE_R else CM[0:r, i, 0:c]

    def cid(n):
        return CM[0:n, C_ID, 0:n]

    k.dma(CM[:], cmd)
    k.cp(identb[:], CM[:, C_ID, :], eng="dve")
    k.cp(CMR[:], CM[:, 0:NCMR, :], eng="dve")
    k.dma(x_t[0:NS + 1, :], call)
    k.act(x_t[0:NS + 1, :], x_t[0:NS + 1, :], AF.Silu)
    b_ = k.bank()
    for kc in range(8):
        k.tr(b_[:, kc * 32:kc * 32 + NS + 1], x_t[0:NS + 1, kc * 128:(kc + 1) * 128], cid(NS + 1))
    k.cp(cT[:], v3(b_[:, 0:256], 8)[:, :, 0:NS + 1], eng="dve")

    psegs = [("p", t, 128, t * 128) for t in range(NT)]
    ssegs = [("s", s, 1, SEQ + s) for s in range(NS)]
    sbatch = ("sb", 0, NS, SEQ)

    def xsrc(l, seg):
        kind, idx, n, row = seg
        if l == 0:
            return xp[idx * 128:(idx + 1) * 128, :] if kind == "p" else xs[idx:idx + n, :]
        return xmid[row:row + n, :]

    def front_into(l, seg, wcols, cache, slot, defer=False):
        save = (proj.t, hT.t)
        proj.t, hT.t = PB[slot], HB[slot]
        r = front(l, seg, wcols, cache, defer)
        proj.t, hT.t = save
        return r

    def run_post_tail():
        f = k.post_tail
        k.post_tail = None
        if f is not None:
            f()

    def run_pre_tail():
        f = k.pre_tail
        k.pre_tail = None
        if f is not None:
            k.post_tail = f()

    def mixer_pass(l, wcols, pre, core, post, batch_core=None):
        cache = None
        if "dn" in PHASES:
            cache = "store" if wcols == DN_W else "load"
        front_into(l, psegs[0], wcols, cache, 0)
        for t, seg in enumerate(psegs):
            proj.t, hT.t = PB[t % 2], HB[t % 2]
            nxt = psegs[t + 1] if t + 1 < NT else sbatch
            k.pre_tail = (lambda nxt=nxt, t=t: front_into(l, nxt, wcols, cache, (t + 1) % 2, defer=True))
            core(l, seg)
            run_pre_tail()
            run_post_tail()
            post(seg)
        proj.t, hT.t = PB[NT % 2], HB[NT % 2]
        if k.next_load is not None:
            k.next_load()
            k.next_load = None
        batch_core(l)

    def bload(dst, src_row, width):
        k.dma(dst, bc(src_row, [128, width]))

    def load_weights(l, c0, width, dst_off, rows_kc=8, src=None):
        src = w_in if src is None else src
        k.e("dve", "memset", _keys={"ap": [f"W{j}" for j in range(12)]}, ap=W[:, 0:2], constant=0.0)
        for kc in range(rows_kc):
            half = -(-width // 2)
            for a in (0, half):
                wd = min(half, width - a)
                dst = W[:, dst_off + kc * width + a: dst_off + kc * width + a + wd]
                k.dma(dst, src[l, kc * 128:(kc + 1) * 128, c0 + a:c0 + a + wd], q="pool", wkey=f"W{kc}")

    def extract_mod(lhsT, n, c0, ncols):
        if lhsT == "prompt":
            k.dma(modseg[0:n, 0:ncols], bc(modall[NS:NS + 1, c0:c0 + ncols], [n, ncols]))
        else:
            k.dma(modseg[0:n, 0:ncols], modall[0:n, c0:c0 + ncols])

    def seg_mod(seg, c0, ncols):
        kind, idx, n, row = seg
        if kind == "sb":
            extract_mod("batch", NS, c0, ncols)

    def front(l, seg, wcols, cache=None, defer=False):
        kind, idx, n, row = seg
        slot = hts[idx if kind == "p" else NT].rearrange("p (a b) -> p a b", a=8)[:, :, 0:n]
        if cache == "load":
            k.dma(hT[:, :, 0:n], slot)
        else:
            front_h(l, seg)
            if cache == "store":
                k.dma(slot, hT[:, :, 0:n], q="pool")
        c = 0
        evacs = []
        pt = proj.t
        while c < wcols:
            wd = min(512, wcols - c)
            b = k.bank()
            for kc in range(8):
                k.mm(b[0:n, 0:wd], hT[:, kc, 0:n], W[:, kc * wcols + c: kc * wcols + c + wd],
                     start=(kc == 0), stop=(kc == 7), rkey=f"W{kc}")
            evacs.append((pt[0:n, c:c + wd], b[0:n, 0:wd]))
            c += wd

        def do_evacs():
            for o_, i_ in evacs:
                k.cp(o_, i_)

        if defer:
            return do_evacs
        do_evacs()
        return None

    def front_h(l, seg):
        kind, idx, n, row = seg
        seg_mod(seg, 0, 2 * D)
        k.dma(x_t[0:n, :], xsrc(l, seg))
        hf = cq[0:n, 0:D]
        k.memset(sm[0:n, 0:1], 0.0)
        k.e("act", "activation", out=hf, in_=x_t[0:n, :], func=AF.Square, accum_out=sm[0:n, 0:1])
        k.act(sm[0:n, 1:2], sm[0:n, 0:1], AF.Sqrt, bias=EPS, scale=1.0 / D)
        k.recip(sm[0:n, 2:3], sm[0:n, 1:2])
        k.stt(hf, x_t[0:n, :], sm[0:n, 2:3], modseg[0:n, D:2 * D], ALU.mult, ALU.mult)
        k.tt(hb[0:n, :], hf, modseg[0:n, 0:D], ALU.add)
        b = k.bank()
        bb = b[:, :].bitcast(BF16)
        for kc in range(8):
            k.tr(bb[:, kc * 128:kc * 128 + n], hb[0:n, kc * 128:(kc + 1) * 128], identb[0:n, 0:n])
        k.cp(hT[:, :, 0:n], v3(bb, 8)[:, :, 0:n], eng="dve")

    def rms_gate_store(seg, nh, hd, gain, gate_in, col0, eps, src=None, otile=None, zs_pre=None):
        kind, idx, n, row = seg
        run_pre_tail()
        o3 = v3(PSO[0:n, :], nh)
        sq, on, zs = G[0], (G[1] if otile is None else otile), G[2]
        if src is None:
            k.cp(on[0:n, :], PSO[0:n, :])
        else:
            on = src
        k.tt(sq[0:n, :], on[0:n, :], on[0:n, :], ALU.mult)
        k.red(sm[0:n, 16:16 + nh], v3(sq[0:n, :], nh))
        k.act(sm[0:n, 24:24 + nh], sm[0:n, 16:16 + nh], AF.Sqrt, bias=eps, scale=1.0 / hd)
        k.recip(sm[0:n, 32:32 + nh], sm[0:n, 24:24 + nh])
        k.tt(v3(on[0:n, :], nh), v3(on[0:n, :], nh), bc(sm[0:n, 32:32 + nh].unsqueeze(2), [n, nh, hd]), ALU.mult)
        k.tt(v3(on[0:n, :], nh), v3(on[0:n, :], nh), bc(gain.unsqueeze(1), [n, nh, hd]), ALU.mult)
        if zs_pre is None:
            k.act(zs[0:n, :], gate_in, AF.Silu)
        else:
            zs = zs_pre
        k.tt(on[0:n, :], on[0:n, :], zs[0:n, :], ALU.mult)
        k.dma(omix[row:row + n, col0:col0 + 512], on[0:n, :], q="pool")
        run_post_tail()

    def neumann(n, Nbd, NbdT, Loff, tiles):
        idb = bc(CM[0:n, C_ID:C_ID + 1, 0:n], [n, 4, n])
        Xa, Xb, Pa, Pb, Qa, Qb = tiles

        def t3(t):
            return v3(t[0:n, :], 4)[:, :, 0:n]

        if n == 1:
            k.memset(t3(Xa), 1.0)
            return t3(Xa)
        X = t3(Xa)
        k.tt(X, NbdT, idb, ALU.add)
        P, Pt = Nbd, NbdT
        spare = [Xb, Pa, Pb, Qa, Qb]
        cur = {"X": Xa}
        pp = [Pa, Pb]
        qq = [Qa, Qb]
        xx = [Xb, Xa]
        for lvl in range(1, 6):
            bP = k.bank()
            for h in range(4):
                k.mm(v3(bP[0:n, :], 4)[:, h, 0:n], Pt[:, h, :], P[:, h, :])
            Pn = t3(pp[lvl % 2])
            k.cp(Pn, v3(bP[0:n, :], 4)[:, :, 0:n])
            if lvl < 5:
                bQ = k.bank()
                for h in range(4):
                    k.mm(v3(bQ[0:n, :], 4)[:, h, 0:n], P[:, h, :], Pt[:, h, :])
                Qn = t3(qq[lvl % 2])
                k.cp(Qn, v3(bQ[0:n, :], 4)[:, :, 0:n])
            bX = k.bank()
            for h in range(4):
                k.mm(v3(bX[0:n, :], 4)[:, h, 0:n], Pn[:, h, :], X[:, h, :])
            Xn = t3(xx[(lvl - 1) % 2])
            k.tt(Xn, X, v3(bX[0:n, :], 4)[:, :, 0:n], ALU.add)
            X = Xn
            P = Pn
            if lvl < 5:
                Pt = Qn
        bT = k.bank()
        for h in range(4):
            k.tr(v3(bT[0:n, :], 4)[:, h, 0:n], X[:, h, :], cid(n))
        Tbd = t3(Pa)
        k.cp(Tbd, v3(bT[0:n, :], 4)[:, :, 0:n])
        bY = k.bank()
        for h in range(4):
            k.mm(v3(bY[0:n, :], 4)[:, h, 0:n], Loff[:, h, :], X[:, h, :])
        Y = t3(Pb)
        k.cp(Y, v3(bY[0:n, :], 4)[:, :, 0:n])
        bZ = k.bank()
        for h in range(4):
            k.mm(v3(bZ[0:n, :], 4)[:, h, 0:n], Tbd[:, h, :], Y[:, h, :])
        Xf = t3(Qa)
        k.tt(Xf, X, v3(bZ[0:n, :], 4)[:, :, 0:n], ALU.subtract)
        return Xf

    def layer_mod(l):
        gi = 0
        for cc in range(6):
            b = k.bank()
            for kc in range(8):
                g = stage[gi % 2][:, (gi // 2 % 2) * 512:(gi // 2 % 2) * 512 + 512]
                gi += 1
                k.dma(g, ada_w[l, kc * 128:(kc + 1) * 128, cc * 512:(cc + 1) * 512])
                k.mm(b[0:NS + 1, :], cT[:, kc, :], g, start=(kc == 0), stop=(kc == 7))
            gb = hT[0:NS + 1, :, :].rearrange("p a b -> p (a b)").bitcast(F32)
            mo = hb[0:NS + 1, :].bitcast(F32)
            k.dma(gb, bc(ada_b[l:l + 1, cc * 512:(cc + 1) * 512], [NS + 1, 512]))
            k.tt(mo, b[0:NS + 1, :], gb, ALU.add)
            if cc in (2, 3):
                k.dma(x_t[0:NS + 1, 0:512], bc(norm_g[l:l + 1, (cc - 2) * 512:(cc - 1) * 512], [NS + 1, 512]))
                k.stt(mo, mo, 1.0, x_t[0:NS + 1, 0:512], ALU.add, ALU.mult)
            k.dma(modall[:, cc * 512:(cc + 1) * 512], mo, q="pool")

    def softplus_parts(out, x, n, w, t1, t2):
        k.act(t1, x, AF.Abs)
        k.act(t2, t1, AF.Exp, scale=-1.0)
        k.act(out, t2, AF.Ln, bias=1.0)

    def dn_phase(l):
        load_weights(l, DN_C0, DN_W, 0)
        for d_ in range(4):
            bload(PC[:, d_ * 1536:(d_ + 1) * 1536], dn_conv_w[l, 3 - d_:4 - d_, :], 1536)
        bload(PC[:, 6144:6272], dn_norm_g[l:l + 1, :], 128)
        bload(PC[:, 6272:6276], dn_a_log[l:l + 1, :], 4)
        bload(PC[:, 6276:6280], dn_dt_bias[l:l + 1, :], 4)
        k.act(PC[:, 6280:6284], PC[:, 6272:6276], AF.Exp)
        k.ts(PC[:, 6280:6284], PC[:, 6280:6284], -1.0, ALU.mult)
        extract_mod("prompt", 128, 0, 2 * D)
        k.memset(TL, 0.0)
        k.memset(S_dn[:], 0.0)
        def pre(seg):
            k.dma(TL, st_conv[l, seg[1]])
            k.dma(v3(stage[0][:, 0:512], 4), st_dn[l, seg[1]].rearrange("h k v -> k h v"))
            k.cp(S_dn[:], v3(stage[0][:, 0:512], 4), eng="dve")

        def post(seg):
            kind, idx, n, row = seg
            if kind == "s":
                k.dma(s_conv[l, idx], TL, q="pool")
                k.dma(s_dn[l, idx].rearrange("h k v -> k h v"), S_dn[:], q="pool")
            elif idx == NT - 1:
                k.dma(p_conv[l], TL, q="pool")
                k.dma(p_dn[l].rearrange("h k v -> k h v"), S_dn[:], q="pool")

        def _nl():
            load_weights(l, GLA_C0, GLA_W, 0)
            k.preloaded = (l, "gla")
        k.next_load = _nl
        mixer_pass(l, DN_W, pre, dn_segment, post, batch_core=dn_batch)

    def dn_segment(l, seg):
        kind, idx, n, row = seg
        qkv = proj[0:n, 0:1536]
        for c in range(3):
            cs = slice(c * 512, (c + 1) * 512)
            b = k.bank()
            ops = []
            for d_ in range(4):
                if n > d_:
                    g = G[d_]
                    k.tt(g[0:n, :], proj[0:n, cs], PC[0:n, d_ * 1536 + c * 512: d_ * 1536 + (c + 1) * 512], ALU.mult)
                    ops.append((cmat((C_ID, C_SH1, C_SH2, C_SH3)[d_], n, n), g[0:n, :]))
            for d_ in range(1, 4):
                g = G[3 + d_]
                k.tt(g[0:3, :], TL[0:3, cs], PC[0:3, d_ * 1536 + c * 512: d_ * 1536 + (c + 1) * 512], ALU.mult)
                ops.append((cmat((C_BD1, C_BD2, C_BD3)[d_ - 1], 3, n), g[0:3, :]))
            for i, (lt, rh) in enumerate(ops):
                k.mm(b[0:n, :], lt, rh, start=(i == 0), stop=(i == len(ops) - 1))
            k.act(cq[0:n, cs], b[0:n, :], AF.Silu)
        for c in range(3):
            cs = slice(c * 512, (c + 1) * 512)
            b = k.bank()
            if n == 128:
                k.mm(b[0:3, :], cmat(C_SELB128, 128, 3), proj[0:n, cs])
            else:
                k.mm(b[0:3, :], cmat(C_A1, 3, 3), TL[0:3, cs], start=True, stop=False)
                k.mm(b[0:3, :], cmat(C_SELB1, 1, 3), proj[0:1, cs], start=False, stop=True)
            k.cp(TL[0:3, cs], b[0:3, :])
        qn, kn, vv, beta, gg = dn_prep(n)
        dn_chunk(l, seg, qn, kn, vv, beta, gg)

    def dn_prep(n):
        sq = G[0]
        for hlf in range(2):
            k.tt(sq[0:n, :], cq[0:n, hlf * 512:(hlf + 1) * 512], cq[0:n, hlf * 512:(hlf + 1) * 512], ALU.mult)
            k.red(sm[0:n, 4 + hlf * 4:8 + hlf * 4], v3(sq[0:n, :], 4))
        k.act(sm[0:n, 12:20], sm[0:n, 4:12], AF.Sqrt, bias=EPS)
        k.recip(sm[0:n, 20:28], sm[0:n, 12:20])
        k.ts(sm[0:n, 20:24], sm[0:n, 20:24], 128.0 ** -0.5, ALU.mult)
        qk3 = v3(cq[0:n, 0:1024], 8)
        k.tt(qk3, qk3, bc(sm[0:n, 20:28].unsqueeze(2), [n, 8, 128]), ALU.mult)
        qn = cq[0:n, 0:512]
        kn = cq[0:n, 512:1024]
        vv = cq[0:n, 1024:1536]
        beta = sm[0:n, 28:32]
        k.act(beta, proj[0:n, 2048:2052], AF.Sigmoid)
        tt_ = sm[0:n, 32:36]
        k.tt(tt_, proj[0:n, 2052:2056], PC[0:n, 6276:6280], ALU.add)
        softplus_parts(sm[0:n, 36:40], tt_, n, 4, sm[0:n, 40:44], sm[0:n, 44:48])
        k.ts(sm[0:n, 40:44], tt_, 0.0, ALU.max)
        k.tt(sm[0:n, 40:44], sm[0:n, 40:44], sm[0:n, 36:40], ALU.add)
        gg = sm[0:n, 48:52]
        k.tt(gg, sm[0:n, 40:44], PC[0:n, 6280:6284], ALU.mult)
        return qn, kn, vv, beta, gg

    def i16(p0, p1, shape, axis):
        a = CM[p0:p1, C_I16:C_I16 + 2, :].rearrange("p a (b c) -> p (a b) c", c=16)[:, 0:NS, 0:NS]
        return bc(a.unsqueeze(axis), shape)

    def dn_batch(l):
        n = NS
        for c in range(3):
            cs = slice(c * 512, (c + 1) * 512)
            acc, tmp = G[0], G[1]
            k.tt(acc[0:n, :], proj[0:n, cs], PC[0:n, c * 512:(c + 1) * 512], ALU.mult)
            for d_ in range(1, 4):
                st = stage[d_ % 2][0:n, (d_ // 2) * 512:(d_ // 2) * 512 + 512]
                k.dma(st, st_conv[l, :, 3 - d_, cs])
                k.tt(tmp[0:n, :], st, PC[0:n, d_ * 1536 + c * 512: d_ * 1536 + (c + 1) * 512], ALU.mult)
                k.tt(acc[0:n, :], acc[0:n, :], tmp[0:n, :], ALU.add)
            k.act(cq[0:n, cs], acc[0:n, :], AF.Silu)
        k.dma(s_conv[l, :, 0:2, :], st_conv[l, :, 1:3, :], q="pool")
        k.dma(s_conv[l, :, 2, :], proj[0:n, 0:1536], q="pool")
        qn, kn, vv, beta, gg = dn_prep(n)
        eG = sm[0:n, 60:64]
        k.act(eG, gg, AF.Exp)
        beG = sm[0:n, 68:72]
        k.tt(beG, beta, eG, ALU.mult)

        def h3(t):
            return v3(t[0:n, :], 4)

        def bcs(s_):
            return bc(s_.unsqueeze(2), [n, 4, 128])

        kbg, vb, qg, t0 = G[0], G[1], G[2], G[3]
        k.tt(h3(kbg), v3(kn, 4), bcs(beG), ALU.mult)
        k.tt(h3(vb), v3(vv, 4), bcs(beta), ALU.mult)
        k.tt(h3(qg), v3(qn, 4), bcs(eG), ALU.mult)
        k.tt(t0[0:n, :], qn, kn, ALU.mult)
        Aqk = sm[0:n, 72:76]
        k.red(Aqk, h3(t0))
        MLk, MLq = (G[4], G[5]), (G[6], G[7])
        for src, ML in ((kbg, MLk), (qg, MLq)):
            b = k.bank()
            for h in range(4):
                k.tr(b[:, h * 128:h * 128 + n], src[0:n, h * 128:(h + 1) * 128], cid(n))
            for hp in range(2):
                o4 = ML[hp][:, :].rearrange("p (a s m) -> p a s m", a=2, s=16)[:, :, 0:n, 0:n]
                i4 = v3(b[:, :], 4)[:, 2 * hp:2 * hp + 2, 0:n]
                k.tt(o4, bc(i4.unsqueeze(2), [128, 2, n, n]), i16(0, 128, [128, 2, n, n], 1), ALU.mult)
        bKS, bQS = k.bank(), k.bank()
        k.memset(bKS[0:n, :], 0.0)
        k.memset(bQS[0:n, :], 0.0)
        for s_ in range(n):
            st = v3(stage[s_ % 2][:, 0:512], 4)
            k.dma(st, st_dn[l, s_].rearrange("h k v -> k h v"))
            for h in range(4):
                for ML, bk in ((MLk, bKS), (MLq, bQS)):
                    lt = ML[h // 2][:, :].rearrange("p (a s m) -> p a s m", a=2, s=16)[:, h % 2, s_, 0:n]
                    k.mm(bk[0:n, h * 128:(h + 1) * 128], lt, st[:, h, :], start=False, stop=True, skip=True)
        vnew, o_ = G[8], G[9]
        k.tt(vnew[0:n, :], vb[0:n, :], bKS[0:n, :], ALU.subtract)
        k.tt(h3(o_), h3(vnew), bcs(Aqk), ALU.mult)
        k.tt(o_[0:n, :], o_[0:n, :], bQS[0:n, :], ALU.add)
        t1 = G[10]
        k.tt(t1[0:n, 0:4 * n].rearrange("p (s h) -> p s h", h=4), bc(eG.unsqueeze(1), [n, n, 4]),
             bc(CM[0:n, C_ID, 0:n].unsqueeze(2), [n, n, 4]), ALU.mult)
        b = k.bank()
        k.mm(b[:, 0:4 * n], cmat(C_ONES, n, 128), t1[0:n, 0:4 * n])
        EGB = G[11]
        k.cp(EGB[:, 0:4 * n], b[:, 0:4 * n])
        for s_ in range(n):
            st = v3(stage[s_ % 2][:, 0:512], 4)
            k.dma(st, st_dn[l, s_].rearrange("h k v -> k h v"))
            vm = G[12 + s_ % 2]
            k.ts(vm[0:n, :], vnew[0:n, :], CM[0:n, C_ID, s_:s_ + 1], ALU.mult)
            b = k.bank()
            for h in range(4):
                k.mm(b[:, h * 128:(h + 1) * 128], kn[:, h * 128:(h + 1) * 128], vm[0:n, h * 128:(h + 1) * 128])
            so = G[14 + s_ % 2]
            k.tt(v3(so[:, :], 4), st, bc(EGB[:, s_ * 4:(s_ + 1) * 4].unsqueeze(2), [128, 4, 128]), ALU.mult)
            k.tt(so[:, :], so[:, :], b[:, :], ALU.add)
            k.dma(s_dn[l, s_].rearrange("h k v -> k h v"), v3(so[:, :], 4), q="pool")
        rms_gate_store(sbatch, 4, 128, PC[0:n, 6144:6272], proj[0:n, 1536:2048], 0, EPS, src=o_)

    def dn_chunk(l, seg, qn, kn, vv, beta, gg):
        kind, idx, n, row = seg
        k.act(G[23][0:n, :], proj[0:n, 1536:2048], AF.Silu)
        b = k.bank()
        k.mm(b[0:n, 0:4], cmat(C_TRIU, n, n), gg)
        Gc = sm[0:n, 52:56]
        k.cp(Gc, b[0:n, 0:4], eng="dve")
        nG = sm[0:n, 56:60]
        k.ts(nG, Gc, -1.0, ALU.mult)
        b = k.bank()
        k.mm(b[:, 0:4], cmat(C_ONES, n, 128), gg)
        GT = sm2[:, 0:4]
        k.cp(GT, b[:, 0:4], eng="dve")
        eGl = sm2[:, 4:8]
        k.act(eGl, GT, AF.Exp)
        eG = sm[0:n, 60:64]
        k.act(eG, Gc, AF.Exp)
        edk = sm[0:n, 64:68]
        k.tt(edk, GT[0:n, :], Gc, ALU.subtract)
        k.act(edk, edk, AF.Exp)
        beG = sm[0:n, 68:72]
        k.tt(beG, beta, eG, ALU.mult)

        def h3(t):
            return v3(t[0:n, :], 4)

        def bcs(s):
            return bc(s.unsqueeze(2), [n, 4, 128])

        kbg, vb, kd, qg = G[0], G[1], G[2], G[3]
        k.tt(h3(kbg), v3(kn, 4), bcs(beG), ALU.mult)
        k.tt(h3(vb), v3(vv, 4), bcs(beta), ALU.mult)
        k.tt(h3(kd), v3(kn, 4), bcs(edk), ALU.mult)
        k.tt(h3(qg), v3(qn, 4), bcs(eG), ALU.mult)
        KT, QT, QGT = G[4], G[5], G[6]
        for src, dst in ((kn, KT), (qn, QT), (qg[0:n, :], QGT)):
            b = k.bank()
            for h in range(4):
                k.tr(b[:, h * 128:h * 128 + n], src[:, h * 128:(h + 1) * 128], cid(n))
            k.cp(v3(dst[:, :], 4)[:, :, 0:n], v3(b[:, :], 4)[:, :, 0:n])

        def f3(t):
            return v3(t[:, :], 4)[:, :, 0:n]

        def m3(t):
            return v3(t[0:n, :], 4)[:, :, 0:n]

        dg = G[7]
        k.tt(m3(dg), bc(CM[0:n, C_ID:C_ID + 1, 0:n], [n, 4, n]), bc(Gc.unsqueeze(2), [n, 4, n]), ALU.mult)
        bR = k.bank()
        for h in range(4):
            k.mm(m3(bR)[:, h, :], cmat(C_ONES, n, n), m3(dg)[:, h, :])
        t1, t2, Dm, DTm = G[8], G[9], G[10], G[11]
        k.stt(m3(t1), m3(bR), -1.0, bc(CM[0:n, C_MLOW:C_MLOW + 1, 0:n], [n, 4, n]), ALU.mult, ALU.add)
        k.tt(m3(t2), m3(bR), bc(CM[0:n, C_MUP:C_MUP + 1, 0:n], [n, 4, n]), ALU.add)
        for h in range(4):
            k.act(m3(Dm)[:, h, :], m3(t1)[:, h, :], AF.Exp, bias=Gc[:, h:h + 1])
            k.act(m3(DTm)[:, h, :], m3(t2)[:, h, :], AF.Exp, bias=nG[:, h:h + 1])
        bK = k.bank()
        bA = k.bank()
        for h in range(4):
            k.mm(m3(bK)[:, h, :], f3(KT)[:, h, :], f3(KT)[:, h, :])
            k.mm(m3(bA)[:, h, :], f3(KT)[:, h, :], f3(QT)[:, h, :])
        KKD, AT = G[12], G[13]
        k.tt(m3(KKD), m3(bK), m3(Dm), ALU.mult)
        k.tt(m3(KKD), m3(KKD), bc(beta.unsqueeze(2), [n, 4, n]), ALU.mult)
        k.tt(m3(AT), m3(bA), m3(DTm), ALU.mult)
        if n > 1:
            Nbd, NbdT, Loff = G[14], G[15], G[16]
            k.tt(m3(Nbd), m3(KKD), bc(CM[0:n, C_NTRIL64:C_NTRIL64 + 1, 0:n], [n, 4, n]), ALU.mult)
            k.tt(m3(Loff), m3(KKD), bc(CM[0:n, C_LOWLEFT:C_LOWLEFT + 1, 0:n], [n, 4, n]), ALU.mult)
            b = k.bank()
            for h in range(4):
                k.tr(m3(b)[:, h, :], m3(Nbd)[:, h, :], cid(n))
            k.cp(m3(NbdT), m3(b))
            X = neumann(n, m3(Nbd), m3(NbdT), m3(Loff), G[17:23])
        else:
            X = neumann(n, None, None, None, G[17:23])
        bW = k.bank()
        for h in range(4):
            k.mm(f3(bW)[:, h, :], h3(kbg)[:, h, :], X[:, h, :])
        nWT = G[7]
        k.ts(f3(nWT), f3(bW), -1.0, ALU.mult)
        bV = k.bank()
        for h in range(4):
            k.mm(bV[0:n, h * 128:(h + 1) * 128], X[:, h, :], h3(vb)[:, h, :], start=True, stop=False)
            k.mm(bV[0:n, h * 128:(h + 1) * 128], f3(nWT)[:, h, :], S_dn[:, h, :], start=False, stop=True)
        VN = G[8]
        k.cp(VN[0:n, :], bV[0:n, :])
        for h in range(4):
            k.mm(PSO[0:n, h * 128:(h + 1) * 128], f3(QGT)[:, h, :], S_dn[:, h, :], start=True, stop=False)
            k.mm(PSO[0:n, h * 128:(h + 1) * 128], m3(AT)[:, h, :], h3(VN)[:, h, :], start=False, stop=True)
        bS = k.bank()
        for h in range(4):
            k.mm(bS[:, h * 128:(h + 1) * 128], h3(kd)[:, h, :], h3(VN)[:, h, :])
        for h in range(4):
            k.stt(S_dn[:, h, :], S_dn[:, h, :], eGl[:, h:h + 1], bS[:, h * 128:(h + 1) * 128], ALU.mult, ALU.add)
        rms_gate_store(seg, 4, 128, PC[0:n, 6144:6272], proj[0:n, 1536:2048], 0, EPS, otile=G[22], zs_pre=G[23])

    def gla_phase(l):
        if k.preloaded != (l, "gla"):
            load_weights(l, GLA_C0, GLA_W, 0)
        bload(PC[:, 0:256], gla_bf[l:l + 1, :], 256)
        bload(PC[:, 256:384], gla_norm_g[l:l + 1, :], 128)
        k.dma(wsm[0:16, 0:256], gla_wf[l])
        extract_mod("prompt", 128, 0, 2 * D)
        k.memset(S_gla[:], 0.0)
        k.memset(G[12][:, :], 0.0)
        k.memset(G[15][:, :], 0.0)
        def pre(seg):
            k.dma(v3(stage[0][:, 0:256], 2), st_gla[l, seg[1]].rearrange("(hp h2) k v -> (h2 k) hp v", h2=2))
            k.cp(S_gla[:], v3(stage[0][:, 0:256], 2), eng="dve")

        def post(seg):
            kind, idx, n, row = seg
            if kind == "s":
                k.dma(s_gla[l, idx].rearrange("(hp h2) k v -> (h2 k) hp v", h2=2), S_gla[:], q="pool")
            elif idx == NT - 1:
                k.dma(p_gla[l].rearrange("(hp h2) k v -> (h2 k) hp v", h2=2), S_gla[:], q="pool")

        def _nl():
            load_weights(l, RW_C0, RW_W, 0)
            k.preloaded = (l, "rw")
        k.next_load = _nl
        mixer_pass(l, GLA_W, pre, gla_segment, post, batch_core=gla_batch)

    def gla_segment(l, seg):
        kind, idx, n, row = seg
        q_ = proj[0:n, 0:256]
        k_ = proj[0:n, 256:512]
        v_ = proj[0:n, 512:1024]
        gz = proj[0:n, 1024:1536]
        lf = gla_lf(n)
        LF = lf[0:n, 0:256]
        gla_chunk(l, seg, q_, k_, v_, gz, lf, LF)

    def gla_lf(n):
        glo = proj[0:n, 1536:1552]
        b = k.bank()
        k.tr(b[0:16, 0:n], glo, cid(n))
        gloT = G[0]
        k.cp(gloT[0:16, 0:n], b[0:16, 0:n])
        b = k.bank()
        k.mm(b[0:n, 0:256], gloT[0:16, 0:n], wsm[0:16, 0:256])
        xb, t1, t2, lf = G[1], G[2], G[3], G[4]
        k.tt(xb[0:n, 0:256], b[0:n, 0:256], PC[0:n, 0:256], ALU.add)
        softplus_parts(t1[0:n, 0:256], xb[0:n, 0:256], n, 256, t2[0:n, 0:256], t2[0:n, 256:512])
        k.ts(t2[0:n, 0:256], xb[0:n, 0:256], 0.0, ALU.min)
        k.tt(lf[0:n, 0:256], t2[0:n, 0:256], t1[0:n, 0:256], ALU.subtract)
        k.ts(lf[0:n, 0:256], lf[0:n, 0:256], 1.0 / 16.0, ALU.mult)
        return lf

    def gla_batch(l):
        n = NS
        q_ = proj[0:n, 0:256]
        k_ = proj[0:n, 256:512]
        v_ = proj[0:n, 512:1024]
        gz = proj[0:n, 1024:1536]
        lf = gla_lf(n)
        LF = lf[0:n, 0:256]
        eG, enG, qg, kg, t0 = G[5][0:n, 0:256], G[6][0:n, 0:256], G[7][0:n, 0:256], G[8][0:n, 0:256], G[9][0:n, 0:256]
        k.act(eG, LF, AF.Exp)
        k.act(enG, LF, AF.Exp, scale=-1.0)
        k.stt(qg, q_, 0.125, eG, ALU.mult, ALU.mult)
        k.tt(kg, k_, enG, ALU.mult)
        k.tt(t0, qg, kg, ALU.mult)
        Aqk = sm[0:n, 72:76]
        k.red(Aqk, v3(t0, 4))
        MLq = (G[12], G[15])
        b = k.bank()
        for hp in range(2):
            k.tr(b[:, hp * 128:hp * 128 + n], qg[:, hp * 128:(hp + 1) * 128], cid(n))
        for h2 in range(2):
            pr0, pr1 = h2 * 64, h2 * 64 + 64
            o4 = MLq[h2][pr0:pr1, :].rearrange("p (a s m) -> p a s m", a=2, s=16)[:, :, 0:n, 0:n]
            i4 = v3(b[pr0:pr1, 0:256], 2)[:, :, 0:n]
            k.tt(o4, bc(i4.unsqueeze(2), [64, 2, n, n]), i16(pr0, pr1, [64, 2, n, n], 1), ALU.mult)
        b = k.bank()
        for hp in range(2):
            k.tr(b[:, hp * 128:hp * 128 + n], lf[0:n, hp * 128:(hp + 1) * 128], cid(n))
        EGB = G[10]
        k.act(v3(EGB[:, 0:256], 2)[:, :, 0:n], v3(b[:, 0:256], 2)[:, :, 0:n], AF.Exp)
        bQS = k.bank()
        k.memset(bQS[0:n, :], 0.0)
        for s_ in range(n):
            st = v3(stage[s_ % 2][:, 0:256], 2)
            k.dma(st, st_gla[l, s_].rearrange("(hp h2) k v -> (h2 k) hp v", h2=2))
            for h in range(4):
                lt = MLq[h % 2][:, :].rearrange("p (a s m) -> p a s m", a=2, s=16)[:, h // 2, s_, 0:n]
                k.mm(bQS[0:n, h * 128:(h + 1) * 128], lt, st[:, h // 2, :], start=False, stop=True, skip=True)
        o_ = G[11]
        k.tt(v3(o_[0:n, :], 4), v3(v_, 4), bc(Aqk.unsqueeze(2), [n, 4, 128]), ALU.mult)
        k.tt(o_[0:n, :], o_[0:n, :], bQS[0:n, :], ALU.add)
        for s_ in range(n):
            st = v3(stage[s_ % 2][:, 0:256], 2)
            k.dma(st, st_gla[l, s_].rearrange("(hp h2) k v -> (h2 k) hp v", h2=2))
            vm = G[0 + s_ % 2]
            k.ts(vm[0:n, :], v_, CM[0:n, C_ID, s_:s_ + 1], ALU.mult)
            b = k.bank()
            for hp in range(2):
                k.mm(b[:, hp * 256:(hp + 1) * 256], k_[:, hp * 128:(hp + 1) * 128], vm[0:n, hp * 256:(hp + 1) * 256])
            so = G[2 + s_ % 2]
            for hp in range(2):
                for h2 in range(2):
                    pr = slice(h2 * 64, h2 * 64 + 64)
                    k.stt(v3(so[:, 0:256], 2)[pr, hp, :], st[pr, hp, :], v3(EGB[:, 0:256], 2)[pr, hp, s_:s_ + 1],
                          b[pr, hp * 256 + h2 * 128: hp * 256 + (h2 + 1) * 128], ALU.mult, ALU.add)
            k.dma(s_gla[l, s_].rearrange("(hp h2) k v -> (h2 k) hp v", h2=2), v3(so[:, 0:256], 2), q="pool")
        rms_gate_store(sbatch, 4, 128, PC[0:n, 256:384], gz, 512, EPS, src=o_)

    def gla_chunk(l, seg, q_, k_, v_, gz, lf, LF):
        kind, idx, n, row = seg
        k.act(G[16][0:n, :], gz, AF.Silu)
        bG = k.bank()
        k.mm(bG[0:n, 0:256], cmat(C_TRIU, n, n), LF)
        Gc = G[5][0:n, 0:256]
        k.cp(Gc, bG[0:n, 0:256])
        bT = k.bank()
        k.mm(bT[0:n, 0:256], cmat(C_ONES, n, n), LF)
        edk = G[6][0:n, 0:256]
        k.tt(edk, bT[0:n, 0:256], Gc, ALU.subtract)
        k.act(edk, edk, AF.Exp)
        bE = k.bank()
        for hp in range(2):
            k.mm(bE[:, 2 * hp:2 * hp + 2], lf[0:n, hp * 128:(hp + 1) * 128], cmat(C_ONES, n, 2))
        eGl = sm2[:, 8:10]
        k.act(eGl, v3(bE[:, 0:4], 2)[:, :, 0], AF.Exp)
        eG = G[7][0:n, 0:256]
        enG = G[8][0:n, 0:256]
        k.act(eG, Gc, AF.Exp)
        k.act(enG, Gc, AF.Exp, scale=-1.0)
        qg = G[9][0:n, 0:256]
        kg = G[10][0:n, 0:256]
        kd = G[11][0:n, 0:256]
        k.stt(qg, q_, 0.125, eG, ALU.mult, ALU.mult)
        k.tt(kg, k_, enG, ALU.mult)
        k.tt(kd, k_, edk, ALU.mult)
        QGTm, KGT = (G[12], G[15]), G[13]
        for src, dst in ((qg, None), (kg, KGT)):
            b = k.bank()
            for hp in range(2):
                k.tr(b[:, hp * 128:hp * 128 + n], src[:, hp * 128:(hp + 1) * 128], cid(n))
            if dst is None:
                for h2 in range(2):
                    pr = slice(h2 * 64, h2 * 64 + 64)
                    k.cp(v3(QGTm[h2][pr, 0:256], 2)[:, :, 0:n], v3(b[pr, 0:256], 2)[:, :, 0:n])
            else:
                k.cp(v3(dst[:, 0:256], 2)[:, :, 0:n], v3(b[:, 0:256], 2)[:, :, 0:n])

        def m3(t):
            return v3(t[0:n, :], 4)[:, :, 0:n]

        bA = k.bank()
        for h in range(4):
            hp = h // 2
            k.mm(m3(bA)[:, h, :], v3(KGT[:, 0:256], 2)[:, hp, 0:n], v3(QGTm[h % 2][:, 0:256], 2)[:, hp, 0:n])
        AT = G[14]
        k.tt(m3(AT), m3(bA), bc(CM[0:n, C_TRIU:C_TRIU + 1, 0:n], [n, 4, n]), ALU.mult)
        for h in range(4):
            hp = h // 2
            k.mm(PSO[0:n, h * 128:(h + 1) * 128], v3(QGTm[h % 2][:, 0:256], 2)[:, hp, 0:n], S_gla[:, hp, :], start=True, stop=False)
            k.mm(PSO[0:n, h * 128:(h + 1) * 128], m3(AT)[:, h, :], v_[:, h * 128:(h + 1) * 128], start=False, stop=True)
        for hp in range(2):
            bS = k.bank()
            k.mm(bS[:, 0:256], kd[:, hp * 128:(hp + 1) * 128], v_[:, hp * 256:(hp + 1) * 256])
            for h2 in range(2):
                pr = slice(h2 * 64, h2 * 64 + 64)
                k.stt(S_gla[pr, hp, :], S_gla[pr, hp, :], eGl[pr, hp:hp + 1], bS[pr, h2 * 128:(h2 + 1) * 128], ALU.mult, ALU.add)
        rms_gate_store(seg, 4, 128, PC[0:n, 256:384], gz, 512, EPS, otile=G[14], zs_pre=G[16])

    RWC = dict(mu=0, w0=1664, a0=2176, kk=2688, ka=3200, rk=3712, lw=4224, lb=4736)

    def rw_phase(l):
        if k.preloaded != (l, "rw"):
            load_weights(l, RW_C0, RW_W, 0)
        bload(PC[:, 0:1664], rw_mu[l:l + 1, :], 1664)
        for nm, src in (("w0", rw_w0), ("a0", rw_a0), ("kk", rw_k_k), ("ka", rw_k_a), ("rk", rw_r_k),
                        ("lw", rw_ln_w), ("lb", rw_ln_b)):
            bload(PC[:, RWC[nm]:RWC[nm] + 512], src[l:l + 1, :], 512)
        k.dma(wsm[0:64, 0:512], rw_w2[l])
        k.dma(wsm[0:64, 512:1024], rw_a2[l])
        extract_mod("prompt", 128, 0, 2 * D)
        k.memset(srow, 0.0)
        k.memset(M_rw[:], 0.0)
        MT = G[26]
        def pre(seg):
            idx = seg[1]
            k.dma(srow, st_rs[l, idx:idx + 1, :])
            k.dma(v3(MT[0:64, :], 8), st_rw[l, idx].rearrange("h v k -> v h k"))
            b = k.bank()
            for hp in range(4):
                k.tr(b[:, hp * 64:(hp + 1) * 64], MT[0:64, hp * 128:(hp + 1) * 128], cid(64))
            k.cp(M_rw[:], v3(b[:, 0:256], 4))

        def post(seg):
            kind, idx, n, row = seg
            last = (kind == "s") or idx == NT - 1
            if last:
                b = k.bank()
                for hp in range(4):
                    k.tr(b[0:64, hp * 128:(hp + 1) * 128], M_rw[:, hp, :], cid(128))
                k.cp(MT[0:64, :], b[0:64, :])
                dst = s_rw[l, idx] if kind == "s" else p_rw[l]
                k.dma(dst.rearrange("h v k -> v h k"), v3(MT[0:64, :], 8), q="pool")
                dst = s_rs[l, idx:idx + 1, :] if kind == "s" else p_rs[l:l + 1, :]
                k.dma(dst, proj[n - 1:n, 0:1664], q="pool")
            else:
                k.dma(srow, proj[n - 1:n, 0:1664])

        def _nl():
            load_weights(l, 0, D, 0, rows_kc=12, src=w_out)
            k.preloaded = (l, "out")
        k.next_load = _nl
        mixer_pass(l, RW_W, pre, rw_segment, post, batch_core=rw_batch)

    def rw_segment(l, seg):
        kind, idx, n, row = seg
        xm = cq
        for c0 in range(0, 1664, 512):
            wd = min(512, 1664 - c0)
            b = k.bank()
            if n > 1:
                k.mm(b[0:n, 0:wd], cmat(C_SH1, n, n), proj[0:n, c0:c0 + wd], start=True, stop=False)
            k.mm(b[0:n, 0:wd], cmat(C_E0, 1, n), srow[0:1, c0:c0 + wd], start=(n == 1), stop=True)
            g = G[0]
            k.tt(g[0:n, 0:wd], b[0:n, 0:wd], proj[0:n, c0:c0 + wd], ALU.subtract)
            k.tt(g[0:n, 0:wd], g[0:n, 0:wd], PC[0:n, c0:c0 + wd], ALU.mult)
            k.tt(xm[0:n, c0:c0 + wd], g[0:n, 0:wd], proj[0:n, c0:c0 + wd], ALU.add)
        rw_chunk(l, seg, *rw_prep(n))

    def rw_prep(n):
        xm = cq
        r_ = xm[0:n, 0:512]
        k_ = xm[0:n, 512:1024]
        v_ = xm[0:n, 1024:1536]
        rz = proj[0:n, 1664:2176]

        def pc(nm):
            return PC[0:n, RWC[nm]:RWC[nm] + 512]

        tw = G[0]
        k.act(tw[0:n, 0:64], xm[0:n, 1536:1600], AF.Tanh)
        b = k.bank()
        k.tr(b[0:64, 0:n], tw[0:n, 0:64], cid(n))
        k.tr(b[0:64, 128:128 + n], xm[0:n, 1600:1664], cid(n))
        loT = G[1]
        k.cp(loT[0:64, 0:256], b[0:64, 0:256])
        bw = k.bank()
        k.mm(bw[0:n, :], loT[0:64, 0:n], wsm[0:64, 0:512])
        LW = G[2]
        k.tt(LW[0:n, :], bw[0:n, :], pc("w0"), ALU.add)
        k.act(LW[0:n, :], LW[0:n, :], AF.Sigmoid)
        k.ts(LW[0:n, :], LW[0:n, :], -float(np.exp(-0.5)), ALU.mult)
        ba = k.bank()
        k.mm(ba[0:n, :], loT[0:64, 128:128 + n], wsm[0:64, 512:1024])
        At = G[3]
        k.tt(At[0:n, :], ba[0:n, :], pc("a0"), ALU.add)
        k.act(At[0:n, :], At[0:n, :], AF.Sigmoid)
        KK, K2, Bt, t0 = G[4], G[5], G[6], G[7]
        k.tt(KK[0:n, :], k_, pc("kk"), ALU.mult)
        k.tt(t0[0:n, :], KK[0:n, :], KK[0:n, :], ALU.mult)
        k.red(sm[0:n, 64:72], v3(t0[0:n, :], 8))
        k.act(sm[0:n, 72:80], sm[0:n, 64:72], AF.Sqrt, bias=EPS)
        k.recip(sm[0:n, 80:88], sm[0:n, 72:80])
        k.tt(v3(KK[0:n, :], 8), v3(KK[0:n, :], 8), bc(sm[0:n, 80:88].unsqueeze(2), [n, 8, 64]), ALU.mult)
        k.stt(t0[0:n, :], At[0:n, :], -1.0, pc("ka"), ALU.add, ALU.mult)
        k.stt(K2[0:n, :], t0[0:n, :], 1.0, k_, ALU.add, ALU.mult)
        k.tt(Bt[0:n, :], KK[0:n, :], At[0:n, :], ALU.mult)
        k.tt(t0[0:n, :], r_, K2[0:n, :], ALU.mult)
        k.tt(t0[0:n, :], t0[0:n, :], pc("rk"), ALU.mult)
        bsum = sm[0:n, 88:96]
        k.red(bsum, v3(t0[0:n, :], 8))
        return r_, k_, v_, rz, LW, At, KK, K2, Bt, bsum, pc

    def rw_chunk(l, seg, r_, k_, v_, rz, LW, At, KK, K2, Bt, bsum, pc):
        kind, idx, n, row = seg
        t0 = G[7]
        bG = k.bank()
        k.mm(bG[0:n, :], cmat(C_TRIU, n, n), LW[0:n, :])
        Gc = G[8]
        k.cp(Gc[0:n, :], bG[0:n, :])
        bT = k.bank()
        k.mm(bT[0:n, :], cmat(C_ONES, n, n), LW[0:n, :])
        edk = G[9]
        k.tt(edk[0:n, :], bT[0:n, :], Gc[0:n, :], ALU.subtract)
        k.act(edk[0:n, :], edk[0:n, :], AF.Exp)
        bE = k.bank()
        for hp in range(4):
            k.mm(bE[:, 2 * hp:2 * hp + 2], LW[0:n, hp * 128:(hp + 1) * 128], cmat(C_ONES, n, 2))
        eGl = sm2[:, 12:16]
        k.act(eGl, v3(bE[:, 0:8], 4)[:, :, 0], AF.Exp)
        eG, enG, eGx = G[10], G[11], G[7]
        k.act(eG[0:n, :], Gc[0:n, :], AF.Exp)
        k.act(enG[0:n, :], Gc[0:n, :], AF.Exp, scale=-1.0)
        k.tt(eGx[0:n, :], Gc[0:n, :], LW[0:n, :], ALU.subtract)
        k.act(eGx[0:n, :], eGx[0:n, :], AF.Exp)
        kkg, rg, bg, k2g, bd, k2d = G[12], G[13], G[14], G[15], G[16], G[17]
        k.tt(kkg[0:n, :], KK[0:n, :], eGx[0:n, :], ALU.mult)
        k.tt(rg[0:n, :], r_, eG[0:n, :], ALU.mult)
        k.tt(bg[0:n, :], Bt[0:n, :], enG[0:n, :], ALU.mult)
        k.tt(k2g[0:n, :], K2[0:n, :], enG[0:n, :], ALU.mult)
        k.tt(bd[0:n, :], Bt[0:n, :], edk[0:n, :], ALU.mult)
        k.tt(k2d[0:n, :], K2[0:n, :], edk[0:n, :], ALU.mult)
        bgT, k2gT, kkgTm, rgTm = G[0], G[1], (G[2], G[3]), (G[10], G[11])
        for src, dst in ((kkg, kkgTm), (rg, rgTm), (bg, bgT), (k2g, k2gT)):
            b = k.bank()
            for hp in range(4):
                k.tr(b[:, hp * 128:hp * 128 + n], src[0:n, hp * 128:(hp + 1) * 128], cid(n))
            if isinstance(dst, tuple):
                for h2 in range(2):
                    pr = slice(h2 * 64, h2 * 64 + 64)
                    po = slice((1 - h2) * 64, (1 - h2) * 64 + 64)
                    k.cp(v3(dst[h2][pr, :], 4)[:, :, 0:n], v3(b[pr, :], 4)[:, :, 0:n])
                    k.memset(v3(dst[h2][po, :], 4)[:, :, 0:n], 0.0)
            else:
                k.cp(v3(dst[:, :], 4)[:, :, 0:n], v3(b[:, :], 4)[:, :, 0:n])

        def fT(t, h):
            if isinstance(t, tuple):
                t = t[h % 2]
            return v3(t[:, :], 4)[:, h // 2, 0:n]

        def m3(t):
            return v3(t[0:n, :], 4)[:, :, 0:n]

        def mk(i):
            return bc(CM[0:n, i:i + 1, 0:n], [n, 4, n])

        for hb_ in range(2):
            heads = [4 * hb_ + i for i in range(4)]
            bN, bNT, bAk, bRb, bRk = k.bank(), k.bank(), k.bank(), k.bank(), k.bank()
            for i, h in enumerate(heads):
                k.mm(m3(bN)[:, i, :], fT(kkgTm, h), fT(bgT, h))
                k.mm(m3(bNT)[:, i, :], fT(bgT, h), fT(kkgTm, h))
                k.mm(m3(bAk)[:, i, :], fT(k2gT, h), fT(kkgTm, h))
                k.mm(m3(bRb)[:, i, :], fT(bgT, h), fT(rgTm, h))
                k.mm(m3(bRk)[:, i, :], fT(k2gT, h), fT(rgTm, h))
            AkT, RbT, RkT = G[4], G[5], G[6]
            k.tt(m3(AkT), m3(bAk), mk(C_TRIUS), ALU.mult)
            k.tt(m3(RbT), m3(bRb), mk(C_TRIU), ALU.mult)
            k.tt(m3(RkT), m3(bRk), mk(C_TRIU), ALU.mult)
            if n > 1:
                Nbd, NbdT, Loff = G[7], G[8], G[9]
                k.tt(m3(Nbd), m3(bN), mk(C_NTRIL64), ALU.mult)
                k.tt(m3(Loff), m3(bN), mk(C_LOWLEFT), ALU.mult)
                k.tt(m3(NbdT), m3(bNT), mk(C_NTRIU64), ALU.mult)
                X = neumann(n, m3(Nbd), m3(NbdT), m3(Loff), G[18:24])
            else:
                X = neumann(n, None, None, None, G[18:24])
            bR = k.bank()
            for i, h in enumerate(heads):
                k.mm(bR[0:n, i * 64:(i + 1) * 64], fT(kkgTm, h), M_rw[:, h // 2, :], start=True, stop=False)
                k.mm(bR[0:n, i * 64:(i + 1) * 64], m3(AkT)[:, i, :], v_[:, h * 64:(h + 1) * 64], start=False, stop=True)
            R0 = G[24]
            k.ts(R0[0:n, 0:256], bR[0:n, 0:256], -1.0, ALU.mult)
            bU = k.bank()
            for i, h in enumerate(heads):
                k.mm(bU[0:n, i * 64:(i + 1) * 64], X[:, i, :], R0[0:n, i * 64:(i + 1) * 64])
            U = G[25]
            k.cp(U[0:n, 0:256], bU[0:n, 0:256])
            for i, h in enumerate(heads):
                o_ = PSO[0:n, h * 64:(h + 1) * 64]
                k.mm(o_, fT(rgTm, h), M_rw[:, h // 2, :], start=True, stop=False)
                k.mm(o_, m3(RbT)[:, i, :], U[0:n, i * 64:(i + 1) * 64], start=False, stop=False)
                k.mm(o_, m3(RkT)[:, i, :], v_[:, h * 64:(h + 1) * 64], start=False, stop=True)
            for j in range(2):
                hp = 2 * hb_ + j
                bM = k.bank()
                k.mm(bM[:, 0:128], bd[0:n, hp * 128:(hp + 1) * 128], U[0:n, j * 128:(j + 1) * 128], start=True, stop=False)
                k.mm(bM[:, 0:128], k2d[0:n, hp * 128:(hp + 1) * 128], v_[:, hp * 128:(hp + 1) * 128], start=False, stop=True)
                for h2 in range(2):
                    pr = slice(h2 * 64, h2 * 64 + 64)
                    k.stt(M_rw[pr, hp, :], M_rw[pr, hp, :], eGl[pr, hp:hp + 1], bM[pr, h2 * 64:(h2 + 1) * 64], ALU.mult, ALU.add)
        rw_out(n, row, None, v_, rz, bsum, pc)

    def rw_batch(l):
        n = NS
        xm = cq
        for j, c0 in enumerate(range(0, 1664, 512)):
            wd = min(512, 1664 - c0)
            st = stage[j % 2][0:n, (j // 2) * 512:(j // 2) * 512 + wd]
            k.dma(st, st_rs[l, 0:n, c0:c0 + wd])
            g = G[0]
            k.tt(g[0:n, 0:wd], st, proj[0:n, c0:c0 + wd], ALU.subtract)
            k.tt(g[0:n, 0:wd], g[0:n, 0:wd], PC[0:n, c0:c0 + wd], ALU.mult)
            k.tt(xm[0:n, c0:c0 + wd], g[0:n, 0:wd], proj[0:n, c0:c0 + wd], ALU.add)
        k.dma(s_rs[l, 0:n, :], proj[0:n, 0:1664], q="pool")
        r_, k_, v_, rz, LW, At, KK, K2, Bt, bsum, pc = rw_prep(n)
        t0, eG, rg = G[7], G[8], G[9]
        k.act(eG[0:n, :], LW[0:n, :], AF.Exp)
        k.tt(rg[0:n, :], r_, eG[0:n, :], ALU.mult)
        rb, rk2 = sm2[0:n, 16:24], sm2[0:n, 24:32]
        k.tt(t0[0:n, :], r_, Bt[0:n, :], ALU.mult)
        k.red(rb, v3(t0[0:n, :], 8))
        k.tt(t0[0:n, :], r_, K2[0:n, :], ALU.mult)
        k.red(rk2, v3(t0[0:n, :], 8))
        MLkk = ((G[10], G[11]), (G[12], G[13]))
        MLrg = ((G[14], G[15]), (G[16], G[17]))

        def mlv(t):
            return t[:, :].rearrange("p (a s m) -> p a s m", a=2, s=16)

        for src, ML in ((KK, MLkk), (rg, MLrg)):
            b = k.bank()
            for hp in range(4):
                k.tr(b[:, hp * 128:hp * 128 + n], src[0:n, hp * 128:(hp + 1) * 128], cid(n))
            for h2 in range(2):
                p0, p1 = h2 * 64, h2 * 64 + 64
                q0, q1 = (1 - h2) * 64, (1 - h2) * 64 + 64
                for hq in range(2):
                    o4 = mlv(ML[h2][hq])[p0:p1, :, 0:n, 0:n]
                    i4 = v3(b[p0:p1, :], 4)[:, 2 * hq:2 * hq + 2, 0:n]
                    k.tt(o4, bc(i4.unsqueeze(2), [64, 2, n, n]), i16(p0, p1, [64, 2, n, n], 1), ALU.mult)
                    k.memset(ML[h2][hq][q0:q1, :], 0.0)
        b = k.bank()
        for hp in range(4):
            k.tr(b[:, hp * 128:hp * 128 + n], LW[0:n, hp * 128:(hp + 1) * 128], cid(n))
        WT = G[18]
        k.act(v3(WT[:, :], 4)[:, :, 0:n], v3(b[:, :], 4)[:, :, 0:n], AF.Exp)
        MT = G[26]
        Mst = (G[19], G[20])

        def load_state(s_):
            k.dma(v3(MT[0:64, :], 8), st_rw[l, s_].rearrange("h v k -> v h k"))
            b_ = k.bank()
            for hp in range(4):
                k.tr(b_[:, hp * 64:(hp + 1) * 64], MT[0:64, hp * 128:(hp + 1) * 128], cid(64))
            Ms = v3(Mst[s_ % 2][:, 0:256], 4)
            k.cp(Ms, v3(b_[:, 0:256], 4))
            return Ms

        k.nrot = 6
        bKM, bRM = k.banks[6], k.banks[7]
        k.memset(bKM[0:n, :], 0.0)
        k.memset(bRM[0:n, :], 0.0)
        for s_ in range(n):
            Ms = load_state(s_)
            for h in range(8):
                hp, h2 = h // 2, h % 2
                for ML, bk in ((MLkk, bKM), (MLrg, bRM)):
                    lt = mlv(ML[h2][hp // 2])[:, hp % 2, s_, 0:n]
                    k.mm(bk[0:n, h * 64:(h + 1) * 64], lt, Ms[:, hp, :], start=False, stop=True, skip=True)
        U, o_, t1 = G[21], G[22], G[23]
        k.ts(U[0:n, :], bKM[0:n, :], -1.0, ALU.mult)
        k.tt(v3(o_[0:n, :], 8), v3(U[0:n, :], 8), bc(rb.unsqueeze(2), [n, 8, 64]), ALU.mult)
        k.tt(v3(t1[0:n, :], 8), v3(v_, 8), bc(rk2.unsqueeze(2), [n, 8, 64]), ALU.mult)
        k.tt(o_[0:n, :], o_[0:n, :], t1[0:n, :], ALU.add)
        k.tt(o_[0:n, :], o_[0:n, :], bRM[0:n, :], ALU.add)
        k.nrot = 7
        um, vm, Mo = G[23], G[24], G[25]
        for s_ in range(n):
            Ms = load_state(s_)
            k.ts(um[0:n, :], U[0:n, :], CM[0:n, C_ID, s_:s_ + 1], ALU.mult)
            k.ts(vm[0:n, :], v_, CM[0:n, C_ID, s_:s_ + 1], ALU.mult)
            b = k.bank()
            for hp in range(4):
                cs = slice(hp * 128, (hp + 1) * 128)
                k.mm(b[:, cs], Bt[0:n, cs], um[0:n, cs], start=True, stop=False)
                k.mm(b[:, cs], K2[0:n, cs], vm[0:n, cs], start=False, stop=True)
            Mo3 = v3(Mo[:, 0:256], 4)
            for hp in range(4):
                for h2 in range(2):
                    pr = slice(h2 * 64, h2 * 64 + 64)
                    k.stt(Mo3[pr, hp, :], Ms[pr, hp, :], v3(WT[:, :], 4)[pr, hp, s_:s_ + 1],
                          b[pr, hp * 128 + h2 * 64: hp * 128 + (h2 + 1) * 64], ALU.mult, ALU.add)
            b2 = k.bank()
            for hp in range(4):
                k.tr(b2[0:64, hp * 128:(hp + 1) * 128], Mo3[:, hp, :], cid(128))
            so = stage[s_ % 2][0:64, 0:512]
            k.cp(so, b2[0:64, :])
            k.dma(s_rw[l, s_].rearrange("h v k -> v h k"), v3(so, 8), q="pool")
        rw_out(n, SEQ, o_, v_, rz, bsum, pc)

    def rw_out(n, row, src, v_, rz, bsum, pc):
        run_pre_tail()
        on, xc, sq, zs = G[0], G[25], G[2], G[3]
        if src is None:
            k.cp(on[0:n, :], PSO[0:n, :])
        else:
            on = src
        k.red(sm[0:n, 96:104], v3(on[0:n, :], 8))
        k.ts(sm[0:n, 96:104], sm[0:n, 96:104], 1.0 / 64.0, ALU.mult)
        k.tt(v3(xc[0:n, :], 8), v3(on[0:n, :], 8), bc(sm[0:n, 96:104].unsqueeze(2), [n, 8, 64]), ALU.subtract)
        k.tt(sq[0:n, :], xc[0:n, :], xc[0:n, :], ALU.mult)
        k.red(sm[0:n, 104:112], v3(sq[0:n, :], 8))
        k.act(sm[0:n, 112:120], sm[0:n, 104:112], AF.Sqrt, bias=64e-5, scale=1.0 / 64.0)
        k.recip(sm[0:n, 120:128], sm[0:n, 112:120])
        k.tt(v3(xc[0:n, :], 8), v3(xc[0:n, :], 8), bc(sm[0:n, 120:128].unsqueeze(2), [n, 8, 64]), ALU.mult)
        k.tt(xc[0:n, :], xc[0:n, :], pc("lw"), ALU.mult)
        k.tt(xc[0:n, :], xc[0:n, :], pc("lb"), ALU.add)
        k.tt(v3(sq[0:n, :], 8), v3(v_, 8), bc(bsum.unsqueeze(2), [n, 8, 64]), ALU.mult)
        k.tt(xc[0:n, :], xc[0:n, :], sq[0:n, :], ALU.add)
        k.act(zs[0:n, :], rz, AF.Silu)
        k.tt(xc[0:n, :], xc[0:n, :], zs[0:n, :], ALU.mult)
        k.dma(omix[row:row + n, 1024:1536], xc[0:n, :], q="pool")
        run_post_tail()

    def out_phase(l):
        if k.preloaded != (l, "out"):
            load_weights(l, 0, D, 0, rows_kc=12, src=w_out)
        last_layer = (l == LAYERS - 1)
        if last_layer:
            bload(PC[:, 0:D], final_norm_g[0:1, :], D)
        extract_mod("prompt", 128, 2 * D, D)
        for si, seg in enumerate(psegs + [sbatch]):
            kind, idx, n, row = seg
            xt = XB[si % 2]
            seg_mod(seg, 2 * D, D)
            k.dma(xt[0:n, :], xsrc(l, seg))
            k.dma(stage[0][0:n, 0:1024], omix[row:row + n, 0:1024])
            k.dma(stage[1][0:n, 0:512], omix[row:row + n, 1024:1536])
            k.cp(ob[0:n, 0:1024], stage[0][0:n, 0:1024], eng="act")
            k.cp(ob[0:n, 1024:1536], stage[1][0:n, 0:512], eng="dve")
            for grp, (k0, k1) in enumerate(((0, 8), (8, 12))):
                b = k.bank()
                bb = b[:, :].bitcast(BF16)
                for kc in range(k0, k1):
                    k.tr(bb[:, (kc - k0) * 128:(kc - k0) * 128 + n], ob[0:n, kc * 128:(kc + 1) * 128], identb[0:n, 0:n])
                k.cp(oT[:, k0:k1, 0:n], v3(bb, 8)[:, 0:k1 - k0, 0:n])
            for c in range(2):
                b = k.bank()
                for kc in range(12):
                    k.mm(b[0:n, :], oT[:, kc, 0:n], W[:, kc * D + c * 512: kc * D + (c + 1) * 512],
                         start=(kc == 0), stop=(kc == 11), rkey=f"W{kc}")
                g = G[c]
                k.tt(g[0:n, :], b[0:n, :], modseg[0:n, c * 512:(c + 1) * 512], ALU.mult)
                k.tt(xt[0:n, c * 512:(c + 1) * 512], xt[0:n, c * 512:(c + 1) * 512], g[0:n, :], ALU.add)
            if not last_layer:
                k.dma(xmid[row:row + n, :], xt[0:n, :], q="pool")
            else:
                yt = (cq, PB[0])[si % 2][0:n, 0:D]
                k.memset(sm[0:n, 0:1], 0.0)
                k.e("act", "activation", out=yt, in_=xt[0:n, :], func=AF.Square, accum_out=sm[0:n, 0:1])
                k.act(sm[0:n, 1:2], sm[0:n, 0:1], AF.Sqrt, bias=EPS, scale=1.0 / D)
                k.recip(sm[0:n, 2:3], sm[0:n, 1:2])
                k.stt(yt, xt[0:n, :], sm[0:n, 2:3], PC[0:n, 0:D], ALU.mult, ALU.mult)
                dst = y_p[idx * 128:(idx + 1) * 128, :] if kind == "p" else y_s[0:NS, :]
                k.dma(dst, yt, q="pool")

    for l in range(LAYERS):
        layer_mod(l)
        if "dn" in PHASES:
            dn_phase(l)
        if "gla" in PHASES:
            gla_phase(l)
        if "rw" in PHASES:
            rw_phase(l)
        if "out" in PHASES:
            out_phase(l)
    k.S.finish()
    return nc, k


_WNAMES = ["norm_g", "ada_w", "ada_b", "w_in", "dn_conv_w", "dn_a_log", "dn_dt_bias", "dn_norm_g", "gla_wf",
           "gla_bf", "gla_norm_g", "rw_mu", "rw_w0", "rw_w2", "rw_a0", "rw_a2", "rw_k_k", "rw_k_a", "rw_r_k",
           "rw_ln_w", "rw_ln_b", "w_out", "final_norm_g"]


def make_in_maps(inputs, SEQ, NS=NSAMP):
    f = lambda a: np.ascontiguousarray(np.asarray(a, dtype=np.float32))
    shared = {nm: f(inputs[nm]) for nm in _WNAMES}
    shared["rw_r_k"] = shared["rw_r_k"].reshape(2, 512)
    shared["final_norm_g"] = shared["final_norm_g"].reshape(1, D)
    shared["cm"] = make_consts(NS)
    maps = []
    for c in range(NCORES):
        sl = slice(c * NS, (c + 1) * NS)
        m = dict(shared)
        m["xp"] = f(inputs["x_prompt"][c, :SEQ])
        m["xs"] = f(inputs["x_sample"][sl, 0])
        m["call"] = f(np.concatenate([inputs["c_sample"][sl], inputs["c_prompt"][c:c + 1]], axis=0))
        m["st_conv"] = f(inputs["state_dn_conv"][:, sl])
        m["st_dn"] = f(inputs["state_dn"][:, sl])
        m["st_gla"] = f(inputs["state_gla"][:, sl])
        m["st_rs"] = f(inputs["state_rwkv_shift"][:, sl])
        m["st_rw"] = f(inputs["state_rwkv"][:, sl])
        maps.append(m)
    return maps


def gather(res, SEQ):
    R = res.results
    cat = lambda nm, ax: np.concatenate([np.asarray(r[nm]) for r in R], axis=ax)
    stk = lambda nm: np.stack([np.asarray(r[nm]) for r in R], axis=1)
    y_p = np.stack([np.asarray(r["y_p"]) for r in R], axis=0)
    y_s = cat("y_s", 0)[:, None, :]
    return (y_p.astype(np.float32), y_s.astype(np.float32), stk("p_conv"), stk("p_dn"), stk("p_gla"), stk("p_rs"),
            stk("p_rw"), cat("s_conv", 1), cat("s_dn", 1), cat("s_gla", 1), cat("s_rs", 1), cat("s_rw", 1))


_CACHE = {}


def kernel(**inputs):
    SEQ = int(np.asarray(inputs["x_prompt"]).shape[1])
    if SEQ not in _CACHE:
        _CACHE[SEQ] = build(SEQ=SEQ)[0]
    nc = _CACHE[SEQ]
    maps = make_in_maps(inputs, SEQ)
    res = run_bass_kernel_spmd(nc, maps, core_ids=list(range(NCORES)))
    return gather(res, SEQ)
```

```python
import numpy as np
import concourse.bass as bass
import concourse.mybir as mybir
from concourse.bass_utils import run_bass_kernel_spmd

F32 = mybir.dt.float32
BF16 = mybir.dt.bfloat16
AF = mybir.ActivationFunctionType
ALU = mybir.AluOpType
AX = mybir.AxisListType

D = 1024
NSAMP = 16
NCORES = 8
EPS = 1e-6
ENGS = ("pe", "dve", "act", "pool", "sp")
N_DMA_SEMS = 32
SEM_LIMIT = 30000
USE_R = True
BATCH_S = True
SAME_ENGINE_SYNC = True

(C_ID, C_ONES, C_TRIU, C_SH1, C_SH2, C_SH3, C_BD1, C_BD2, C_BD3, C_SELB128, C_A1, C_SELB1, C_E0, C_SELP,
 C_TRIUS, C_NTRIL64, C_NTRIU64, C_LOWLEFT, C_MLOW, C_MUP) = range(20)
C_I16 = 20
NCM = 22
NCMR = 14
F32R = mybir.dt.float32r


def make_consts(NS=NSAMP):
    cm = np.zeros((128, NCM, 128), np.float32)
    i = np.arange(128)[:, None]
    j = np.arange(128)[None, :]
    same = (i // 64) == (j // 64)
    cm[:, C_ID] = (i == j)
    cm[:, C_ONES] = 1.0
    cm[:, C_TRIU] = (i <= j)
    cm[:, C_TRIUS] = (i < j)
    cm[:, C_NTRIL64] = -1.0 * ((i > j) & same)
    cm[:, C_NTRIU64] = -1.0 * ((i < j) & same)
    cm[:, C_LOWLEFT] = ((i >= 64) & (j < 64))
    cm[:, C_MLOW] = np.where(i >= j, 0.0, -1e30)
    cm[:, C_MUP] = np.where(j >= i, 0.0, -1e30)
    cm[:, C_SH1] = (i == j - 1)
    cm[:, C_SH2] = (i == j - 2)
    cm[:, C_SH3] = (i == j - 3)
    for d, c in ((1, C_BD1), (2, C_BD2), (3, C_BD3)):
        cm[:, c] = ((i == 3 + j - d) & (i < 3) & (j < d))
    cm[:, C_SELB128] = ((i == 125 + j) & (j < 3))
    cm[:, C_A1] = ((i == j + 1) & (i < 3) & (j < 2))
    cm[0, C_SELB1, 2] = 1.0
    cm[0, C_E0, 0] = 1.0
    cm[NS, C_SELP, :] = 1.0
    cm[:, C_I16:C_I16 + 2, :] = np.eye(16, dtype=np.float32).reshape(1, 2, 128)
    return cm


class Buf:
    __slots__ = ("last_w", "reads", "excl")

    def __init__(self, excl):
        self.last_w = None
        self.reads = []
        self.excl = excl


class Sched:
    def __init__(self, nc):
        self.nc = nc
        self.streams = {e: [] for e in ENGS}
        self.gen = {e: 0 for e in ENGS}
        self.sems = {}
        for e in ENGS:
            self.sems[(e, 0)] = nc.alloc_semaphore(name=f"s_{e}0")
        self.cnt = {e: 0 for e in ENGS}
        self.dsems = [nc.alloc_semaphore(name=f"s_dma{i}") for i in range(N_DMA_SEMS)]
        self.dcnt = [0] * N_DMA_SEMS
        self.dnext = 0
        self.dnext2 = 0
        self.dnext3 = 0
        self.seen = {e: {} for e in ENGS}
        self.pe_waited = {}
        self.pe_rank = {}
        self.bufs = {}
        self.n_instr = 0
        self.n_wait = 0

    def buf(self, key):
        b = self.bufs.get(key)
        if b is None:
            b = self.bufs[key] = Buf(key.startswith("ps"))
        return b

    def _sem(self, key):
        return self.dsems[key] if isinstance(key, int) else self.sems[key]

    def _deps(self, eng, rb, wb):
        need = {}

        def add(ev):
            if ev is None:
                return
            k, v = ev
            if not isinstance(k, int) and k[0] == eng and (eng == "pe" or not SAME_ENGINE_SYNC):
                return
            if need.get(k, 0) < v:
                need[k] = v

        for b in rb:
            add(b.last_w)
            if b.excl:
                for r in b.reads:
                    add(r)
        for b in wb:
            add(b.last_w)
            for r in b.reads:
                add(r)
        out = []
        seen = self.seen[eng]
        for k, v in need.items():
            if seen.get(k, 0) < v:
                seen[k] = v
                out.append((k, v))
                if not isinstance(k, int) and k[0] == "pe":
                    self.pe_waited.setdefault(k, set()).add(v)
        return out

    def _val(self, k, v):
        if not isinstance(k, int) and k[0] == "pe":
            return self.pe_rank[k][v]
        return v

    def _commit(self, ev, rb, wb):
        for b in rb:
            if b.excl:
                b.last_w = ev
                b.reads = []
            else:
                b.reads.append(ev)
                if len(b.reads) > 48:
                    best = {}
                    for k, v in b.reads:
                        if best.get(k, 0) < v:
                            best[k] = v
                    b.reads = list(best.items())
        for b in wb:
            b.last_w = ev
            b.reads = []

    def op(self, eng, fn, reads, writes, desc=None):
        if not hasattr(self, "meta"):
            self.meta = {e: [] for e in ENGS}
        rb = [self.buf(k) for k in reads]
        wb = [self.buf(k) for k in writes]
        wl = self._deps(eng, rb, wb)
        if self.cnt[eng] >= SEM_LIMIT:
            self.gen[eng] += 1
            self.cnt[eng] = 0
            self.sems[(eng, self.gen[eng])] = self.nc.alloc_semaphore(name=f"s_{eng}{self.gen[eng]}")
        self.cnt[eng] += 1
        key = (eng, self.gen[eng])
        ev = (key, self.cnt[eng])
        sem = self.sems[key]
        self.n_wait += len(wl)
        self.n_instr += 1
        self.meta[eng].append((len(wl), desc))

        def emit(e, fn=fn, wl=wl, sem=sem, key=key, n_=self.cnt[eng]):
            for k_, v in wl:
                e.wait_ge(self._sem(k_), self._val(k_, v))
            ins = fn(e)
            if key[0] != "pe" or n_ in self.pe_rank.get(key, ()):
                ins.then_inc(sem, 1)

        self.streams[eng].append(emit)
        self._commit(ev, rb, wb)

    def dma(self, q, out, in_, reads, writes):
        rb = [self.buf(k) for k in reads]
        wb = [self.buf(k) for k in writes]
        wl = self._deps(q, rb, wb)
        if q == "sp":
            j = self.dnext
            self.dnext = (self.dnext + 1) % 16
        elif q == "pool":
            j = 16 + self.dnext2
            self.dnext2 = (self.dnext2 + 1) % 8
        else:
            j = 24 + self.dnext3
            self.dnext3 = (self.dnext3 + 1) % 8
        prev = self.dcnt[j]
        if prev > 0 and self.seen[q].get(j, 0) < prev:
            self.seen[q][j] = prev
            wl.append((j, prev))
        self.dcnt[j] += 16
        ev = (j, self.dcnt[j])
        dsem = self.dsems[j]
        self.n_wait += len(wl)
        self.n_instr += 1

        def emit(e, wl=wl, dsem=dsem, out=out, in_=in_):
            for k_, v in wl:
                e.wait_ge(self._sem(k_), self._val(k_, v))
            e.dma_start(out=out, in_=in_).then_inc(dsem, 16)

        self.streams[q].append(emit)
        self._commit(ev, rb, wb)

    def finish(self):
        wl = [(j, self.dcnt[j]) for j in range(N_DMA_SEMS) if self.dcnt[j] > 0]
        for e in ENGS:
            if e not in ("sp",) and self.cnt[e] > 0:
                wl.append(((e, self.gen[e]), self.cnt[e]))
        if self.cnt["pe"] > 0:
            self.pe_waited.setdefault(("pe", self.gen["pe"]), set()).add(self.cnt["pe"])
        for key, vs in self.pe_waited.items():
            self.pe_rank[key] = {v: i + 1 for i, v in enumerate(sorted(vs))}

        def emit(e, wl=wl):
            for k_, v in wl:
                e.wait_ge(self._sem(k_), self._val(k_, v))

        self.streams["sp"].append(emit)
        streams = self.streams
        with self.nc.Block() as block:
            @block.tensor
            def _(e):
                for f in streams["pe"]:
                    f(e)

            @block.vector
            def _(e):
                for f in streams["dve"]:
                    f(e)

            @block.scalar
            def _(e):
                for f in streams["act"]:
                    f(e)

            @block.gpsimd
            def _(e):
                for f in streams["pool"]:
                    f(e)

            @block.sync
            def _(e):
                for f in streams["sp"]:
                    f(e)


def _isap(v):
    return hasattr(v, "tensor") and hasattr(v, "ap")


class K:
    def __init__(self, nc):
        self.nc = nc
        self.S = Sched(nc)
        self.banks = [nc.alloc_psum_tensor(f"ps{i}", [128, 512], F32) for i in range(8)]
        self.nb = 0
        self.nrot = 7
        self.ncp = 0
        self.R = set()

    def bank(self):
        b = self.banks[self.nb % self.nrot]
        self.nb += 1
        return b

    def e(self, eng, method, _keys=None, **kw):
        reads, writes = [], []
        for name, val in list(kw.items()):
            if _isap(val):
                if _keys and name in _keys:
                    kk = _keys[name]
                    (writes if name in ("out", "accum_out", "ap") else reads).extend(kk if isinstance(kk, list) else [kk])
                elif name in ("out", "accum_out", "ap"):
                    writes.append(val.name)
                    if val.name in self.R and val.dtype == F32 and name != "ap":
                        kw[name] = val.bitcast(F32R)
                else:
                    reads.append(val.name)
        desc = method + " " + " ".join(f"{n_}={tuple(v_.shape)}:{v_.dtype}@{v_.name}" for n_, v_ in kw.items() if _isap(v_))
        self.S.op(eng, lambda e: getattr(e, method)(**kw), reads, writes, desc)

    def dma(self, out, in_, q="sp", wkey=None):
        self.S.dma(q, out, in_, [in_.name], [wkey if wkey is not None else out.name])

    def mm(self, out, lhsT, rhs, start=True, stop=True, skip=False, rkey=None):
        if rkey is not None:
            self.e("pe", "matmul", _keys={"rhs": rkey}, out=out, lhsT=lhsT, rhs=rhs, start=start, stop=stop)
            return
        if (lhsT.name in self.R and rhs.name in self.R and lhsT.dtype == F32 and rhs.dtype == F32
                and lhsT.shape[0] >= 2 and lhsT.shape[-1] >= 2 and rhs.shape[-1] >= 2):
            lhsT = lhsT.bitcast(F32R)
            rhs = rhs.bitcast(F32R)
        if skip:
            self.e("pe", "matmul", out=out, lhsT=lhsT, rhs=rhs, start=start, stop=stop, skip_group_check=True)
        else:
            self.e("pe", "matmul", out=out, lhsT=lhsT, rhs=rhs, start=start, stop=stop)

    def tr(self, out, in_, ident):
        self.e("pe", "transpose", out=out, in_=in_, identity=ident)

    def tt(self, out, a, b, op, eng="dve"):
        self.e(eng, "tensor_tensor", out=out, in0=a, in1=b, op=op)

    def ts(self, out, a, s1, op0, s2=None, op1=None, eng="dve"):
        if op1 is None:
            self.e(eng, "tensor_scalar", out=out, in0=a, scalar1=s1, scalar2=None, op0=op0)
        else:
            self.e(eng, "tensor_scalar", out=out, in0=a, scalar1=s1, scalar2=s2, op0=op0, op1=op1)

    def stt(self, out, a, s, b, op0, op1, eng="dve"):
        self.e(eng, "scalar_tensor_tensor", out=out, in0=a, scalar=s, in1=b, op0=op0, op1=op1)

    def red(self, out, in_, op=ALU.add):
        self.e("dve", "tensor_reduce", out=out, in_=in_, axis=AX.X, op=op)

    def recip(self, out, in_):
        self.e("dve", "reciprocal", out=out, in_=in_)

    def act(self, out, in_, func, bias=None, scale=None):
        kw = dict(out=out, in_=in_, func=func)
        if bias is not None:
            kw["bias"] = bias
        if scale is not None:
            kw["scale"] = scale
        self.e("act", "activation", **kw)

    def cp(self, out, in_, eng=None, okey=None, ikey=None):
        if eng is None:
            eng = ("dve", "act")[self.ncp % 2]
            self.ncp += 1
        kk = {"out": okey} if okey is not None else None
        if ikey is not None:
            kk = dict(kk or {}, in_=ikey)
        if eng == "act":
            self.e("act", "copy", _keys=kk, out=out, in_=in_)
        else:
            self.e(eng, "tensor_copy", _keys=kk, out=out, in_=in_)

    def memset(self, ap, val, eng="dve"):
        self.e(eng, "memset", ap=ap, constant=val)


class Ref:
    def __init__(self, t):
        self.t = t

    def __getitem__(self, key):
        return self.t[key]


def v3(ap, a):
    return ap.rearrange("p (a b) -> p a b", a=a)


def bc(ap, shape):
    return ap.to_broadcast(list(shape))


DN_C0, DN_W = 0, 2056
GLA_C0, GLA_W = 2056, 1552
RW_C0, RW_W = 3608, 2176
NG = 27


def build(SEQ=2048, NS=NSAMP, LAYERS=2, PHASES=("dn", "gla", "rw", "out")):
    nc = bass.Bass("TRN2", target_bir_lowering=False)
    NT = SEQ // 128
    NTOK = SEQ + NS

    def din(name, shape):
        return nc.dram_tensor(name, list(shape), F32, kind="ExternalInput").ap()

    def dout(name, shape):
        return nc.dram_tensor(name, list(shape), F32, kind="ExternalOutput").ap()

    xp = din("xp", [SEQ, D]); xs = din("xs", [NS, D]); call = din("call", [NS + 1, D])
    st_conv = din("st_conv", [2, NS, 3, 1536]); st_dn = din("st_dn", [2, NS, 4, 128, 128])
    st_gla = din("st_gla", [2, NS, 4, 64, 128]); st_rs = din("st_rs", [2, NS, 1664])
    st_rw = din("st_rw", [2, NS, 8, 64, 64])
    cmd = din("cm", [128, NCM, 128])
    norm_g = din("norm_g", [2, D]); ada_w = din("ada_w", [2, D, 3 * D]); ada_b = din("ada_b", [2, 3 * D])
    w_in = din("w_in", [2, D, 5784]); dn_conv_w = din("dn_conv_w", [2, 4, 1536])
    dn_a_log = din("dn_a_log", [2, 4]); dn_dt_bias = din("dn_dt_bias", [2, 4]); dn_norm_g = din("dn_norm_g", [2, 128])
    gla_wf = din("gla_wf", [2, 16, 256]); gla_bf = din("gla_bf", [2, 256]); gla_norm_g = din("gla_norm_g", [2, 128])
    rw_mu = din("rw_mu", [2, 1664]); rw_w0 = din("rw_w0", [2, 512]); rw_w2 = din("rw_w2", [2, 64, 512])
    rw_a0 = din("rw_a0", [2, 512]); rw_a2 = din("rw_a2", [2, 64, 512]); rw_k_k = din("rw_k_k", [2, 512])
    rw_k_a = din("rw_k_a", [2, 512]); rw_r_k = din("rw_r_k", [2, 512]); rw_ln_w = din("rw_ln_w", [2, 512])
    rw_ln_b = din("rw_ln_b", [2, 512]); w_out = din("w_out", [2, 1536, D]); final_norm_g = din("final_norm_g", [1, D])

    y_p = dout("y_p", [SEQ, D]); y_s = dout("y_s", [NS, D])
    p_conv = dout("p_conv", [2, 3, 1536]); p_dn = dout("p_dn", [2, 4, 128, 128]); p_gla = dout("p_gla", [2, 4, 64, 128])
    p_rs = dout("p_rs", [2, 1664]); p_rw = dout("p_rw", [2, 8, 64, 64])
    s_conv = dout("s_conv", [2, NS, 3, 1536]); s_dn = dout("s_dn", [2, NS, 4, 128, 128])
    s_gla = dout("s_gla", [2, NS, 4, 64, 128]); s_rs = dout("s_rs", [2, NS, 1664]); s_rw = dout("s_rw", [2, NS, 8, 64, 64])

    xmid = nc.dram_tensor("xmid", [NTOK, D], F32).ap()
    omix = nc.dram_tensor("omix", [NTOK, 1536], F32).ap()
    projs = nc.dram_tensor("projs", [NS, RW_W], F32).ap()
    modall = nc.dram_tensor("modall", [NS + 1, 3 * D], F32).ap()
    hts = nc.dram_tensor("hts", [NT + 1, 128, 8 * 128], BF16).ap()

    k = K(nc)
    sb = nc.alloc_sbuf_tensor

    CM = sb("CM", [128, NCM, 128], F32)
    identb = sb("identb", [128, 128], BF16)
    W = sb("W", [128, 8 * RW_W], BF16)
    stage = [sb(f"stage{i}", [128, 1088], F32) for i in range(2)]
    modseg = sb("modseg", [128, 2 * D], F32)
    PC = sb("PC", [128, 6912], F32)
    x_t = sb("x_t", [128, D], F32)
    hb = sb("hb", [128, D], BF16)
    HB = [sb("hT", [128, 8, 128], BF16), sb("hT1", [128, 8, 128], BF16)]
    PB = [sb("proj", [128, RW_W], F32), sb("proj1", [128, RW_W], F32)]
    hT = Ref(HB[0])
    proj = Ref(PB[0])
    k.pre_tail = None
    k.post_tail = None
    k.next_load = None
    k.preloaded = None
    cq = sb("cq", [128, 1664], F32)
    cT = sb("cT", [128, 8, NS + 1], F32)
    TLS = sb("TLS", [3, 1664], F32)
    TL = TLS[0:3, 0:1536]
    srow = TLS[0:1, 0:1664]
    S_dn = sb("S_dn", [128, 4, 128], F32)
    S_gla = sb("S_gla", [128, 2, 128], F32)
    M_rw = sb("M_rw", [128, 4, 64], F32)
    sm = sb("sm", [128, 128], F32)
    sm2 = sb("sm2", [128, 64], F32)
    wsm = sb("wsm", [64, 1024], F32)
    ob = modseg[:, 1024:1792].bitcast(BF16)
    XB = [x_t, sb("x_t2", [128, D], F32)]
    oT = sb("oT", [128, 12, 128], BF16)
    G = [sb(f"g{i}", [128, 512], F32) for i in range(NG)]
    PSO = k.banks[7]

    CMR = sb("CMR", [128, NCMR, 128], F32)
    if USE_R:
        k.R.update(["CMR", "cq", "proj", "proj1", "S_dn", "S_gla", "M_rw"] + [f"g{i}" for i in range(26)])

    def cmat(i, r, c):
        return CMR[0:r, i, 0:c] if USE_R else CM[0:r, i, 0:c]

    def cid(n):
        return CM[0:n, C_ID, 0:n]

    k.dma(CM[:], cmd)
    k.cp(identb[:], CM[:, C_ID, :], eng="dve")
    k.cp(CMR[:], CM[:, 0:NCMR, :], eng="dve")
    k.dma(x_t[0:NS + 1, :], call)
    k.act(x_t[0:NS + 1, :], x_t[0:NS + 1, :], AF.Silu)
    b_ = k.bank()
    for kc in range(8):
        k.tr(b_[:, kc * 32:kc * 32 + NS + 1], x_t[0:NS + 1, kc * 128:(kc + 1) * 128], cid(NS + 1))
    k.cp(cT[:], v3(b_[:, 0:256], 8)[:, :, 0:NS + 1], eng="dve")

    psegs = [("p", t, 128, t * 128) for t in range(NT)]
    ssegs = [("s", s, 1, SEQ + s) for s in range(NS)]
    sbatch = ("sb", 0, NS, SEQ)

    def xsrc(l, seg):
        kind, idx, n, row = seg
        if l == 0:
            return xp[idx * 128:(idx + 1) * 128, :] if kind == "p" else xs[idx:idx + n, :]
        return xmid[row:row + n, :]

    def front_into(l, seg, wcols, cache, slot, defer=False):
        save = (proj.t, hT.t)
        proj.t, hT.t = PB[slot], HB[slot]
        r = front(l, seg, wcols, cache, defer)
        proj.t, hT.t = save
        return r

    def run_post_tail():
        f = k.post_tail
        k.post_tail = None
        if f is not None:
            f()

    def run_pre_tail():
        f = k.pre_tail
        k.pre_tail = None
        if f is not None:
            k.post_tail = f()

    def mixer_pass(l, wcols, pre, core, post, batch_core=None):
        cache = None
        if "dn" in PHASES:
            cache = "store" if wcols == DN_W else "load"
        front_into(l, psegs[0], wcols, cache, 0)
        for t, seg in enumerate(psegs):
            proj.t, hT.t = PB[t % 2], HB[t % 2]
            nxt = psegs[t + 1] if t + 1 < NT else sbatch
            k.pre_tail = (lambda nxt=nxt, t=t: front_into(l, nxt, wcols, cache, (t + 1) % 2, defer=True))
            core(l, seg)
            run_pre_tail()
            run_post_tail()
            post(seg)
        proj.t, hT.t = PB[NT % 2], HB[NT % 2]
        if k.next_load is not None:
            k.next_load()
            k.next_load = None
        batch_core(l)

    def bload(dst, src_row, width):
        k.dma(dst, bc(src_row, [128, width]))

    def load_weights(l, c0, width, dst_off, rows_kc=8, src=None):
        src = w_in if src is None else src
        k.e("dve", "memset", _keys={"ap": [f"W{j}" for j in range(12)]}, ap=W[:, 0:2], constant=0.0)
        for kc in range(rows_kc):
            half = -(-width // 2)
            for a in (0, half):
                wd = min(half, width - a)
                dst = W[:, dst_off + kc * width + a: dst_off + kc * width + a + wd]
                k.dma(dst, src[l, kc * 128:(kc + 1) * 128, c0 + a:c0 + a + wd], q="pool", wkey=f"W{kc}")

    def extract_mod(lhsT, n, c0, ncols):
        if lhsT == "prompt":
            k.dma(modseg[0:n, 0:ncols], bc(modall[NS:NS + 1, c0:c0 + ncols], [n, ncols]))
        else:
            k.dma(modseg[0:n, 0:ncols], modall[0:n, c0:c0 + ncols])

    def seg_mod(seg, c0, ncols):
        kind, idx, n, row = seg
        if kind == "sb":
            extract_mod("batch", NS, c0, ncols)

    def front(l, seg, wcols, cache=None, defer=False):
        kind, idx, n, row = seg
        slot = hts[idx if kind == "p" else NT].rearrange("p (a b) -> p a b", a=8)[:, :, 0:n]
        if cache == "load":
            k.dma(hT[:, :, 0:n], slot)
        else:
            front_h(l, seg)
            if cache == "store":
                k.dma(slot, hT[:, :, 0:n], q="pool")
        c = 0
        evacs = []
        pt = proj.t
        while c < wcols:
            wd = min(512, wcols - c)
            b = k.bank()
            for kc in range(8):
                k.mm(b[0:n, 0:wd], hT[:, kc, 0:n], W[:, kc * wcols + c: kc * wcols + c + wd],
                     start=(kc == 0), stop=(kc == 7), rkey=f"W{kc}")
            evacs.append((pt[0:n, c:c + wd], b[0:n, 0:wd]))
            c += wd

        def do_evacs():
            for o_, i_ in evacs:
                k.cp(o_, i_)

        if defer:
            return do_evacs
        do_evacs()
        return None

    def front_h(l, seg):
        kind, idx, n, row = seg
        seg_mod(seg, 0, 2 * D)
        k.dma(x_t[0:n, :], xsrc(l, seg))
        hf = cq[0:n, 0:D]
        k.memset(sm[0:n, 0:1], 0.0)
        k.e("act", "activation", out=hf, in_=x_t[0:n, :], func=AF.Square, accum_out=sm[0:n, 0:1])
        k.act(sm[0:n, 1:2], sm[0:n, 0:1], AF.Sqrt, bias=EPS, scale=1.0 / D)
        k.recip(sm[0:n, 2:3], sm[0:n, 1:2])
        k.stt(hf, x_t[0:n, :], sm[0:n, 2:3], modseg[0:n, D:2 * D], ALU.mult, ALU.mult)
        k.tt(hb[0:n, :], hf, modseg[0:n, 0:D], ALU.add)
        b = k.bank()
        bb = b[:, :].bitcast(BF16)
        for kc in range(8):
            k.tr(bb[:, kc * 128:kc * 128 + n], hb[0:n, kc * 128:(kc + 1) * 128], identb[0:n, 0:n])
        k.cp(hT[:, :, 0:n], v3(bb, 8)[:, :, 0:n], eng="dve")

    def rms_gate_store(seg, nh, hd, gain, gate_in, col0, eps, src=None, otile=None):
        kind, idx, n, row = seg
        run_pre_tail()
        o3 = v3(PSO[0:n, :], nh)
        sq, on, zs = G[0], (G[1] if otile is None else otile), G[2]
        if src is None:
            k.cp(on[0:n, :], PSO[0:n, :])
        else:
            on = src
        k.tt(sq[0:n, :], on[0:n, :], on[0:n, :], ALU.mult)
        k.red(sm[0:n, 16:16 + nh], v3(sq[0:n, :], nh))
        k.act(sm[0:n, 24:24 + nh], sm[0:n, 16:16 + nh], AF.Sqrt, bias=eps, scale=1.0 / hd)
        k.recip(sm[0:n, 32:32 + nh], sm[0:n, 24:24 + nh])
        k.tt(v3(on[0:n, :], nh), v3(on[0:n, :], nh), bc(sm[0:n, 32:32 + nh].unsqueeze(2), [n, nh, hd]), ALU.mult)
        k.tt(v3(on[0:n, :], nh), v3(on[0:n, :], nh), bc(gain.unsqueeze(1), [n, nh, hd]), ALU.mult)
        k.act(zs[0:n, :], gate_in, AF.Silu)
        k.tt(on[0:n, :], on[0:n, :], zs[0:n, :], ALU.mult)
        k.dma(omix[row:row + n, col0:col0 + 512], on[0:n, :], q="pool")
        run_post_tail()

    def neumann(n, Nbd, NbdT, Loff, tiles):
        idb = bc(CM[0:n, C_ID:C_ID + 1, 0:n], [n, 4, n])
        Xa, Xb, Pa, Pb, Qa, Qb = tiles

        def t3(t):
            return v3(t[0:n, :], 4)[:, :, 0:n]

        if n == 1:
            k.memset(t3(Xa), 1.0)
            return t3(Xa)
        X = t3(Xa)
        k.tt(X, NbdT, idb, ALU.add)
        P, Pt = Nbd, NbdT
        spare = [Xb, Pa, Pb, Qa, Qb]
        cur = {"X": Xa}
        pp = [Pa, Pb]
        qq = [Qa, Qb]
        xx = [Xb, Xa]
        for lvl in range(1, 6):
            bP = k.bank()
            for h in range(4):
                k.mm(v3(bP[0:n, :], 4)[:, h, 0:n], Pt[:, h, :], P[:, h, :])
            Pn = t3(pp[lvl % 2])
            k.cp(Pn, v3(bP[0:n, :], 4)[:, :, 0:n])
            if lvl < 5:
                bQ = k.bank()
                for h in range(4):
                    k.mm(v3(bQ[0:n, :], 4)[:, h, 0:n], P[:, h, :], Pt[:, h, :])
                Qn = t3(qq[lvl % 2])
                k.cp(Qn, v3(bQ[0:n, :], 4)[:, :, 0:n])
            bX = k.bank()
            for h in range(4):
                k.mm(v3(bX[0:n, :], 4)[:, h, 0:n], Pn[:, h, :], X[:, h, :])
            Xn = t3(xx[(lvl - 1) % 2])
            k.tt(Xn, X, v3(bX[0:n, :], 4)[:, :, 0:n], ALU.add)
            X = Xn
            P = Pn
            if lvl < 5:
                Pt = Qn
        bT = k.bank()
        for h in range(4):
            k.tr(v3(bT[0:n, :], 4)[:, h, 0:n], X[:, h, :], cid(n))
        Tbd = t3(Pa)
        k.cp(Tbd, v3(bT[0:n, :], 4)[:, :, 0:n])
        bY = k.bank()
        for h in range(4):
            k.mm(v3(bY[0:n, :], 4)[:, h, 0:n], Loff[:, h, :], X[:, h, :])
        Y = t3(Pb)
        k.cp(Y, v3(bY[0:n, :], 4)[:, :, 0:n])
        bZ = k.bank()
        for h in range(4):
            k.mm(v3(bZ[0:n, :], 4)[:, h, 0:n], Tbd[:, h, :], Y[:, h, :])
        Xf = t3(Qa)
        k.tt(Xf, X, v3(bZ[0:n, :], 4)[:, :, 0:n], ALU.subtract)
        return Xf

    def layer_mod(l):
        gi = 0
        for cc in range(6):
            b = k.bank()
            for kc in range(8):
                g = stage[gi % 2][:, (gi // 2 % 2) * 512:(gi // 2 % 2) * 512 + 512]
                gi += 1
                k.dma(g, ada_w[l, kc * 128:(kc + 1) * 128, cc * 512:(cc + 1) * 512])
                k.mm(b[0:NS + 1, :], cT[:, kc, :], g, start=(kc == 0), stop=(kc == 7))
            gb = hT[0:NS + 1, :, :].rearrange("p a b -> p (a b)").bitcast(F32)
            mo = hb[0:NS + 1, :].bitcast(F32)
            k.dma(gb, bc(ada_b[l:l + 1, cc * 512:(cc + 1) * 512], [NS + 1, 512]))
            k.tt(mo, b[0:NS + 1, :], gb, ALU.add)
            if cc in (2, 3):
                k.dma(x_t[0:NS + 1, 0:512], bc(norm_g[l:l + 1, (cc - 2) * 512:(cc - 1) * 512], [NS + 1, 512]))
                k.stt(mo, mo, 1.0, x_t[0:NS + 1, 0:512], ALU.add, ALU.mult)
            k.dma(modall[:, cc * 512:(cc + 1) * 512], mo, q="pool")

    def softplus_parts(out, x, n, w, t1, t2):
        k.act(t1, x, AF.Abs)
        k.act(t2, t1, AF.Exp, scale=-1.0)
        k.act(out, t2, AF.Ln, bias=1.0)

    def dn_phase(l):
        load_weights(l, DN_C0, DN_W, 0)
        for d_ in range(4):
            bload(PC[:, d_ * 1536:(d_ + 1) * 1536], dn_conv_w[l, 3 - d_:4 - d_, :], 1536)
        bload(PC[:, 6144:6272], dn_norm_g[l:l + 1, :], 128)
        bload(PC[:, 6272:6276], dn_a_log[l:l + 1, :], 4)
        bload(PC[:, 6276:6280], dn_dt_bias[l:l + 1, :], 4)
        k.act(PC[:, 6280:6284], PC[:, 6272:6276], AF.Exp)
        k.ts(PC[:, 6280:6284], PC[:, 6280:6284], -1.0, ALU.mult)
        extract_mod("prompt", 128, 0, 2 * D)
        k.memset(TL, 0.0)
        k.memset(S_dn[:], 0.0)
        def pre(seg):
            k.dma(TL, st_conv[l, seg[1]])
            k.dma(v3(stage[0][:, 0:512], 4), st_dn[l, seg[1]].rearrange("h k v -> k h v"))
            k.cp(S_dn[:], v3(stage[0][:, 0:512], 4), eng="dve")

        def post(seg):
            kind, idx, n, row = seg
            if kind == "s":
                k.dma(s_conv[l, idx], TL, q="pool")
                k.dma(s_dn[l, idx].rearrange("h k v -> k h v"), S_dn[:], q="pool")
            elif idx == NT - 1:
                k.dma(p_conv[l], TL, q="pool")
                k.dma(p_dn[l].rearrange("h k v -> k h v"), S_dn[:], q="pool")

        def _nl():
            load_weights(l, GLA_C0, GLA_W, 0)
            k.preloaded = (l, "gla")
        k.next_load = _nl
        mixer_pass(l, DN_W, pre, dn_segment, post, batch_core=dn_batch)

    def dn_segment(l, seg):
        kind, idx, n, row = seg
        qkv = proj[0:n, 0:1536]
        for c in range(3):
            cs = slice(c * 512, (c + 1) * 512)
            b = k.bank()
            ops = []
            for d_ in range(4):
                if n > d_:
                    g = G[d_]
                    k.tt(g[0:n, :], proj[0:n, cs], PC[0:n, d_ * 1536 + c * 512: d_ * 1536 + (c + 1) * 512], ALU.mult)
                    ops.append((cmat((C_ID, C_SH1, C_SH2, C_SH3)[d_], n, n), g[0:n, :]))
            for d_ in range(1, 4):
                g = G[3 + d_]
                k.tt(g[0:3, :], TL[0:3, cs], PC[0:3, d_ * 1536 + c * 512: d_ * 1536 + (c + 1) * 512], ALU.mult)
                ops.append((cmat((C_BD1, C_BD2, C_BD3)[d_ - 1], 3, n), g[0:3, :]))
            for i, (lt, rh) in enumerate(ops):
                k.mm(b[0:n, :], lt, rh, start=(i == 0), stop=(i == len(ops) - 1))
            k.act(cq[0:n, cs], b[0:n, :], AF.Silu)
        for c in range(3):
            cs = slice(c * 512, (c + 1) * 512)
            b = k.bank()
            if n == 128:
                k.mm(b[0:3, :], cmat(C_SELB128, 128, 3), proj[0:n, cs])
            else:
                k.mm(b[0:3, :], cmat(C_A1, 3, 3), TL[0:3, cs], start=True, stop=False)
                k.mm(b[0:3, :], cmat(C_SELB1, 1, 3), proj[0:1, cs], start=False, stop=True)
            k.cp(TL[0:3, cs], b[0:3, :])
        qn, kn, vv, beta, gg = dn_prep(n)
        dn_chunk(l, seg, qn, kn, vv, beta, gg)

    def dn_prep(n):
        sq = G[0]
        for hlf in range(2):
            k.tt(sq[0:n, :], cq[0:n, hlf * 512:(hlf + 1) * 512], cq[0:n, hlf * 512:(hlf + 1) * 512], ALU.mult)
            k.red(sm[0:n, 4 + hlf * 4:8 + hlf * 4], v3(sq[0:n, :], 4))
        k.act(sm[0:n, 12:20], sm[0:n, 4:12], AF.Sqrt, bias=EPS)
        k.recip(sm[0:n, 20:28], sm[0:n, 12:20])
        k.ts(sm[0:n, 20:24], sm[0:n, 20:24], 128.0 ** -0.5, ALU.mult)
        qk3 = v3(cq[0:n, 0:1024], 8)
        k.tt(qk3, qk3, bc(sm[0:n, 20:28].unsqueeze(2), [n, 8, 128]), ALU.mult)
        qn = cq[0:n, 0:512]
        kn = cq[0:n, 512:1024]
        vv = cq[0:n, 1024:1536]
        beta = sm[0:n, 28:32]
        k.act(beta, proj[0:n, 2048:2052], AF.Sigmoid)
        tt_ = sm[0:n, 32:36]
        k.tt(tt_, proj[0:n, 2052:2056], PC[0:n, 6276:6280], ALU.add)
        softplus_parts(sm[0:n, 36:40], tt_, n, 4, sm[0:n, 40:44], sm[0:n, 44:48])
        k.ts(sm[0:n, 40:44], tt_, 0.0, ALU.max)
        k.tt(sm[0:n, 40:44], sm[0:n, 40:44], sm[0:n, 36:40], ALU.add)
        gg = sm[0:n, 48:52]
        k.tt(gg, sm[0:n, 40:44], PC[0:n, 6280:6284], ALU.mult)
        return qn, kn, vv, beta, gg

    def i16(p0, p1, shape, axis):
        a = CM[p0:p1, C_I16:C_I16 + 2, :].rearrange("p a (b c) -> p (a b) c", c=16)[:, 0:NS, 0:NS]
        return bc(a.unsqueeze(axis), shape)

    def dn_batch(l):
        n = NS
        for c in range(3):
            cs = slice(c * 512, (c + 1) * 512)
            acc, tmp = G[0], G[1]
            k.tt(acc[0:n, :], proj[0:n, cs], PC[0:n, c * 512:(c + 1) * 512], ALU.mult)
            for d_ in range(1, 4):
                st = stage[d_ % 2][0:n, (d_ // 2) * 512:(d_ // 2) * 512 + 512]
                k.dma(st, st_conv[l, :, 3 - d_, cs])
                k.tt(tmp[0:n, :], st, PC[0:n, d_ * 1536 + c * 512: d_ * 1536 + (c + 1) * 512], ALU.mult)
                k.tt(acc[0:n, :], acc[0:n, :], tmp[0:n, :], ALU.add)
            k.act(cq[0:n, cs], acc[0:n, :], AF.Silu)
        k.dma(s_conv[l, :, 0:2, :], st_conv[l, :, 1:3, :], q="pool")
        k.dma(s_conv[l, :, 2, :], proj[0:n, 0:1536], q="pool")
        qn, kn, vv, beta, gg = dn_prep(n)
        eG = sm[0:n, 60:64]
        k.act(eG, gg, AF.Exp)
        beG = sm[0:n, 68:72]
        k.tt(beG, beta, eG, ALU.mult)

        def h3(t):
            return v3(t[0:n, :], 4)

        def bcs(s_):
            return bc(s_.unsqueeze(2), [n, 4, 128])

        kbg, vb, qg, t0 = G[0], G[1], G[2], G[3]
        k.tt(h3(kbg), v3(kn, 4), bcs(beG), ALU.mult)
        k.tt(h3(vb), v3(vv, 4), bcs(beta), ALU.mult)
        k.tt(h3(qg), v3(qn, 4), bcs(eG), ALU.mult)
        k.tt(t0[0:n, :], qn, kn, ALU.mult)
        Aqk = sm[0:n, 72:76]
        k.red(Aqk, h3(t0))
        MLk, MLq = (G[4], G[5]), (G[6], G[7])
        for src, ML in ((kbg, MLk), (qg, MLq)):
            b = k.bank()
            for h in range(4):
                k.tr(b[:, h * 128:h * 128 + n], src[0:n, h * 128:(h + 1) * 128], cid(n))
            for hp in range(2):
                o4 = ML[hp][:, :].rearrange("p (a s m) -> p a s m", a=2, s=16)[:, :, 0:n, 0:n]
                i4 = v3(b[:, :], 4)[:, 2 * hp:2 * hp + 2, 0:n]
                k.tt(o4, bc(i4.unsqueeze(2), [128, 2, n, n]), i16(0, 128, [128, 2, n, n], 1), ALU.mult)
        bKS, bQS = k.bank(), k.bank()
        k.memset(bKS[0:n, :], 0.0)
        k.memset(bQS[0:n, :], 0.0)
        for s_ in range(n):
            st = v3(stage[s_ % 2][:, 0:512], 4)
            k.dma(st, st_dn[l, s_].rearrange("h k v -> k h v"))
            for h in range(4):
                for ML, bk in ((MLk, bKS), (MLq, bQS)):
                    lt = ML[h // 2][:, :].rearrange("p (a s m) -> p a s m", a=2, s=16)[:, h % 2, s_, 0:n]
                    k.mm(bk[0:n, h * 128:(h + 1) * 128], lt, st[:, h, :], start=False, stop=True, skip=True)
        vnew, o_ = G[8], G[9]
        k.tt(vnew[0:n, :], vb[0:n, :], bKS[0:n, :], ALU.subtract)
        k.tt(h3(o_), h3(vnew), bcs(Aqk), ALU.mult)
        k.tt(o_[0:n, :], o_[0:n, :], bQS[0:n, :], ALU.add)
        t1 = G[10]
        k.tt(t1[0:n, 0:4 * n].rearrange("p (s h) -> p s h", h=4), bc(eG.unsqueeze(1), [n, n, 4]),
             bc(CM[0:n, C_ID, 0:n].unsqueeze(2), [n, n, 4]), ALU.mult)
        b = k.bank()
        k.mm(b[:, 0:4 * n], cmat(C_ONES, n, 128), t1[0:n, 0:4 * n])
        EGB = G[11]
        k.cp(EGB[:, 0:4 * n], b[:, 0:4 * n])
        for s_ in range(n):
            st = v3(stage[s_ % 2][:, 0:512], 4)
            k.dma(st, st_dn[l, s_].rearrange("h k v -> k h v"))
            vm = G[12 + s_ % 2]
            k.ts(vm[0:n, :], vnew[0:n, :], CM[0:n, C_ID, s_:s_ + 1], ALU.mult)
            b = k.bank()
            for h in range(4):
                k.mm(b[:, h * 128:(h + 1) * 128], kn[:, h * 128:(h + 1) * 128], vm[0:n, h * 128:(h + 1) * 128])
            so = G[14 + s_ % 2]
            k.tt(v3(so[:, :], 4), st, bc(EGB[:, s_ * 4:(s_ + 1) * 4].unsqueeze(2), [128, 4, 128]), ALU.mult)
            k.tt(so[:, :], so[:, :], b[:, :], ALU.add)
            k.dma(s_dn[l, s_].rearrange("h k v -> k h v"), v3(so[:, :], 4), q="pool")
        rms_gate_store(sbatch, 4, 128, PC[0:n, 6144:6272], proj[0:n, 1536:2048], 0, EPS, src=o_)

    def dn_chunk(l, seg, qn, kn, vv, beta, gg):
        kind, idx, n, row = seg
        b = k.bank()
        k.mm(b[0:n, 0:4], cmat(C_TRIU, n, n), gg)
        Gc = sm[0:n, 52:56]
        k.cp(Gc, b[0:n, 0:4], eng="dve")
        nG = sm[0:n, 56:60]
        k.ts(nG, Gc, -1.0, ALU.mult)
        b = k.bank()
        k.mm(b[:, 0:4], cmat(C_ONES, n, 128), gg)
        GT = sm2[:, 0:4]
        k.cp(GT, b[:, 0:4], eng="dve")
        eGl = sm2[:, 4:8]
        k.act(eGl, GT, AF.Exp)
        eG = sm[0:n, 60:64]
        k.act(eG, Gc, AF.Exp)
        edk = sm[0:n, 64:68]
        k.tt(edk, GT[0:n, :], Gc, ALU.subtract)
        k.act(edk, edk, AF.Exp)
        beG = sm[0:n, 68:72]
        k.tt(beG, beta, eG, ALU.mult)

        def h3(t):
            return v3(t[0:n, :], 4)

        def bcs(s):
            return bc(s.unsqueeze(2), [n, 4, 128])

        kbg, vb, kd, qg = G[0], G[1], G[2], G[3]
        k.tt(h3(kbg), v3(kn, 4), bcs(beG), ALU.mult)
        k.tt(h3(vb), v3(vv, 4), bcs(beta), ALU.mult)
        k.tt(h3(kd), v3(kn, 4), bcs(edk), ALU.mult)
        k.tt(h3(qg), v3(qn, 4), bcs(eG), ALU.mult)
        KT, QT, QGT = G[4], G[5], G[6]
        for src, dst in ((kn, KT), (qn, QT), (qg[0:n, :], QGT)):
            b = k.bank()
            for h in range(4):
                k.tr(b[:, h * 128:h * 128 + n], src[:, h * 128:(h + 1) * 128], cid(n))
            k.cp(v3(dst[:, :], 4)[:, :, 0:n], v3(b[:, :], 4)[:, :, 0:n])

        def f3(t):
            return v3(t[:, :], 4)[:, :, 0:n]

        def m3(t):
            return v3(t[0:n, :], 4)[:, :, 0:n]

        dg = G[7]
        k.tt(m3(dg), bc(CM[0:n, C_ID:C_ID + 1, 0:n], [n, 4, n]), bc(Gc.unsqueeze(2), [n, 4, n]), ALU.mult)
        bR = k.bank()
        for h in range(4):
            k.mm(m3(bR)[:, h, :], cmat(C_ONES, n, n), m3(dg)[:, h, :])
        t1, t2, Dm, DTm = G[8], G[9], G[10], G[11]
        k.stt(m3(t1), m3(bR), -1.0, bc(CM[0:n, C_MLOW:C_MLOW + 1, 0:n], [n, 4, n]), ALU.mult, ALU.add)
        k.tt(m3(t2), m3(bR), bc(CM[0:n, C_MUP:C_MUP + 1, 0:n], [n, 4, n]), ALU.add)
        for h in range(4):
            k.act(m3(Dm)[:, h, :], m3(t1)[:, h, :], AF.Exp, bias=Gc[:, h:h + 1])
            k.act(m3(DTm)[:, h, :], m3(t2)[:, h, :], AF.Exp, bias=nG[:, h:h + 1])
        bK = k.bank()
        bA = k.bank()
        for h in range(4):
            k.mm(m3(bK)[:, h, :], f3(KT)[:, h, :], f3(KT)[:, h, :])
            k.mm(m3(bA)[:, h, :], f3(KT)[:, h, :], f3(QT)[:, h, :])
        KKD, AT = G[12], G[13]
        k.tt(m3(KKD), m3(bK), m3(Dm), ALU.mult)
        k.tt(m3(KKD), m3(KKD), bc(beta.unsqueeze(2), [n, 4, n]), ALU.mult)
        k.tt(m3(AT), m3(bA), m3(DTm), ALU.mult)
        if n > 1:
            Nbd, NbdT, Loff = G[14], G[15], G[16]
            k.tt(m3(Nbd), m3(KKD), bc(CM[0:n, C_NTRIL64:C_NTRIL64 + 1, 0:n], [n, 4, n]), ALU.mult)
            k.tt(m3(Loff), m3(KKD), bc(CM[0:n, C_LOWLEFT:C_LOWLEFT + 1, 0:n], [n, 4, n]), ALU.mult)
            b = k.bank()
            for h in range(4):
                k.tr(m3(b)[:, h, :], m3(Nbd)[:, h, :], cid(n))
            k.cp(m3(NbdT), m3(b))
            X = neumann(n, m3(Nbd), m3(NbdT), m3(Loff), G[17:23])
        else:
            X = neumann(n, None, None, None, G[17:23])
        bW = k.bank()
        for h in range(4):
            k.mm(f3(bW)[:, h, :], h3(kbg)[:, h, :], X[:, h, :])
        nWT = G[7]
        k.ts(f3(nWT), f3(bW), -1.0, ALU.mult)
        bV = k.bank()
        for h in range(4):
            k.mm(bV[0:n, h * 128:(h + 1) * 128], X[:, h, :], h3(vb)[:, h, :], start=True, stop=False)
            k.mm(bV[0:n, h * 128:(h + 1) * 128], f3(nWT)[:, h, :], S_dn[:, h, :], start=False, stop=True)
        VN = G[8]
        k.cp(VN[0:n, :], bV[0:n, :])
        for h in range(4):
            k.mm(PSO[0:n, h * 128:(h + 1) * 128], f3(QGT)[:, h, :], S_dn[:, h, :], start=True, stop=False)
            k.mm(PSO[0:n, h * 128:(h + 1) * 128], m3(AT)[:, h, :], h3(VN)[:, h, :], start=False, stop=True)
        bS = k.bank()
        for h in range(4):
            k.mm(bS[:, h * 128:(h + 1) * 128], h3(kd)[:, h, :], h3(VN)[:, h, :])
        for h in range(4):
            k.stt(S_dn[:, h, :], S_dn[:, h, :], eGl[:, h:h + 1], bS[:, h * 128:(h + 1) * 128], ALU.mult, ALU.add)
        rms_gate_store(seg, 4, 128, PC[0:n, 6144:6272], proj[0:n, 1536:2048], 0, EPS, otile=G[22])

    def gla_phase(l):
        if k.preloaded != (l, "gla"):
            load_weights(l, GLA_C0, GLA_W, 0)
        bload(PC[:, 0:256], gla_bf[l:l + 1, :], 256)
        bload(PC[:, 256:384], gla_norm_g[l:l + 1, :], 128)
        k.dma(wsm[0:16, 0:256], gla_wf[l])
        extract_mod("prompt", 128, 0, 2 * D)
        k.memset(S_gla[:], 0.0)
        k.memset(G[12][:, :], 0.0)
        k.memset(G[15][:, :], 0.0)
        def pre(seg):
            k.dma(v3(stage[0][:, 0:256], 2), st_gla[l, seg[1]].rearrange("(hp h2) k v -> (h2 k) hp v", h2=2))
            k.cp(S_gla[:], v3(stage[0][:, 0:256], 2), eng="dve")

        def post(seg):
            kind, idx, n, row = seg
            if kind == "s":
                k.dma(s_gla[l, idx].rearrange("(hp h2) k v -> (h2 k) hp v", h2=2), S_gla[:], q="pool")
            elif idx == NT - 1:
                k.dma(p_gla[l].rearrange("(hp h2) k v -> (h2 k) hp v", h2=2), S_gla[:], q="pool")

        def _nl():
            load_weights(l, RW_C0, RW_W, 0)
            k.preloaded = (l, "rw")
        k.next_load = _nl
        mixer_pass(l, GLA_W, pre, gla_segment, post, batch_core=gla_batch)

    def gla_segment(l, seg):
        kind, idx, n, row = seg
        q_ = proj[0:n, 0:256]
        k_ = proj[0:n, 256:512]
        v_ = proj[0:n, 512:1024]
        gz = proj[0:n, 1024:1536]
        lf = gla_lf(n)
        LF = lf[0:n, 0:256]
        gla_chunk(l, seg, q_, k_, v_, gz, lf, LF)

    def gla_lf(n):
        glo = proj[0:n, 1536:1552]
        b = k.bank()
        k.tr(b[0:16, 0:n], glo, cid(n))
        gloT = G[0]
        k.cp(gloT[0:16, 0:n], b[0:16, 0:n])
        b = k.bank()
        k.mm(b[0:n, 0:256], gloT[0:16, 0:n], wsm[0:16, 0:256])
        xb, t1, t2, lf = G[1], G[2], G[3], G[4]
        k.tt(xb[0:n, 0:256], b[0:n, 0:256], PC[0:n, 0:256], ALU.add)
        softplus_parts(t1[0:n, 0:256], xb[0:n, 0:256], n, 256, t2[0:n, 0:256], t2[0:n, 256:512])
        k.ts(t2[0:n, 0:256], xb[0:n, 0:256], 0.0, ALU.min)
        k.tt(lf[0:n, 0:256], t2[0:n, 0:256], t1[0:n, 0:256], ALU.subtract)
        k.ts(lf[0:n, 0:256], lf[0:n, 0:256], 1.0 / 16.0, ALU.mult)
        return lf

    def gla_batch(l):
        n = NS
        q_ = proj[0:n, 0:256]
        k_ = proj[0:n, 256:512]
        v_ = proj[0:n, 512:1024]
        gz = proj[0:n, 1024:1536]
        lf = gla_lf(n)
        LF = lf[0:n, 0:256]
        eG, enG, qg, kg, t0 = G[5][0:n, 0:256], G[6][0:n, 0:256], G[7][0:n, 0:256], G[8][0:n, 0:256], G[9][0:n, 0:256]
        k.act(eG, LF, AF.Exp)
        k.act(enG, LF, AF.Exp, scale=-1.0)
        k.stt(qg, q_, 0.125, eG, ALU.mult, ALU.mult)
        k.tt(kg, k_, enG, ALU.mult)
        k.tt(t0, qg, kg, ALU.mult)
        Aqk = sm[0:n, 72:76]
        k.red(Aqk, v3(t0, 4))
        MLq = (G[12], G[15])
        b = k.bank()
        for hp in range(2):
            k.tr(b[:, hp * 128:hp * 128 + n], qg[:, hp * 128:(hp + 1) * 128], cid(n))
        for h2 in range(2):
            pr0, pr1 = h2 * 64, h2 * 64 + 64
            o4 = MLq[h2][pr0:pr1, :].rearrange("p (a s m) -> p a s m", a=2, s=16)[:, :, 0:n, 0:n]
            i4 = v3(b[pr0:pr1, 0:256], 2)[:, :, 0:n]
            k.tt(o4, bc(i4.unsqueeze(2), [64, 2, n, n]), i16(pr0, pr1, [64, 2, n, n], 1), ALU.mult)
        b = k.bank()
        for hp in range(2):
            k.tr(b[:, hp * 128:hp * 128 + n], lf[0:n, hp * 128:(hp + 1) * 128], cid(n))
        EGB = G[10]
        k.act(v3(EGB[:, 0:256], 2)[:, :, 0:n], v3(b[:, 0:256], 2)[:, :, 0:n], AF.Exp)
        bQS = k.bank()
        k.memset(bQS[0:n, :], 0.0)
        for s_ in range(n):
            st = v3(stage[s_ % 2][:, 0:256], 2)
            k.dma(st, st_gla[l, s_].rearrange("(hp h2) k v -> (h2 k) hp v", h2=2))
            for h in range(4):
                lt = MLq[h % 2][:, :].rearrange("p (a s m) -> p a s m", a=2, s=16)[:, h // 2, s_, 0:n]
                k.mm(bQS[0:n, h * 128:(h + 1) * 128], lt, st[:, h // 2, :], start=False, stop=True, skip=True)
        o_ = G[11]
        k.tt(v3(o_[0:n, :], 4), v3(v_, 4), bc(Aqk.unsqueeze(2), [n, 4, 128]), ALU.mult)
        k.tt(o_[0:n, :], o_[0:n, :], bQS[0:n, :], ALU.add)
        for s_ in range(n):
            st = v3(stage[s_ % 2][:, 0:256], 2)
            k.dma(st, st_gla[l, s_].rearrange("(hp h2) k v -> (h2 k) hp v", h2=2))
            vm = G[0 + s_ % 2]
            k.ts(vm[0:n, :], v_, CM[0:n, C_ID, s_:s_ + 1], ALU.mult)
            b = k.bank()
            for hp in range(2):
                k.mm(b[:, hp * 256:(hp + 1) * 256], k_[:, hp * 128:(hp + 1) * 128], vm[0:n, hp * 256:(hp + 1) * 256])
            so = G[2 + s_ % 2]
            for hp in range(2):
                for h2 in range(2):
                    pr = slice(h2 * 64, h2 * 64 + 64)
                    k.stt(v3(so[:, 0:256], 2)[pr, hp, :], st[pr, hp, :], v3(EGB[:, 0:256], 2)[pr, hp, s_:s_ + 1],
                          b[pr, hp * 256 + h2 * 128: hp * 256 + (h2 + 1) * 128], ALU.mult, ALU.add)
            k.dma(s_gla[l, s_].rearrange("(hp h2) k v -> (h2 k) hp v", h2=2), v3(so[:, 0:256], 2), q="pool")
        rms_gate_store(sbatch, 4, 128, PC[0:n, 256:384], gz, 512, EPS, src=o_)

    def gla_chunk(l, seg, q_, k_, v_, gz, lf, LF):
        kind, idx, n, row = seg
        bG = k.bank()
        k.mm(bG[0:n, 0:256], cmat(C_TRIU, n, n), LF)
        Gc = G[5][0:n, 0:256]
        k.cp(Gc, bG[0:n, 0:256])
        bT = k.bank()
        k.mm(bT[0:n, 0:256], cmat(C_ONES, n, n), LF)
        edk = G[6][0:n, 0:256]
        k.tt(edk, bT[0:n, 0:256], Gc, ALU.subtract)
        k.act(edk, edk, AF.Exp)
        bE = k.bank()
        for hp in range(2):
            k.mm(bE[:, 2 * hp:2 * hp + 2], lf[0:n, hp * 128:(hp + 1) * 128], cmat(C_ONES, n, 2))
        eGl = sm2[:, 8:10]
        k.act(eGl, v3(bE[:, 0:4], 2)[:, :, 0], AF.Exp)
        eG = G[7][0:n, 0:256]
        enG = G[8][0:n, 0:256]
        k.act(eG, Gc, AF.Exp)
        k.act(enG, Gc, AF.Exp, scale=-1.0)
        qg = G[9][0:n, 0:256]
        kg = G[10][0:n, 0:256]
        kd = G[11][0:n, 0:256]
        k.stt(qg, q_, 0.125, eG, ALU.mult, ALU.mult)
        k.tt(kg, k_, enG, ALU.mult)
        k.tt(kd, k_, edk, ALU.mult)
        QGTm, KGT = (G[12], G[15]), G[13]
        for src, dst in ((qg, None), (kg, KGT)):
            b = k.bank()
            for hp in range(2):
                k.tr(b[:, hp * 128:hp * 128 + n], src[:, hp * 128:(hp + 1) * 128], cid(n))
            if dst is None:
                for h2 in range(2):
                    pr = slice(h2 * 64, h2 * 64 + 64)
                    k.cp(v3(QGTm[h2][pr, 0:256], 2)[:, :, 0:n], v3(b[pr, 0:256], 2)[:, :, 0:n])
            else:
                k.cp(v3(dst[:, 0:256], 2)[:, :, 0:n], v3(b[:, 0:256], 2)[:, :, 0:n])

        def m3(t):
            return v3(t[0:n, :], 4)[:, :, 0:n]

        bA = k.bank()
        for h in range(4):
            hp = h // 2
            k.mm(m3(bA)[:, h, :], v3(KGT[:, 0:256], 2)[:, hp, 0:n], v3(QGTm[h % 2][:, 0:256], 2)[:, hp, 0:n])
        AT = G[14]
        k.tt(m3(AT), m3(bA), bc(CM[0:n, C_TRIU:C_TRIU + 1, 0:n], [n, 4, n]), ALU.mult)
        for h in range(4):
            hp = h // 2
            k.mm(PSO[0:n, h * 128:(h + 1) * 128], v3(QGTm[h % 2][:, 0:256], 2)[:, hp, 0:n], S_gla[:, hp, :], start=True, stop=False)
            k.mm(PSO[0:n, h * 128:(h + 1) * 128], m3(AT)[:, h, :], v_[:, h * 128:(h + 1) * 128], start=False, stop=True)
        for hp in range(2):
            bS = k.bank()
            k.mm(bS[:, 0:256], kd[:, hp * 128:(hp + 1) * 128], v_[:, hp * 256:(hp + 1) * 256])
            for h2 in range(2):
                pr = slice(h2 * 64, h2 * 64 + 64)
                k.stt(S_gla[pr, hp, :], S_gla[pr, hp, :], eGl[pr, hp:hp + 1], bS[pr, h2 * 128:(h2 + 1) * 128], ALU.mult, ALU.add)
        rms_gate_store(seg, 4, 128, PC[0:n, 256:384], gz, 512, EPS, otile=G[14])

    RWC = dict(mu=0, w0=1664, a0=2176, kk=2688, ka=3200, rk=3712, lw=4224, lb=4736)

    def rw_phase(l):
        if k.preloaded != (l, "rw"):
            load_weights(l, RW_C0, RW_W, 0)
        bload(PC[:, 0:1664], rw_mu[l:l + 1, :], 1664)
        for nm, src in (("w0", rw_w0), ("a0", rw_a0), ("kk", rw_k_k), ("ka", rw_k_a), ("rk", rw_r_k),
                        ("lw", rw_ln_w), ("lb", rw_ln_b)):
            bload(PC[:, RWC[nm]:RWC[nm] + 512], src[l:l + 1, :], 512)
        k.dma(wsm[0:64, 0:512], rw_w2[l])
        k.dma(wsm[0:64, 512:1024], rw_a2[l])
        extract_mod("prompt", 128, 0, 2 * D)
        k.memset(srow, 0.0)
        k.memset(M_rw[:], 0.0)
        MT = G[26]
        def pre(seg):
            idx = seg[1]
            k.dma(srow, st_rs[l, idx:idx + 1, :])
            k.dma(v3(MT[0:64, :], 8), st_rw[l, idx].rearrange("h v k -> v h k"))
            b = k.bank()
            for hp in range(4):
                k.tr(b[:, hp * 64:(hp + 1) * 64], MT[0:64, hp * 128:(hp + 1) * 128], cid(64))
            k.cp(M_rw[:], v3(b[:, 0:256], 4))

        def post(seg):
            kind, idx, n, row = seg
            last = (kind == "s") or idx == NT - 1
            if last:
                b = k.bank()
                for hp in range(4):
                    k.tr(b[0:64, hp * 128:(hp + 1) * 128], M_rw[:, hp, :], cid(128))
                k.cp(MT[0:64, :], b[0:64, :])
                dst = s_rw[l, idx] if kind == "s" else p_rw[l]
                k.dma(dst.rearrange("h v k -> v h k"), v3(MT[0:64, :], 8), q="pool")
                dst = s_rs[l, idx:idx + 1, :] if kind == "s" else p_rs[l:l + 1, :]
                k.dma(dst, proj[n - 1:n, 0:1664], q="pool")
            else:
                k.dma(srow, proj[n - 1:n, 0:1664])

        def _nl():
            load_weights(l, 0, D, 0, rows_kc=12, src=w_out)
            k.preloaded = (l, "out")
        k.next_load = _nl
        mixer_pass(l, RW_W, pre, rw_segment, post, batch_core=rw_batch)

    def rw_segment(l, seg):
        kind, idx, n, row = seg
        xm = cq
        for c0 in range(0, 1664, 512):
            wd = min(512, 1664 - c0)
            b = k.bank()
            if n > 1:
                k.mm(b[0:n, 0:wd], cmat(C_SH1, n, n), proj[0:n, c0:c0 + wd], start=True, stop=False)
            k.mm(b[0:n, 0:wd], cmat(C_E0, 1, n), srow[0:1, c0:c0 + wd], start=(n == 1), stop=True)
            g = G[0]
            k.tt(g[0:n, 0:wd], b[0:n, 0:wd], proj[0:n, c0:c0 + wd], ALU.subtract)
            k.tt(g[0:n, 0:wd], g[0:n, 0:wd], PC[0:n, c0:c0 + wd], ALU.mult)
            k.tt(xm[0:n, c0:c0 + wd], g[0:n, 0:wd], proj[0:n, c0:c0 + wd], ALU.add)
        rw_chunk(l, seg, *rw_prep(n))

    def rw_prep(n):
        xm = cq
        r_ = xm[0:n, 0:512]
        k_ = xm[0:n, 512:1024]
        v_ = xm[0:n, 1024:1536]
        rz = proj[0:n, 1664:2176]

        def pc(nm):
            return PC[0:n, RWC[nm]:RWC[nm] + 512]

        tw = G[0]
        k.act(tw[0:n, 0:64], xm[0:n, 1536:1600], AF.Tanh)
        b = k.bank()
        k.tr(b[0:64, 0:n], tw[0:n, 0:64], cid(n))
        k.tr(b[0:64, 128:128 + n], xm[0:n, 1600:1664], cid(n))
        loT = G[1]
        k.cp(loT[0:64, 0:256], b[0:64, 0:256])
        bw = k.bank()
        k.mm(bw[0:n, :], loT[0:64, 0:n], wsm[0:64, 0:512])
        LW = G[2]
        k.tt(LW[0:n, :], bw[0:n, :], pc("w0"), ALU.add)
        k.act(LW[0:n, :], LW[0:n, :], AF.Sigmoid)
        k.ts(LW[0:n, :], LW[0:n, :], -float(np.exp(-0.5)), ALU.mult)
        ba = k.bank()
        k.mm(ba[0:n, :], loT[0:64, 128:128 + n], wsm[0:64, 512:1024])
        At = G[3]
        k.tt(At[0:n, :], ba[0:n, :], pc("a0"), ALU.add)
        k.act(At[0:n, :], At[0:n, :], AF.Sigmoid)
        KK, K2, Bt, t0 = G[4], G[5], G[6], G[7]
        k.tt(KK[0:n, :], k_, pc("kk"), ALU.mult)
        k.tt(t0[0:n, :], KK[0:n, :], KK[0:n, :], ALU.mult)
        k.red(sm[0:n, 64:72], v3(t0[0:n, :], 8))
        k.act(sm[0:n, 72:80], sm[0:n, 64:72], AF.Sqrt, bias=EPS)
        k.recip(sm[0:n, 80:88], sm[0:n, 72:80])
        k.tt(v3(KK[0:n, :], 8), v3(KK[0:n, :], 8), bc(sm[0:n, 80:88].unsqueeze(2), [n, 8, 64]), ALU.mult)
        k.stt(t0[0:n, :], At[0:n, :], -1.0, pc("ka"), ALU.add, ALU.mult)
        k.stt(K2[0:n, :], t0[0:n, :], 1.0, k_, ALU.add, ALU.mult)
        k.tt(Bt[0:n, :], KK[0:n, :], At[0:n, :], ALU.mult)
        k.tt(t0[0:n, :], r_, K2[0:n, :], ALU.mult)
        k.tt(t0[0:n, :], t0[0:n, :], pc("rk"), ALU.mult)
        bsum = sm[0:n, 88:96]
        k.red(bsum, v3(t0[0:n, :], 8))
        return r_, k_, v_, rz, LW, At, KK, K2, Bt, bsum, pc

    def rw_chunk(l, seg, r_, k_, v_, rz, LW, At, KK, K2, Bt, bsum, pc):
        kind, idx, n, row = seg
        t0 = G[7]
        bG = k.bank()
        k.mm(bG[0:n, :], cmat(C_TRIU, n, n), LW[0:n, :])
        Gc = G[8]
        k.cp(Gc[0:n, :], bG[0:n, :])
        bT = k.bank()
        k.mm(bT[0:n, :], cmat(C_ONES, n, n), LW[0:n, :])
        edk = G[9]
        k.tt(edk[0:n, :], bT[0:n, :], Gc[0:n, :], ALU.subtract)
        k.act(edk[0:n, :], edk[0:n, :], AF.Exp)
        bE = k.bank()
        for hp in range(4):
            k.mm(bE[:, 2 * hp:2 * hp + 2], LW[0:n, hp * 128:(hp + 1) * 128], cmat(C_ONES, n, 2))
        eGl = sm2[:, 12:16]
        k.act(eGl, v3(bE[:, 0:8], 4)[:, :, 0], AF.Exp)
        eG, enG, eGx = G[10], G[11], G[7]
        k.act(eG[0:n, :], Gc[0:n, :], AF.Exp)
        k.act(enG[0:n, :], Gc[0:n, :], AF.Exp, scale=-1.0)
        k.tt(eGx[0:n, :], Gc[0:n, :], LW[0:n, :], ALU.subtract)
        k.act(eGx[0:n, :], eGx[0:n, :], AF.Exp)
        kkg, rg, bg, k2g, bd, k2d = G[12], G[13], G[14], G[15], G[16], G[17]
        k.tt(kkg[0:n, :], KK[0:n, :], eGx[0:n, :], ALU.mult)
        k.tt(rg[0:n, :], r_, eG[0:n, :], ALU.mult)
        k.tt(bg[0:n, :], Bt[0:n, :], enG[0:n, :], ALU.mult)
        k.tt(k2g[0:n, :], K2[0:n, :], enG[0:n, :], ALU.mult)
        k.tt(bd[0:n, :], Bt[0:n, :], edk[0:n, :], ALU.mult)
        k.tt(k2d[0:n, :], K2[0:n, :], edk[0:n, :], ALU.mult)
        bgT, k2gT, kkgTm, rgTm = G[0], G[1], (G[2], G[3]), (G[10], G[11])
        for src, dst in ((kkg, kkgTm), (rg, rgTm), (bg, bgT), (k2g, k2gT)):
            b = k.bank()
            for hp in range(4):
                k.tr(b[:, hp * 128:hp * 128 + n], src[0:n, hp * 128:(hp + 1) * 128], cid(n))
            if isinstance(dst, tuple):
                for h2 in range(2):
                    pr = slice(h2 * 64, h2 * 64 + 64)
                    po = slice((1 - h2) * 64, (1 - h2) * 64 + 64)
                    k.cp(v3(dst[h2][pr, :], 4)[:, :, 0:n], v3(b[pr, :], 4)[:, :, 0:n])
                    k.memset(v3(dst[h2][po, :], 4)[:, :, 0:n], 0.0)
            else:
                k.cp(v3(dst[:, :], 4)[:, :, 0:n], v3(b[:, :], 4)[:, :, 0:n])

        def fT(t, h):
            if isinstance(t, tuple):
                t = t[h % 2]
            return v3(t[:, :], 4)[:, h // 2, 0:n]

        def m3(t):
            return v3(t[0:n, :], 4)[:, :, 0:n]

        def mk(i):
            return bc(CM[0:n, i:i + 1, 0:n], [n, 4, n])

        for hb_ in range(2):
            heads = [4 * hb_ + i for i in range(4)]
            bN, bNT, bAk, bRb, bRk = k.bank(), k.bank(), k.bank(), k.bank(), k.bank()
            for i, h in enumerate(heads):
                k.mm(m3(bN)[:, i, :], fT(kkgTm, h), fT(bgT, h))
                k.mm(m3(bNT)[:, i, :], fT(bgT, h), fT(kkgTm, h))
                k.mm(m3(bAk)[:, i, :], fT(k2gT, h), fT(kkgTm, h))
                k.mm(m3(bRb)[:, i, :], fT(bgT, h), fT(rgTm, h))
                k.mm(m3(bRk)[:, i, :], fT(k2gT, h), fT(rgTm, h))
            AkT, RbT, RkT = G[4], G[5], G[6]
            k.tt(m3(AkT), m3(bAk), mk(C_TRIUS), ALU.mult)
            k.tt(m3(RbT), m3(bRb), mk(C_TRIU), ALU.mult)
            k.tt(m3(RkT), m3(bRk), mk(C_TRIU), ALU.mult)
            if n > 1:
                Nbd, NbdT, Loff = G[7], G[8], G[9]
                k.tt(m3(Nbd), m3(bN), mk(C_NTRIL64), ALU.mult)
                k.tt(m3(Loff), m3(bN), mk(C_LOWLEFT), ALU.mult)
                k.tt(m3(NbdT), m3(bNT), mk(C_NTRIU64), ALU.mult)
                X = neumann(n, m3(Nbd), m3(NbdT), m3(Loff), G[18:24])
            else:
                X = neumann(n, None, None, None, G[18:24])
            bR = k.bank()
            for i, h in enumerate(heads):
                k.mm(bR[0:n, i * 64:(i + 1) * 64], fT(kkgTm, h), M_rw[:, h // 2, :], start=True, stop=False)
                k.mm(bR[0:n, i * 64:(i + 1) * 64], m3(AkT)[:, i, :], v_[:, h * 64:(h + 1) * 64], start=False, stop=True)
            R0 = G[24]
            k.ts(R0[0:n, 0:256], bR[0:n, 0:256], -1.0, ALU.mult)
            bU = k.bank()
            for i, h in enumerate(heads):
                k.mm(bU[0:n, i * 64:(i + 1) * 64], X[:, i, :], R0[0:n, i * 64:(i + 1) * 64])
            U = G[25]
            k.cp(U[0:n, 0:256], bU[0:n, 0:256])
            for i, h in enumerate(heads):
                o_ = PSO[0:n, h * 64:(h + 1) * 64]
                k.mm(o_, fT(rgTm, h), M_rw[:, h // 2, :], start=True, stop=False)
                k.mm(o_, m3(RbT)[:, i, :], U[0:n, i * 64:(i + 1) * 64], start=False, stop=False)
                k.mm(o_, m3(RkT)[:, i, :], v_[:, h * 64:(h + 1) * 64], start=False, stop=True)
            for j in range(2):
                hp = 2 * hb_ + j
                bM = k.bank()
                k.mm(bM[:, 0:128], bd[0:n, hp * 128:(hp + 1) * 128], U[0:n, j * 128:(j + 1) * 128], start=True, stop=False)
                k.mm(bM[:, 0:128], k2d[0:n, hp * 128:(hp + 1) * 128], v_[:, hp * 128:(hp + 1) * 128], start=False, stop=True)
                for h2 in range(2):
                    pr = slice(h2 * 64, h2 * 64 + 64)
                    k.stt(M_rw[pr, hp, :], M_rw[pr, hp, :], eGl[pr, hp:hp + 1], bM[pr, h2 * 64:(h2 + 1) * 64], ALU.mult, ALU.add)
        rw_out(n, row, None, v_, rz, bsum, pc)

    def rw_batch(l):
        n = NS
        xm = cq
        for j, c0 in enumerate(range(0, 1664, 512)):
            wd = min(512, 1664 - c0)
            st = stage[j % 2][0:n, (j // 2) * 512:(j // 2) * 512 + wd]
            k.dma(st, st_rs[l, 0:n, c0:c0 + wd])
            g = G[0]
            k.tt(g[0:n, 0:wd], st, proj[0:n, c0:c0 + wd], ALU.subtract)
            k.tt(g[0:n, 0:wd], g[0:n, 0:wd], PC[0:n, c0:c0 + wd], ALU.mult)
            k.tt(xm[0:n, c0:c0 + wd], g[0:n, 0:wd], proj[0:n, c0:c0 + wd], ALU.add)
        k.dma(s_rs[l, 0:n, :], proj[0:n, 0:1664], q="pool")
        r_, k_, v_, rz, LW, At, KK, K2, Bt, bsum, pc = rw_prep(n)
        t0, eG, rg = G[7], G[8], G[9]
        k.act(eG[0:n, :], LW[0:n, :], AF.Exp)
        k.tt(rg[0:n, :], r_, eG[0:n, :], ALU.mult)
        rb, rk2 = sm2[0:n, 16:24], sm2[0:n, 24:32]
        k.tt(t0[0:n, :], r_, Bt[0:n, :], ALU.mult)
        k.red(rb, v3(t0[0:n, :], 8))
        k.tt(t0[0:n, :], r_, K2[0:n, :], ALU.mult)
        k.red(rk2, v3(t0[0:n, :], 8))
        MLkk = ((G[10], G[11]), (G[12], G[13]))
        MLrg = ((G[14], G[15]), (G[16], G[17]))

        def mlv(t):
            return t[:, :].rearrange("p (a s m) -> p a s m", a=2, s=16)

        for src, ML in ((KK, MLkk), (rg, MLrg)):
            b = k.bank()
            for hp in range(4):
                k.tr(b[:, hp * 128:hp * 128 + n], src[0:n, hp * 128:(hp + 1) * 128], cid(n))
            for h2 in range(2):
                p0, p1 = h2 * 64, h2 * 64 + 64
                q0, q1 = (1 - h2) * 64, (1 - h2) * 64 + 64
                for hq in range(2):
                    o4 = mlv(ML[h2][hq])[p0:p1, :, 0:n, 0:n]
                    i4 = v3(b[p0:p1, :], 4)[:, 2 * hq:2 * hq + 2, 0:n]
                    k.tt(o4, bc(i4.unsqueeze(2), [64, 2, n, n]), i16(p0, p1, [64, 2, n, n], 1), ALU.mult)
                    k.memset(ML[h2][hq][q0:q1, :], 0.0)
        b = k.bank()
        for hp in range(4):
            k.tr(b[:, hp * 128:hp * 128 + n], LW[0:n, hp * 128:(hp + 1) * 128], cid(n))
        WT = G[18]
        k.act(v3(WT[:, :], 4)[:, :, 0:n], v3(b[:, :], 4)[:, :, 0:n], AF.Exp)
        MT = G[26]
        Mst = (G[19], G[20])

        def load_state(s_):
            k.dma(v3(MT[0:64, :], 8), st_rw[l, s_].rearrange("h v k -> v h k"))
            b_ = k.bank()
            for hp in range(4):
                k.tr(b_[:, hp * 64:(hp + 1) * 64], MT[0:64, hp * 128:(hp + 1) * 128], cid(64))
            Ms = v3(Mst[s_ % 2][:, 0:256], 4)
            k.cp(Ms, v3(b_[:, 0:256], 4))
            return Ms

        k.nrot = 6
        bKM, bRM = k.banks[6], k.banks[7]
        k.memset(bKM[0:n, :], 0.0)
        k.memset(bRM[0:n, :], 0.0)
        for s_ in range(n):
            Ms = load_state(s_)
            for h in range(8):
                hp, h2 = h // 2, h % 2
                for ML, bk in ((MLkk, bKM), (MLrg, bRM)):
                    lt = mlv(ML[h2][hp // 2])[:, hp % 2, s_, 0:n]
                    k.mm(bk[0:n, h * 64:(h + 1) * 64], lt, Ms[:, hp, :], start=False, stop=True, skip=True)
        U, o_, t1 = G[21], G[22], G[23]
        k.ts(U[0:n, :], bKM[0:n, :], -1.0, ALU.mult)
        k.tt(v3(o_[0:n, :], 8), v3(U[0:n, :], 8), bc(rb.unsqueeze(2), [n, 8, 64]), ALU.mult)
        k.tt(v3(t1[0:n, :], 8), v3(v_, 8), bc(rk2.unsqueeze(2), [n, 8, 64]), ALU.mult)
        k.tt(o_[0:n, :], o_[0:n, :], t1[0:n, :], ALU.add)
        k.tt(o_[0:n, :], o_[0:n, :], bRM[0:n, :], ALU.add)
        k.nrot = 7
        um, vm, Mo = G[23], G[24], G[25]
        for s_ in range(n):
            Ms = load_state(s_)
            k.ts(um[0:n, :], U[0:n, :], CM[0:n, C_ID, s_:s_ + 1], ALU.mult)
            k.ts(vm[0:n, :], v_, CM[0:n, C_ID, s_:s_ + 1], ALU.mult)
            b = k.bank()
            for hp in range(4):
                cs = slice(hp * 128, (hp + 1) * 128)
                k.mm(b[:, cs], Bt[0:n, cs], um[0:n, cs], start=True, stop=False)
                k.mm(b[:, cs], K2[0:n, cs], vm[0:n, cs], start=False, stop=True)
            Mo3 = v3(Mo[:, 0:256], 4)
            for hp in range(4):
                for h2 in range(2):
                    pr = slice(h2 * 64, h2 * 64 + 64)
                    k.stt(Mo3[pr, hp, :], Ms[pr, hp, :], v3(WT[:, :], 4)[pr, hp, s_:s_ + 1],
                          b[pr, hp * 128 + h2 * 64: hp * 128 + (h2 + 1) * 64], ALU.mult, ALU.add)
            b2 = k.bank()
            for hp in range(4):
                k.tr(b2[0:64, hp * 128:(hp + 1) * 128], Mo3[:, hp, :], cid(128))
            so = stage[s_ % 2][0:64, 0:512]
            k.cp(so, b2[0:64, :])
            k.dma(s_rw[l, s_].rearrange("h v k -> v h k"), v3(so, 8), q="pool")
        rw_out(n, SEQ, o_, v_, rz, bsum, pc)

    def rw_out(n, row, src, v_, rz, bsum, pc):
        run_pre_tail()
        on, xc, sq, zs = G[0], G[25], G[2], G[3]
        if src is None:
            k.cp(on[0:n, :], PSO[0:n, :])
        else:
            on = src
        k.red(sm[0:n, 96:104], v3(on[0:n, :], 8))
        k.ts(sm[0:n, 96:104], sm[0:n, 96:104], 1.0 / 64.0, ALU.mult)
        k.tt(v3(xc[0:n, :], 8), v3(on[0:n, :], 8), bc(sm[0:n, 96:104].unsqueeze(2), [n, 8, 64]), ALU.subtract)
        k.tt(sq[0:n, :], xc[0:n, :], xc[0:n, :], ALU.mult)
        k.red(sm[0:n, 104:112], v3(sq[0:n, :], 8))
        k.act(sm[0:n, 112:120], sm[0:n, 104:112], AF.Sqrt, bias=64e-5, scale=1.0 / 64.0)
        k.recip(sm[0:n, 120:128], sm[0:n, 112:120])
        k.tt(v3(xc[0:n, :], 8), v3(xc[0:n, :], 8), bc(sm[0:n, 120:128].unsqueeze(2), [n, 8, 64]), ALU.mult)
        k.tt(xc[0:n, :], xc[0:n, :], pc("lw"), ALU.mult)
        k.tt(xc[0:n, :], xc[0:n, :], pc("lb"), ALU.add)
        k.tt(v3(sq[0:n, :], 8), v3(v_, 8), bc(bsum.unsqueeze(2), [n, 8, 64]), ALU.mult)
        k.tt(xc[0:n, :], xc[0:n, :], sq[0:n, :], ALU.add)
        k.act(zs[0:n, :], rz, AF.Silu)
        k.tt(xc[0:n, :], xc[0:n, :], zs[0:n, :], ALU.mult)
        k.dma(omix[row:row + n, 1024:1536], xc[0:n, :], q="pool")
        run_post_tail()

    def out_phase(l):
        if k.preloaded != (l, "out"):
            load_weights(l, 0, D, 0, rows_kc=12, src=w_out)
        last_layer = (l == LAYERS - 1)
        if last_layer:
            bload(PC[:, 0:D], final_norm_g[0:1, :], D)
        extract_mod("prompt", 128, 2 * D, D)
        for si, seg in enumerate(psegs + [sbatch]):
            kind, idx, n, row = seg
            xt = XB[si % 2]
            seg_mod(seg, 2 * D, D)
            k.dma(xt[0:n, :], xsrc(l, seg))
            k.dma(stage[0][0:n, 0:1024], omix[row:row + n, 0:1024])
            k.dma(stage[1][0:n, 0:512], omix[row:row + n, 1024:1536])
            k.cp(ob[0:n, 0:1024], stage[0][0:n, 0:1024], eng="act")
            k.cp(ob[0:n, 1024:1536], stage[1][0:n, 0:512], eng="dve")
            for grp, (k0, k1) in enumerate(((0, 8), (8, 12))):
                b = k.bank()
                bb = b[:, :].bitcast(BF16)
                for kc in range(k0, k1):
                    k.tr(bb[:, (kc - k0) * 128:(kc - k0) * 128 + n], ob[0:n, kc * 128:(kc + 1) * 128], identb[0:n, 0:n])
                k.cp(oT[:, k0:k1, 0:n], v3(bb, 8)[:, 0:k1 - k0, 0:n])
            for c in range(2):
                b = k.bank()
                for kc in range(12):
                    k.mm(b[0:n, :], oT[:, kc, 0:n], W[:, kc * D + c * 512: kc * D + (c + 1) * 512],
                         start=(kc == 0), stop=(kc == 11), rkey=f"W{kc}")
                g = G[c]
                k.tt(g[0:n, :], b[0:n, :], modseg[0:n, c * 512:(c + 1) * 512], ALU.mult)
                k.tt(xt[0:n, c * 512:(c + 1) * 512], xt[0:n, c * 512:(c + 1) * 512], g[0:n, :], ALU.add)
            if not last_layer:
                k.dma(xmid[row:row + n, :], xt[0:n, :], q="pool")
            else:
                yt = (cq, PB[0])[si % 2][0:n, 0:D]
                k.memset(sm[0:n, 0:1], 0.0)
                k.e("act", "activation", out=yt, in_=xt[0:n, :], func=AF.Square, accum_out=sm[0:n, 0:1])
                k.act(sm[0:n, 1:2], sm[0:n, 0:1], AF.Sqrt, bias=EPS, scale=1.0 / D)
                k.recip(sm[0:n, 2:3], sm[0:n, 1:2])
                k.stt(yt, xt[0:n, :], sm[0:n, 2:3], PC[0:n, 0:D], ALU.mult, ALU.mult)
                dst = y_p[idx * 128:(idx + 1) * 128, :] if kind == "p" else y_s[0:NS, :]
                k.dma(dst, yt, q="pool")

    for l in range(LAYERS):
        layer_mod(l)
        if "dn" in PHASES:
            dn_phase(l)
        if "gla" in PHASES:
            gla_phase(l)
        if "rw" in PHASES:
            rw_phase(l)
        if "out" in PHASES:
            out_phase(l)
    k.S.finish()
    return nc, k


_WNAMES = ["norm_g", "ada_w", "ada_b", "w_in", "dn_conv_w", "dn_a_log", "dn_dt_bias", "dn_norm_g", "gla_wf",
           "gla_bf", "gla_norm_g", "rw_mu", "rw_w0", "rw_w2", "rw_a0", "rw_a2", "rw_k_k", "rw_k_a", "rw_r_k",
           "rw_ln_w", "rw_ln_b", "w_out", "final_norm_g"]


def make_in_maps(inputs, SEQ, NS=NSAMP):
    f = lambda a: np.ascontiguousarray(np.asarray(a, dtype=np.float32))
    shared = {nm: f(inputs[nm]) for nm in _WNAMES}
    shared["rw_r_k"] = shared["rw_r_k"].reshape(2, 512)
    shared["final_norm_g"] = shared["final_norm_g"].reshape(1, D)
    shared["cm"] = make_consts(NS)
    maps = []
    for c in range(NCORES):
        sl = slice(c * NS, (c + 1) * NS)
        m = dict(shared)
        m["xp"] = f(inputs["x_prompt"][c, :SEQ])
        m["xs"] = f(inputs["x_sample"][sl, 0])
        m["call"] = f(np.concatenate([inputs["c_sample"][sl], inputs["c_prompt"][c:c + 1]], axis=0))
        m["st_conv"] = f(inputs["state_dn_conv"][:, sl])
        m["st_dn"] = f(inputs["state_dn"][:, sl])
        m["st_gla"] = f(inputs["state_gla"][:, sl])
        m["st_rs"] = f(inputs["state_rwkv_shift"][:, sl])
        m["st_rw"] = f(inputs["state_rwkv"][:, sl])
        maps.append(m)
    return maps


def gather(res, SEQ):
    R = res.results
    cat = lambda nm, ax: np.concatenate([np.asarray(r[nm]) for r in R], axis=ax)
    stk = lambda nm: np.stack([np.asarray(r[nm]) for r in R], axis=1)
    y_p = np.stack([np.asarray(r["y_p"]) for r in R], axis=0)
    y_s = cat("y_s", 0)[:, None, :]
    return (y_p.astype(np.float32), y_s.astype(np.float32), stk("p_conv"), stk("p_dn"), stk("p_gla"), stk("p_rs"),
            stk("p_rw"), cat("s_conv", 1), cat("s_dn", 1), cat("s_gla", 1), cat("s_rs", 1), cat("s_rw", 1))


_CACHE = {}


def kernel(**inputs):
    SEQ = int(np.asarray(inputs["x_prompt"]).shape[1])
    if SEQ not in _CACHE:
        _CACHE[SEQ] = build(SEQ=SEQ)[0]
    nc = _CACHE[SEQ]
    maps = make_in_maps(inputs, SEQ)
    res = run_bass_kernel_spmd(nc, maps, core_ids=list(range(NCORES)))
    return gather(res, SEQ)
```

```python
import numpy as np
import concourse.bass as bass
import concourse.mybir as mybir
from concourse.bass_utils import run_bass_kernel_spmd

F32 = mybir.dt.float32
BF16 = mybir.dt.bfloat16
AF = mybir.ActivationFunctionType
ALU = mybir.AluOpType
AX = mybir.AxisListType

D = 1024
NSAMP = 16
NCORES = 8
EPS = 1e-6
ENGS = ("pe", "dve", "act", "pool", "sp")
N_DMA_SEMS = 32
SEM_LIMIT = 30000
USE_R = True
BATCH_S = True
SAME_ENGINE_SYNC = True

(C_ID, C_ONES, C_TRIU, C_SH1, C_SH2, C_SH3, C_BD1, C_BD2, C_BD3, C_SELB128, C_A1, C_SELB1, C_E0, C_SELP,
 C_TRIUS, C_NTRIL64, C_NTRIU64, C_LOWLEFT, C_MLOW, C_MUP) = range(20)
C_I16 = 20
NCM = 22
NCMR = 14
F32R = mybir.dt.float32r


def make_consts(NS=NSAMP):
    cm = np.zeros((128, NCM, 128), np.float32)
    i = np.arange(128)[:, None]
    j = np.arange(128)[None, :]
    same = (i // 64) == (j // 64)
    cm[:, C_ID] = (i == j)
    cm[:, C_ONES] = 1.0
    cm[:, C_TRIU] = (i <= j)
    cm[:, C_TRIUS] = (i < j)
    cm[:, C_NTRIL64] = -1.0 * ((i > j) & same)
    cm[:, C_NTRIU64] = -1.0 * ((i < j) & same)
    cm[:, C_LOWLEFT] = ((i >= 64) & (j < 64))
    cm[:, C_MLOW] = np.where(i >= j, 0.0, -1e30)
    cm[:, C_MUP] = np.where(j >= i, 0.0, -1e30)
    cm[:, C_SH1] = (i == j - 1)
    cm[:, C_SH2] = (i == j - 2)
    cm[:, C_SH3] = (i == j - 3)
    for d, c in ((1, C_BD1), (2, C_BD2), (3, C_BD3)):
        cm[:, c] = ((i == 3 + j - d) & (i < 3) & (j < d))
    cm[:, C_SELB128] = ((i == 125 + j) & (j < 3))
    cm[:, C_A1] = ((i == j + 1) & (i < 3) & (j < 2))
    cm[0, C_SELB1, 2] = 1.0
    cm[0, C_E0, 0] = 1.0
    cm[NS, C_SELP, :] = 1.0
    cm[:, C_I16:C_I16 + 2, :] = np.eye(16, dtype=np.float32).reshape(1, 2, 128)
    return cm


class Buf:
    __slots__ = ("last_w", "reads", "excl")

    def __init__(self, excl):
        self.last_w = None
        self.reads = []
        self.excl = excl


class Sched:
    def __init__(self, nc):
        self.nc = nc
        self.streams = {e: [] for e in ENGS}
        self.gen = {e: 0 for e in ENGS}
        self.sems = {}
        for e in ENGS:
            self.sems[(e, 0)] = nc.alloc_semaphore(name=f"s_{e}0")
        self.cnt = {e: 0 for e in ENGS}
        self.dsems = [nc.alloc_semaphore(name=f"s_dma{i}") for i in range(N_DMA_SEMS)]
        self.dcnt = [0] * N_DMA_SEMS
        self.dnext = 0
        self.dnext2 = 0
        self.dnext3 = 0
        self.seen = {e: {} for e in ENGS}
        self.bufs = {}
        self.n_instr = 0
        self.n_wait = 0

    def buf(self, key):
        b = self.bufs.get(key)
        if b is None:
            b = self.bufs[key] = Buf(key.startswith("ps"))
        return b

    def _sem(self, key):
        return self.dsems[key] if isinstance(key, int) else self.sems[key]

    def _deps(self, eng, rb, wb):
        need = {}

        def add(ev):
            if ev is None:
                return
            k, v = ev
            if not isinstance(k, int) and k[0] == eng and (eng == "pe" or not SAME_ENGINE_SYNC):
                return
            if need.get(k, 0) < v:
                need[k] = v

        for b in rb:
            add(b.last_w)
            if b.excl:
                for r in b.reads:
                    add(r)
        for b in wb:
            add(b.last_w)
            for r in b.reads:
                add(r)
        out = []
        seen = self.seen[eng]
        for k, v in need.items():
            if seen.get(k, 0) < v:
                seen[k] = v
                out.append((self._sem(k), v))
        return out

    def _commit(self, ev, rb, wb):
        for b in rb:
            if b.excl:
                b.last_w = ev
                b.reads = []
            else:
                b.reads.append(ev)
                if len(b.reads) > 48:
                    best = {}
                    for k, v in b.reads:
                        if best.get(k, 0) < v:
                            best[k] = v
                    b.reads = list(best.items())
        for b in wb:
            b.last_w = ev
            b.reads = []

    def op(self, eng, fn, reads, writes, desc=None):
        if not hasattr(self, "meta"):
            self.meta = {e: [] for e in ENGS}
        rb = [self.buf(k) for k in reads]
        wb = [self.buf(k) for k in writes]
        wl = self._deps(eng, rb, wb)
        if self.cnt[eng] >= SEM_LIMIT:
            self.gen[eng] += 1
            self.cnt[eng] = 0
            self.sems[(eng, self.gen[eng])] = self.nc.alloc_semaphore(name=f"s_{eng}{self.gen[eng]}")
        self.cnt[eng] += 1
        key = (eng, self.gen[eng])
        ev = (key, self.cnt[eng])
        sem = self.sems[key]
        self.n_wait += len(wl)
        self.n_instr += 1
        self.meta[eng].append((len(wl), desc))

        def emit(e, fn=fn, wl=wl, sem=sem):
            for s, v in wl:
                e.wait_ge(s, v)
            fn(e).then_inc(sem, 1)

        self.streams[eng].append(emit)
        self._commit(ev, rb, wb)

    def dma(self, q, out, in_, reads, writes):
        rb = [self.buf(k) for k in reads]
        wb = [self.buf(k) for k in writes]
        wl = self._deps(q, rb, wb)
        if q == "sp":
            j = self.dnext
            self.dnext = (self.dnext + 1) % 16
        elif q == "pool":
            j = 16 + self.dnext2
            self.dnext2 = (self.dnext2 + 1) % 8
        else:
            j = 24 + self.dnext3
            self.dnext3 = (self.dnext3 + 1) % 8
        prev = self.dcnt[j]
        if prev > 0 and self.seen[q].get(j, 0) < prev:
            self.seen[q][j] = prev
            wl.append((self.dsems[j], prev))
        self.dcnt[j] += 16
        ev = (j, self.dcnt[j])
        dsem = self.dsems[j]
        self.n_wait += len(wl)
        self.n_instr += 1

        def emit(e, wl=wl, dsem=dsem, out=out, in_=in_):
            for s, v in wl:
                e.wait_ge(s, v)
            e.dma_start(out=out, in_=in_).then_inc(dsem, 16)

        self.streams[q].append(emit)
        self._commit(ev, rb, wb)

    def finish(self):
        wl = [(self.dsems[j], self.dcnt[j]) for j in range(N_DMA_SEMS) if self.dcnt[j] > 0]
        for e in ENGS:
            if e not in ("sp",) and self.cnt[e] > 0:
                wl.append((self.sems[(e, self.gen[e])], self.cnt[e]))

        def emit(e, wl=wl):
            for s, v in wl:
                e.wait_ge(s, v)

        self.streams["sp"].append(emit)
        streams = self.streams
        with self.nc.Block() as block:
            @block.tensor
            def _(e):
                for f in streams["pe"]:
                    f(e)

            @block.vector
            def _(e):
                for f in streams["dve"]:
                    f(e)

            @block.scalar
            def _(e):
                for f in streams["act"]:
                    f(e)

            @block.gpsimd
            def _(e):
                for f in streams["pool"]:
                    f(e)

            @block.sync
            def _(e):
                for f in streams["sp"]:
                    f(e)


def _isap(v):
    return hasattr(v, "tensor") and hasattr(v, "ap")


class K:
    def __init__(self, nc):
        self.nc = nc
        self.S = Sched(nc)
        self.banks = [nc.alloc_psum_tensor(f"ps{i}", [128, 512], F32) for i in range(8)]
        self.nb = 0
        self.nrot = 7
        self.ncp = 0
        self.R = set()

    def bank(self):
        b = self.banks[self.nb % self.nrot]
        self.nb += 1
        return b

    def e(self, eng, method, _keys=None, **kw):
        reads, writes = [], []
        for name, val in list(kw.items()):
            if _isap(val):
                if _keys and name in _keys:
                    kk = _keys[name]
                    (writes if name in ("out", "accum_out", "ap") else reads).extend(kk if isinstance(kk, list) else [kk])
                elif name in ("out", "accum_out", "ap"):
                    writes.append(val.name)
                    if val.name in self.R and val.dtype == F32 and name != "ap":
                        kw[name] = val.bitcast(F32R)
                else:
                    reads.append(val.name)
        desc = method + " " + " ".join(f"{n_}={tuple(v_.shape)}:{v_.dtype}@{v_.name}" for n_, v_ in kw.items() if _isap(v_))
        self.S.op(eng, lambda e: getattr(e, method)(**kw), reads, writes, desc)

    def dma(self, out, in_, q="sp", wkey=None):
        self.S.dma(q, out, in_, [in_.name], [wkey if wkey is not None else out.name])

    def mm(self, out, lhsT, rhs, start=True, stop=True, skip=False, rkey=None):
        if rkey is not None:
            self.e("pe", "matmul", _keys={"rhs": rkey}, out=out, lhsT=lhsT, rhs=rhs, start=start, stop=stop)
            return
        if (lhsT.name in self.R and rhs.name in self.R and lhsT.dtype == F32 and rhs.dtype == F32
                and lhsT.shape[0] >= 2 and lhsT.shape[-1] >= 2 and rhs.shape[-1] >= 2):
            lhsT = lhsT.bitcast(F32R)
            rhs = rhs.bitcast(F32R)
        if skip:
            self.e("pe", "matmul", out=out, lhsT=lhsT, rhs=rhs, start=start, stop=stop, skip_group_check=True)
        else:
            self.e("pe", "matmul", out=out, lhsT=lhsT, rhs=rhs, start=start, stop=stop)

    def tr(self, out, in_, ident):
        self.e("pe", "transpose", out=out, in_=in_, identity=ident)

    def tt(self, out, a, b, op, eng="dve"):
        self.e(eng, "tensor_tensor", out=out, in0=a, in1=b, op=op)

    def ts(self, out, a, s1, op0, s2=None, op1=None, eng="dve"):
        if op1 is None:
            self.e(eng, "tensor_scalar", out=out, in0=a, scalar1=s1, scalar2=None, op0=op0)
        else:
            self.e(eng, "tensor_scalar", out=out, in0=a, scalar1=s1, scalar2=s2, op0=op0, op1=op1)

    def stt(self, out, a, s, b, op0, op1, eng="dve"):
        self.e(eng, "scalar_tensor_tensor", out=out, in0=a, scalar=s, in1=b, op0=op0, op1=op1)

    def red(self, out, in_, op=ALU.add):
        self.e("dve", "tensor_reduce", out=out, in_=in_, axis=AX.X, op=op)

    def recip(self, out, in_):
        self.e("dve", "reciprocal", out=out, in_=in_)

    def act(self, out, in_, func, bias=None, scale=None):
        kw = dict(out=out, in_=in_, func=func)
        if bias is not None:
            kw["bias"] = bias
        if scale is not None:
            kw["scale"] = scale
        self.e("act", "activation", **kw)

    def cp(self, out, in_, eng=None, okey=None, ikey=None):
        if eng is None:
            eng = ("dve", "act")[self.ncp % 2]
            self.ncp += 1
        kk = {"out": okey} if okey is not None else None
        if ikey is not None:
            kk = dict(kk or {}, in_=ikey)
        if eng == "act":
            self.e("act", "copy", _keys=kk, out=out, in_=in_)
        else:
            self.e(eng, "tensor_copy", _keys=kk, out=out, in_=in_)

    def memset(self, ap, val, eng="dve"):
        self.e(eng, "memset", ap=ap, constant=val)


class Ref:
    def __init__(self, t):
        self.t = t

    def __getitem__(self, key):
        return self.t[key]


def v3(ap, a):
    return ap.rearrange("p (a b) -> p a b", a=a)


def bc(ap, shape):
    return ap.to_broadcast(list(shape))


DN_C0, DN_W = 0, 2056
GLA_C0, GLA_W = 2056, 1552
RW_C0, RW_W = 3608, 2176
NG = 27


def build(SEQ=2048, NS=NSAMP, LAYERS=2, PHASES=("dn", "gla", "rw", "out")):
    nc = bass.Bass("TRN2", target_bir_lowering=False)
    NT = SEQ // 128
    NTOK = SEQ + NS

    def din(name, shape):
        return nc.dram_tensor(name, list(shape), F32, kind="ExternalInput").ap()

    def dout(name, shape):
        return nc.dram_tensor(name, list(shape), F32, kind="ExternalOutput").ap()

    xp = din("xp", [SEQ, D]); xs = din("xs", [NS, D]); call = din("call", [NS + 1, D])
    st_conv = din("st_conv", [2, NS, 3, 1536]); st_dn = din("st_dn", [2, NS, 4, 128, 128])
    st_gla = din("st_gla", [2, NS, 4, 64, 128]); st_rs = din("st_rs", [2, NS, 1664])
    st_rw = din("st_rw", [2, NS, 8, 64, 64])
    cmd = din("cm", [128, NCM, 128])
    norm_g = din("norm_g", [2, D]); ada_w = din("ada_w", [2, D, 3 * D]); ada_b = din("ada_b", [2, 3 * D])
    w_in = din("w_in", [2, D, 5784]); dn_conv_w = din("dn_conv_w", [2, 4, 1536])
    dn_a_log = din("dn_a_log", [2, 4]); dn_dt_bias = din("dn_dt_bias", [2, 4]); dn_norm_g = din("dn_norm_g", [2, 128])
    gla_wf = din("gla_wf", [2, 16, 256]); gla_bf = din("gla_bf", [2, 256]); gla_norm_g = din("gla_norm_g", [2, 128])
    rw_mu = din("rw_mu", [2, 1664]); rw_w0 = din("rw_w0", [2, 512]); rw_w2 = din("rw_w2", [2, 64, 512])
    rw_a0 = din("rw_a0", [2, 512]); rw_a2 = din("rw_a2", [2, 64, 512]); rw_k_k = din("rw_k_k", [2, 512])
    rw_k_a = din("rw_k_a", [2, 512]); rw_r_k = din("rw_r_k", [2, 512]); rw_ln_w = din("rw_ln_w", [2, 512])
    rw_ln_b = din("rw_ln_b", [2, 512]); w_out = din("w_out", [2, 1536, D]); final_norm_g = din("final_norm_g", [1, D])

    y_p = dout("y_p", [SEQ, D]); y_s = dout("y_s", [NS, D])
    p_conv = dout("p_conv", [2, 3, 1536]); p_dn = dout("p_dn", [2, 4, 128, 128]); p_gla = dout("p_gla", [2, 4, 64, 128])
    p_rs = dout("p_rs", [2, 1664]); p_rw = dout("p_rw", [2, 8, 64, 64])
    s_conv = dout("s_conv", [2, NS, 3, 1536]); s_dn = dout("s_dn", [2, NS, 4, 128, 128])
    s_gla = dout("s_gla", [2, NS, 4, 64, 128]); s_rs = dout("s_rs", [2, NS, 1664]); s_rw = dout("s_rw", [2, NS, 8, 64, 64])

    xmid = nc.dram_tensor("xmid", [NTOK, D], F32).ap()
    omix = nc.dram_tensor("omix", [NTOK, 1536], F32).ap()
    projs = nc.dram_tensor("projs", [NS, RW_W], F32).ap()
    modall = nc.dram_tensor("modall", [NS + 1, 3 * D], F32).ap()
    hts = nc.dram_tensor("hts", [NT + 1, 128, 8 * 128], BF16).ap()

    k = K(nc)
    sb = nc.alloc_sbuf_tensor

    CM = sb("CM", [128, NCM, 128], F32)
    identb = sb("identb", [128, 128], BF16)
    W = sb("W", [128, 8 * RW_W], BF16)
    stage = [sb(f"stage{i}", [128, 1088], F32) for i in range(2)]
    modseg = sb("modseg", [128, 2 * D], F32)
    PC = sb("PC", [128, 6912], F32)
    x_t = sb("x_t", [128, D], F32)
    hb = sb("hb", [128, D], BF16)
    HB = [sb("hT", [128, 8, 128], BF16), sb("hT1", [128, 8, 128], BF16)]
    PB = [sb("proj", [128, RW_W], F32), sb("proj1", [128, RW_W], F32)]
    hT = Ref(HB[0])
    proj = Ref(PB[0])
    k.pre_tail = None
    k.post_tail = None
    k.next_load = None
    k.preloaded = None
    cq = sb("cq", [128, 1664], F32)
    cT = sb("cT", [128, 8, NS + 1], F32)
    TLS = sb("TLS", [3, 1664], F32)
    TL = TLS[0:3, 0:1536]
    srow = TLS[0:1, 0:1664]
    S_dn = sb("S_dn", [128, 4, 128], F32)
    S_gla = sb("S_gla", [128, 2, 128], F32)
    M_rw = sb("M_rw", [128, 4, 64], F32)
    sm = sb("sm", [128, 128], F32)
    sm2 = sb("sm2", [128, 64], F32)
    wsm = sb("wsm", [64, 1024], F32)
    ob = modseg[:, 1024:1792].bitcast(BF16)
    XB = [x_t, sb("x_t2", [128, D], F32)]
    oT = sb("oT", [128, 12, 128], BF16)
    G = [sb(f"g{i}", [128, 512], F32) for i in range(NG)]
    PSO = k.banks[7]

    CMR = sb("CMR", [128, NCMR, 128], F32)
    if USE_R:
        k.R.update(["CMR", "cq", "proj", "proj1", "S_dn", "S_gla", "M_rw"] + [f"g{i}" for i in range(26)])

    def cmat(i, r, c):
        return CMR[0:r, i, 0:c] if USE_R else CM[0:r, i, 0:c]

    def cid(n):
        return CM[0:n, C_ID, 0:n]

    k.dma(CM[:], cmd)
    k.cp(identb[:], CM[:, C_ID, :], eng="dve")
    k.cp(CMR[:], CM[:, 0:NCMR, :], eng="dve")
    k.dma(x_t[0:NS + 1, :], call)
    k.act(x_t[0:NS + 1, :], x_t[0:NS + 1, :], AF.Silu)
    b_ = k.bank()
    for kc in range(8):
        k.tr(b_[:, kc * 32:kc * 32 + NS + 1], x_t[0:NS + 1, kc * 128:(kc + 1) * 128], cid(NS + 1))
    k.cp(cT[:], v3(b_[:, 0:256], 8)[:, :, 0:NS + 1], eng="dve")

    psegs = [("p", t, 128, t * 128) for t in range(NT)]
    ssegs = [("s", s, 1, SEQ + s) for s in range(NS)]
    sbatch = ("sb", 0, NS, SEQ)

    def xsrc(l, seg):
        kind, idx, n, row = seg
        if l == 0:
            return xp[idx * 128:(idx + 1) * 128, :] if kind == "p" else xs[idx:idx + n, :]
        return xmid[row:row + n, :]

    def front_into(l, seg, wcols, cache, slot, defer=False):
        save = (proj.t, hT.t)
        proj.t, hT.t = PB[slot], HB[slot]
        r = front(l, seg, wcols, cache, defer)
        proj.t, hT.t = save
        return r

    def run_post_tail():
        f = k.post_tail
        k.post_tail = None
        if f is not None:
            f()

    def run_pre_tail():
        f = k.pre_tail
        k.pre_tail = None
        if f is not None:
            k.post_tail = f()

    def mixer_pass(l, wcols, pre, core, post, batch_core=None):
        cache = None
        if "dn" in PHASES:
            cache = "store" if wcols == DN_W else "load"
        front_into(l, psegs[0], wcols, cache, 0)
        for t, seg in enumerate(psegs):
            proj.t, hT.t = PB[t % 2], HB[t % 2]
            nxt = psegs[t + 1] if t + 1 < NT else sbatch
            k.pre_tail = (lambda nxt=nxt, t=t: front_into(l, nxt, wcols, cache, (t + 1) % 2, defer=True))
            core(l, seg)
            run_pre_tail()
            run_post_tail()
            post(seg)
        proj.t, hT.t = PB[NT % 2], HB[NT % 2]
        if k.next_load is not None:
            k.next_load()
            k.next_load = None
        batch_core(l)

    def bload(dst, src_row, width):
        k.dma(dst, bc(src_row, [128, width]))

    def load_weights(l, c0, width, dst_off, rows_kc=8, src=None):
        src = w_in if src is None else src
        k.e("dve", "memset", _keys={"ap": [f"W{j}" for j in range(12)]}, ap=W[:, 0:2], constant=0.0)
        for kc in range(rows_kc):
            half = -(-width // 2)
            for a in (0, half):
                wd = min(half, width - a)
                dst = W[:, dst_off + kc * width + a: dst_off + kc * width + a + wd]
                k.dma(dst, src[l, kc * 128:(kc + 1) * 128, c0 + a:c0 + a + wd], q="pool", wkey=f"W{kc}")

    def extract_mod(lhsT, n, c0, ncols):
        if lhsT == "prompt":
            k.dma(modseg[0:n, 0:ncols], bc(modall[NS:NS + 1, c0:c0 + ncols], [n, ncols]))
        else:
            k.dma(modseg[0:n, 0:ncols], modall[0:n, c0:c0 + ncols])

    def seg_mod(seg, c0, ncols):
        kind, idx, n, row = seg
        if kind == "sb":
            extract_mod("batch", NS, c0, ncols)

    def front(l, seg, wcols, cache=None, defer=False):
        kind, idx, n, row = seg
        slot = hts[idx if kind == "p" else NT].rearrange("p (a b) -> p a b", a=8)[:, :, 0:n]
        if cache == "load":
            k.dma(hT[:, :, 0:n], slot)
        else:
            front_h(l, seg)
            if cache == "store":
                k.dma(slot, hT[:, :, 0:n], q="pool")
        c = 0
        evacs = []
        pt = proj.t
        while c < wcols:
            wd = min(512, wcols - c)
            b = k.bank()
            for kc in range(8):
                k.mm(b[0:n, 0:wd], hT[:, kc, 0:n], W[:, kc * wcols + c: kc * wcols + c + wd],
                     start=(kc == 0), stop=(kc == 7), rkey=f"W{kc}")
            evacs.append((pt[0:n, c:c + wd], b[0:n, 0:wd]))
            c += wd

        def do_evacs():
            for o_, i_ in evacs:
                k.cp(o_, i_)

        if defer:
            return do_evacs
        do_evacs()
        return None

    def front_h(l, seg):
        kind, idx, n, row = seg
        seg_mod(seg, 0, 2 * D)
        k.dma(x_t[0:n, :], xsrc(l, seg))
        hf = cq[0:n, 0:D]
        k.memset(sm[0:n, 0:1], 0.0)
        k.e("act", "activation", out=hf, in_=x_t[0:n, :], func=AF.Square, accum_out=sm[0:n, 0:1])
        k.act(sm[0:n, 1:2], sm[0:n, 0:1], AF.Ln, bias=EPS, scale=1.0 / D)
        k.act(sm[0:n, 2:3], sm[0:n, 1:2], AF.Exp, scale=-0.5)
        k.stt(hf, x_t[0:n, :], sm[0:n, 2:3], modseg[0:n, D:2 * D], ALU.mult, ALU.mult)
        k.tt(hb[0:n, :], hf, modseg[0:n, 0:D], ALU.add)
        b = k.bank()
        bb = b[:, :].bitcast(BF16)
        for kc in range(8):
            k.tr(bb[:, kc * 128:kc * 128 + n], hb[0:n, kc * 128:(kc + 1) * 128], identb[0:n, 0:n])
        k.cp(hT[:, :, 0:n], v3(bb, 8)[:, :, 0:n], eng="dve")

    def rms_gate_store(seg, nh, hd, gain, gate_in, col0, eps, src=None, otile=None):
        kind, idx, n, row = seg
        run_pre_tail()
        o3 = v3(PSO[0:n, :], nh)
        sq, on, zs = G[0], (G[1] if otile is None else otile), G[2]
        if src is None:
            k.cp(on[0:n, :], PSO[0:n, :])
        else:
            on = src
        k.tt(sq[0:n, :], on[0:n, :], on[0:n, :], ALU.mult)
        k.red(sm[0:n, 16:16 + nh], v3(sq[0:n, :], nh))
        k.act(sm[0:n, 24:24 + nh], sm[0:n, 16:16 + nh], AF.Ln, bias=eps, scale=1.0 / hd)
        k.act(sm[0:n, 32:32 + nh], sm[0:n, 24:24 + nh], AF.Exp, scale=-0.5)
        k.tt(v3(on[0:n, :], nh), v3(on[0:n, :], nh), bc(sm[0:n, 32:32 + nh].unsqueeze(2), [n, nh, hd]), ALU.mult)
        k.tt(v3(on[0:n, :], nh), v3(on[0:n, :], nh), bc(gain.unsqueeze(1), [n, nh, hd]), ALU.mult)
        k.act(zs[0:n, :], gate_in, AF.Silu)
        k.tt(on[0:n, :], on[0:n, :], zs[0:n, :], ALU.mult)
        k.dma(omix[row:row + n, col0:col0 + 512], on[0:n, :], q="pool")
        run_post_tail()

    def neumann(n, Nbd, NbdT, Loff, tiles):
        idb = bc(CM[0:n, C_ID:C_ID + 1, 0:n], [n, 4, n])
        Xa, Xb, Pa, Pb, Qa, Qb = tiles

        def t3(t):
            return v3(t[0:n, :], 4)[:, :, 0:n]

        if n == 1:
            k.memset(t3(Xa), 1.0)
            return t3(Xa)
        X = t3(Xa)
        k.tt(X, NbdT, idb, ALU.add)
        P, Pt = Nbd, NbdT
        spare = [Xb, Pa, Pb, Qa, Qb]
        cur = {"X": Xa}
        pp = [Pa, Pb]
        qq = [Qa, Qb]
        xx = [Xb, Xa]
        for lvl in range(1, 6):
            bP = k.bank()
            for h in range(4):
                k.mm(v3(bP[0:n, :], 4)[:, h, 0:n], Pt[:, h, :], P[:, h, :])
            Pn = t3(pp[lvl % 2])
            k.cp(Pn, v3(bP[0:n, :], 4)[:, :, 0:n])
            if lvl < 5:
                bQ = k.bank()
                for h in range(4):
                    k.mm(v3(bQ[0:n, :], 4)[:, h, 0:n], P[:, h, :], Pt[:, h, :])
                Qn = t3(qq[lvl % 2])
                k.cp(Qn, v3(bQ[0:n, :], 4)[:, :, 0:n])
            bX = k.bank()
            for h in range(4):
                k.mm(v3(bX[0:n, :], 4)[:, h, 0:n], Pn[:, h, :], X[:, h, :])
            Xn = t3(xx[(lvl - 1) % 2])
            k.tt(Xn, X, v3(bX[0:n, :], 4)[:, :, 0:n], ALU.add)
            X = Xn
            P = Pn
            if lvl < 5:
                Pt = Qn
        bT = k.bank()
        for h in range(4):
            k.tr(v3(bT[0:n, :], 4)[:, h, 0:n], X[:, h, :], cid(n))
        Tbd = t3(Pa)
        k.cp(Tbd, v3(bT[0:n, :], 4)[:, :, 0:n])
        bY = k.bank()
        for h in range(4):
            k.mm(v3(bY[0:n, :], 4)[:, h, 0:n], Loff[:, h, :], X[:, h, :])
        Y = t3(Pb)
        k.cp(Y, v3(bY[0:n, :], 4)[:, :, 0:n])
        bZ = k.bank()
        for h in range(4):
            k.mm(v3(bZ[0:n, :], 4)[:, h, 0:n], Tbd[:, h, :], Y[:, h, :])
        Xf = t3(Qa)
        k.tt(Xf, X, v3(bZ[0:n, :], 4)[:, :, 0:n], ALU.subtract)
        return Xf

    def layer_mod(l):
        gi = 0
        for cc in range(6):
            b = k.bank()
            for kc in range(8):
                g = stage[gi % 2][:, (gi // 2 % 2) * 512:(gi // 2 % 2) * 512 + 512]
                gi += 1
                k.dma(g, ada_w[l, kc * 128:(kc + 1) * 128, cc * 512:(cc + 1) * 512])
                k.mm(b[0:NS + 1, :], cT[:, kc, :], g, start=(kc == 0), stop=(kc == 7))
            gb = hT[0:NS + 1, :, :].rearrange("p a b -> p (a b)").bitcast(F32)
            mo = hb[0:NS + 1, :].bitcast(F32)
            k.dma(gb, bc(ada_b[l:l + 1, cc * 512:(cc + 1) * 512], [NS + 1, 512]))
            k.tt(mo, b[0:NS + 1, :], gb, ALU.add)
            if cc in (2, 3):
                k.dma(x_t[0:NS + 1, 0:512], bc(norm_g[l:l + 1, (cc - 2) * 512:(cc - 1) * 512], [NS + 1, 512]))
                k.stt(mo, mo, 1.0, x_t[0:NS + 1, 0:512], ALU.add, ALU.mult)
            k.dma(modall[:, cc * 512:(cc + 1) * 512], mo, q="pool")

    def softplus_parts(out, x, n, w, t1, t2):
        k.act(t1, x, AF.Abs)
        k.act(t2, t1, AF.Exp, scale=-1.0)
        k.act(out, t2, AF.Ln, bias=1.0)

    def dn_phase(l):
        load_weights(l, DN_C0, DN_W, 0)
        for d_ in range(4):
            bload(PC[:, d_ * 1536:(d_ + 1) * 1536], dn_conv_w[l, 3 - d_:4 - d_, :], 1536)
        bload(PC[:, 6144:6272], dn_norm_g[l:l + 1, :], 128)
        bload(PC[:, 6272:6276], dn_a_log[l:l + 1, :], 4)
        bload(PC[:, 6276:6280], dn_dt_bias[l:l + 1, :], 4)
        k.act(PC[:, 6280:6284], PC[:, 6272:6276], AF.Exp)
        k.ts(PC[:, 6280:6284], PC[:, 6280:6284], -1.0, ALU.mult)
        extract_mod("prompt", 128, 0, 2 * D)
        k.memset(TL, 0.0)
        k.memset(S_dn[:], 0.0)
        def pre(seg):
            k.dma(TL, st_conv[l, seg[1]])
            k.dma(v3(stage[0][:, 0:512], 4), st_dn[l, seg[1]].rearrange("h k v -> k h v"))
            k.cp(S_dn[:], v3(stage[0][:, 0:512], 4), eng="dve")

        def post(seg):
            kind, idx, n, row = seg
            if kind == "s":
                k.dma(s_conv[l, idx], TL, q="pool")
                k.dma(s_dn[l, idx].rearrange("h k v -> k h v"), S_dn[:], q="pool")
            elif idx == NT - 1:
                k.dma(p_conv[l], TL, q="pool")
                k.dma(p_dn[l].rearrange("h k v -> k h v"), S_dn[:], q="pool")

        def _nl():
            load_weights(l, GLA_C0, GLA_W, 0)
            k.preloaded = (l, "gla")
        k.next_load = _nl
        mixer_pass(l, DN_W, pre, dn_segment, post, batch_core=dn_batch)

    def dn_segment(l, seg):
        kind, idx, n, row = seg
        qkv = proj[0:n, 0:1536]
        for c in range(3):
            cs = slice(c * 512, (c + 1) * 512)
            b = k.bank()
            ops = []
            for d_ in range(4):
                if n > d_:
                    g = G[d_]
                    k.tt(g[0:n, :], proj[0:n, cs], PC[0:n, d_ * 1536 + c * 512: d_ * 1536 + (c + 1) * 512], ALU.mult)
                    ops.append((cmat((C_ID, C_SH1, C_SH2, C_SH3)[d_], n, n), g[0:n, :]))
            for d_ in range(1, 4):
                g = G[3 + d_]
                k.tt(g[0:3, :], TL[0:3, cs], PC[0:3, d_ * 1536 + c * 512: d_ * 1536 + (c + 1) * 512], ALU.mult)
                ops.append((cmat((C_BD1, C_BD2, C_BD3)[d_ - 1], 3, n), g[0:3, :]))
            for i, (lt, rh) in enumerate(ops):
                k.mm(b[0:n, :], lt, rh, start=(i == 0), stop=(i == len(ops) - 1))
            k.act(cq[0:n, cs], b[0:n, :], AF.Silu)
        for c in range(3):
            cs = slice(c * 512, (c + 1) * 512)
            b = k.bank()
            if n == 128:
                k.mm(b[0:3, :], cmat(C_SELB128, 128, 3), proj[0:n, cs])
            else:
                k.mm(b[0:3, :], cmat(C_A1, 3, 3), TL[0:3, cs], start=True, stop=False)
                k.mm(b[0:3, :], cmat(C_SELB1, 1, 3), proj[0:1, cs], start=False, stop=True)
            k.cp(TL[0:3, cs], b[0:3, :])
        qn, kn, vv, beta, gg = dn_prep(n)
        dn_chunk(l, seg, qn, kn, vv, beta, gg)

    def dn_prep(n):
        sq = G[0]
        for hlf in range(2):
            k.tt(sq[0:n, :], cq[0:n, hlf * 512:(hlf + 1) * 512], cq[0:n, hlf * 512:(hlf + 1) * 512], ALU.mult)
            k.red(sm[0:n, 4 + hlf * 4:8 + hlf * 4], v3(sq[0:n, :], 4))
        k.act(sm[0:n, 12:20], sm[0:n, 4:12], AF.Ln, bias=EPS)
        k.act(sm[0:n, 20:28], sm[0:n, 12:20], AF.Exp, scale=-0.5)
        k.ts(sm[0:n, 20:24], sm[0:n, 20:24], 128.0 ** -0.5, ALU.mult)
        qk3 = v3(cq[0:n, 0:1024], 8)
        k.tt(qk3, qk3, bc(sm[0:n, 20:28].unsqueeze(2), [n, 8, 128]), ALU.mult)
        qn = cq[0:n, 0:512]
        kn = cq[0:n, 512:1024]
        vv = cq[0:n, 1024:1536]
        beta = sm[0:n, 28:32]
        k.act(beta, proj[0:n, 2048:2052], AF.Sigmoid)
        tt_ = sm[0:n, 32:36]
        k.tt(tt_, proj[0:n, 2052:2056], PC[0:n, 6276:6280], ALU.add)
        softplus_parts(sm[0:n, 36:40], tt_, n, 4, sm[0:n, 40:44], sm[0:n, 44:48])
        k.ts(sm[0:n, 40:44], tt_, 0.0, ALU.max)
        k.tt(sm[0:n, 40:44], sm[0:n, 40:44], sm[0:n, 36:40], ALU.add)
        gg = sm[0:n, 48:52]
        k.tt(gg, sm[0:n, 40:44], PC[0:n, 6280:6284], ALU.mult)
        return qn, kn, vv, beta, gg

    def i16(p0, p1, shape, axis):
        a = CM[p0:p1, C_I16:C_I16 + 2, :].rearrange("p a (b c) -> p (a b) c", c=16)[:, 0:NS, 0:NS]
        return bc(a.unsqueeze(axis), shape)

    def dn_batch(l):
        n = NS
        for c in range(3):
            cs = slice(c * 512, (c + 1) * 512)
            acc, tmp = G[0], G[1]
            k.tt(acc[0:n, :], proj[0:n, cs], PC[0:n, c * 512:(c + 1) * 512], ALU.mult)
            for d_ in range(1, 4):
                st = stage[d_ % 2][0:n, (d_ // 2) * 512:(d_ // 2) * 512 + 512]
                k.dma(st, st_conv[l, :, 3 - d_, cs])
                k.tt(tmp[0:n, :], st, PC[0:n, d_ * 1536 + c * 512: d_ * 1536 + (c + 1) * 512], ALU.mult)
                k.tt(acc[0:n, :], acc[0:n, :], tmp[0:n, :], ALU.add)
            k.act(cq[0:n, cs], acc[0:n, :], AF.Silu)
        k.dma(s_conv[l, :, 0:2, :], st_conv[l, :, 1:3, :], q="pool")
        k.dma(s_conv[l, :, 2, :], proj[0:n, 0:1536], q="pool")
        qn, kn, vv, beta, gg = dn_prep(n)
        eG = sm[0:n, 60:64]
        k.act(eG, gg, AF.Exp)
        beG = sm[0:n, 68:72]
        k.tt(beG, beta, eG, ALU.mult)

        def h3(t):
            return v3(t[0:n, :], 4)

        def bcs(s_):
            return bc(s_.unsqueeze(2), [n, 4, 128])

        kbg, vb, qg, t0 = G[0], G[1], G[2], G[3]
        k.tt(h3(kbg), v3(kn, 4), bcs(beG), ALU.mult)
        k.tt(h3(vb), v3(vv, 4), bcs(beta), ALU.mult)
        k.tt(h3(qg), v3(qn, 4), bcs(eG), ALU.mult)
        k.tt(t0[0:n, :], qn, kn, ALU.mult)
        Aqk = sm[0:n, 72:76]
        k.red(Aqk, h3(t0))
        MLk, MLq = (G[4], G[5]), (G[6], G[7])
        for src, ML in ((kbg, MLk), (qg, MLq)):
            b = k.bank()
            for h in range(4):
                k.tr(b[:, h * 128:h * 128 + n], src[0:n, h * 128:(h + 1) * 128], cid(n))
            for hp in range(2):
                o4 = ML[hp][:, :].rearrange("p (a s m) -> p a s m", a=2, s=16)[:, :, 0:n, 0:n]
                i4 = v3(b[:, :], 4)[:, 2 * hp:2 * hp + 2, 0:n]
                k.tt(o4, bc(i4.unsqueeze(2), [128, 2, n, n]), i16(0, 128, [128, 2, n, n], 1), ALU.mult)
        bKS, bQS = k.bank(), k.bank()
        k.memset(bKS[0:n, :], 0.0)
        k.memset(bQS[0:n, :], 0.0)
        for s_ in range(n):
            st = v3(stage[s_ % 2][:, 0:512], 4)
            k.dma(st, st_dn[l, s_].rearrange("h k v -> k h v"))
            for h in range(4):
                for ML, bk in ((MLk, bKS), (MLq, bQS)):
                    lt = ML[h // 2][:, :].rearrange("p (a s m) -> p a s m", a=2, s=16)[:, h % 2, s_, 0:n]
                    k.mm(bk[0:n, h * 128:(h + 1) * 128], lt, st[:, h, :], start=False, stop=True, skip=True)
        vnew, o_ = G[8], G[9]
        k.tt(vnew[0:n, :], vb[0:n, :], bKS[0:n, :], ALU.subtract)
        k.tt(h3(o_), h3(vnew), bcs(Aqk), ALU.mult)
        k.tt(o_[0:n, :], o_[0:n, :], bQS[0:n, :], ALU.add)
        t1 = G[10]
        k.tt(t1[0:n, 0:4 * n].rearrange("p (s h) -> p s h", h=4), bc(eG.unsqueeze(1), [n, n, 4]),
             bc(CM[0:n, C_ID, 0:n].unsqueeze(2), [n, n, 4]), ALU.mult)
        b = k.bank()
        k.mm(b[:, 0:4 * n], cmat(C_ONES, n, 128), t1[0:n, 0:4 * n])
        EGB = G[11]
        k.cp(EGB[:, 0:4 * n], b[:, 0:4 * n])
        for s_ in range(n):
            st = v3(stage[s_ % 2][:, 0:512], 4)
            k.dma(st, st_dn[l, s_].rearrange("h k v -> k h v"))
            vm = G[12 + s_ % 2]
            k.ts(vm[0:n, :], vnew[0:n, :], CM[0:n, C_ID, s_:s_ + 1], ALU.mult)
            b = k.bank()
            for h in range(4):
                k.mm(b[:, h * 128:(h + 1) * 128], kn[:, h * 128:(h + 1) * 128], vm[0:n, h * 128:(h + 1) * 128])
            so = G[14 + s_ % 2]
            k.tt(v3(so[:, :], 4), st, bc(EGB[:, s_ * 4:(s_ + 1) * 4].unsqueeze(2), [128, 4, 128]), ALU.mult)
            k.tt(so[:, :], so[:, :], b[:, :], ALU.add)
            k.dma(s_dn[l, s_].rearrange("h k v -> k h v"), v3(so[:, :], 4), q="pool")
        rms_gate_store(sbatch, 4, 128, PC[0:n, 6144:6272], proj[0:n, 1536:2048], 0, EPS, src=o_)

    def dn_chunk(l, seg, qn, kn, vv, beta, gg):
        kind, idx, n, row = seg
        b = k.bank()
        k.mm(b[0:n, 0:4], cmat(C_TRIU, n, n), gg)
        Gc = sm[0:n, 52:56]
        k.cp(Gc, b[0:n, 0:4], eng="dve")
        nG = sm[0:n, 56:60]
        k.ts(nG, Gc, -1.0, ALU.mult)
        b = k.bank()
        k.mm(b[:, 0:4], cmat(C_ONES, n, 128), gg)
        GT = sm2[:, 0:4]
        k.cp(GT, b[:, 0:4], eng="dve")
        eGl = sm2[:, 4:8]
        k.act(eGl, GT, AF.Exp)
        eG = sm[0:n, 60:64]
        k.act(eG, Gc, AF.Exp)
        edk = sm[0:n, 64:68]
        k.tt(edk, GT[0:n, :], Gc, ALU.subtract)
        k.act(edk, edk, AF.Exp)
        beG = sm[0:n, 68:72]
        k.tt(beG, beta, eG, ALU.mult)

        def h3(t):
            return v3(t[0:n, :], 4)

        def bcs(s):
            return bc(s.unsqueeze(2), [n, 4, 128])

        kbg, vb, kd, qg = G[0], G[1], G[2], G[3]
        k.tt(h3(kbg), v3(kn, 4), bcs(beG), ALU.mult)
        k.tt(h3(vb), v3(vv, 4), bcs(beta), ALU.mult)
        k.tt(h3(kd), v3(kn, 4), bcs(edk), ALU.mult)
        k.tt(h3(qg), v3(qn, 4), bcs(eG), ALU.mult)
        KT, QT, QGT = G[4], G[5], G[6]
        for src, dst in ((kn, KT), (qn, QT), (qg[0:n, :], QGT)):
            b = k.bank()
            for h in range(4):
                k.tr(b[:, h * 128:h * 128 + n], src[:, h * 128:(h + 1) * 128], cid(n))
            k.cp(v3(dst[:, :], 4)[:, :, 0:n], v3(b[:, :], 4)[:, :, 0:n])

        def f3(t):
            return v3(t[:, :], 4)[:, :, 0:n]

        def m3(t):
            return v3(t[0:n, :], 4)[:, :, 0:n]

        dg = G[7]
        k.tt(m3(dg), bc(CM[0:n, C_ID:C_ID + 1, 0:n], [n, 4, n]), bc(Gc.unsqueeze(2), [n, 4, n]), ALU.mult)
        bR = k.bank()
        for h in range(4):
            k.mm(m3(bR)[:, h, :], cmat(C_ONES, n, n), m3(dg)[:, h, :])
        t1, t2, Dm, DTm = G[8], G[9], G[10], G[11]
        k.stt(m3(t1), m3(bR), -1.0, bc(CM[0:n, C_MLOW:C_MLOW + 1, 0:n], [n, 4, n]), ALU.mult, ALU.add)
        k.tt(m3(t2), m3(bR), bc(CM[0:n, C_MUP:C_MUP + 1, 0:n], [n, 4, n]), ALU.add)
        for h in range(4):
            k.act(m3(Dm)[:, h, :], m3(t1)[:, h, :], AF.Exp, bias=Gc[:, h:h + 1])
            k.act(m3(DTm)[:, h, :], m3(t2)[:, h, :], AF.Exp, bias=nG[:, h:h + 1])
        bK = k.bank()
        bA = k.bank()
        for h in range(4):
            k.mm(m3(bK)[:, h, :], f3(KT)[:, h, :], f3(KT)[:, h, :])
            k.mm(m3(bA)[:, h, :], f3(KT)[:, h, :], f3(QT)[:, h, :])
        KKD, AT = G[12], G[13]
        k.tt(m3(KKD), m3(bK), m3(Dm), ALU.mult)
        k.tt(m3(KKD), m3(KKD), bc(beta.unsqueeze(2), [n, 4, n]), ALU.mult)
        k.tt(m3(AT), m3(bA), m3(DTm), ALU.mult)
        if n > 1:
            Nbd, NbdT, Loff = G[14], G[15], G[16]
            k.tt(m3(Nbd), m3(KKD), bc(CM[0:n, C_NTRIL64:C_NTRIL64 + 1, 0:n], [n, 4, n]), ALU.mult)
            k.tt(m3(Loff), m3(KKD), bc(CM[0:n, C_LOWLEFT:C_LOWLEFT + 1, 0:n], [n, 4, n]), ALU.mult)
            b = k.bank()
            for h in range(4):
                k.tr(m3(b)[:, h, :], m3(Nbd)[:, h, :], cid(n))
            k.cp(m3(NbdT), m3(b))
            X = neumann(n, m3(Nbd), m3(NbdT), m3(Loff), G[17:23])
        else:
            X = neumann(n, None, None, None, G[17:23])
        bW = k.bank()
        for h in range(4):
            k.mm(f3(bW)[:, h, :], h3(kbg)[:, h, :], X[:, h, :])
        nWT = G[7]
        k.ts(f3(nWT), f3(bW), -1.0, ALU.mult)
        bV = k.bank()
        for h in range(4):
            k.mm(bV[0:n, h * 128:(h + 1) * 128], X[:, h, :], h3(vb)[:, h, :], start=True, stop=False)
            k.mm(bV[0:n, h * 128:(h + 1) * 128], f3(nWT)[:, h, :], S_dn[:, h, :], start=False, stop=True)
        VN = G[8]
        k.cp(VN[0:n, :], bV[0:n, :])
        for h in range(4):
            k.mm(PSO[0:n, h * 128:(h + 1) * 128], f3(QGT)[:, h, :], S_dn[:, h, :], start=True, stop=False)
            k.mm(PSO[0:n, h * 128:(h + 1) * 128], m3(AT)[:, h, :], h3(VN)[:, h, :], start=False, stop=True)
        bS = k.bank()
        for h in range(4):
            k.mm(bS[:, h * 128:(h + 1) * 128], h3(kd)[:, h, :], h3(VN)[:, h, :])
        for h in range(4):
            k.stt(S_dn[:, h, :], S_dn[:, h, :], eGl[:, h:h + 1], bS[:, h * 128:(h + 1) * 128], ALU.mult, ALU.add)
        rms_gate_store(seg, 4, 128, PC[0:n, 6144:6272], proj[0:n, 1536:2048], 0, EPS, otile=G[22])

    def gla_phase(l):
        if k.preloaded != (l, "gla"):
            load_weights(l, GLA_C0, GLA_W, 0)
        bload(PC[:, 0:256], gla_bf[l:l + 1, :], 256)
        bload(PC[:, 256:384], gla_norm_g[l:l + 1, :], 128)
        k.dma(wsm[0:16, 0:256], gla_wf[l])
        extract_mod("prompt", 128, 0, 2 * D)
        k.memset(S_gla[:], 0.0)
        k.memset(G[12][:, :], 0.0)
        k.memset(G[15][:, :], 0.0)
        def pre(seg):
            k.dma(v3(stage[0][:, 0:256], 2), st_gla[l, seg[1]].rearrange("(hp h2) k v -> (h2 k) hp v", h2=2))
            k.cp(S_gla[:], v3(stage[0][:, 0:256], 2), eng="dve")

        def post(seg):
            kind, idx, n, row = seg
            if kind == "s":
                k.dma(s_gla[l, idx].rearrange("(hp h2) k v -> (h2 k) hp v", h2=2), S_gla[:], q="pool")
            elif idx == NT - 1:
                k.dma(p_gla[l].rearrange("(hp h2) k v -> (h2 k) hp v", h2=2), S_gla[:], q="pool")

        def _nl():
            load_weights(l, RW_C0, RW_W, 0)
            k.preloaded = (l, "rw")
        k.next_load = _nl
        mixer_pass(l, GLA_W, pre, gla_segment, post, batch_core=gla_batch)

    def gla_segment(l, seg):
        kind, idx, n, row = seg
        q_ = proj[0:n, 0:256]
        k_ = proj[0:n, 256:512]
        v_ = proj[0:n, 512:1024]
        gz = proj[0:n, 1024:1536]
        lf = gla_lf(n)
        LF = lf[0:n, 0:256]
        gla_chunk(l, seg, q_, k_, v_, gz, lf, LF)

    def gla_lf(n):
        glo = proj[0:n, 1536:1552]
        b = k.bank()
        k.tr(b[0:16, 0:n], glo, cid(n))
        gloT = G[0]
        k.cp(gloT[0:16, 0:n], b[0:16, 0:n])
        b = k.bank()
        k.mm(b[0:n, 0:256], gloT[0:16, 0:n], wsm[0:16, 0:256])
        xb, t1, t2, lf = G[1], G[2], G[3], G[4]
        k.tt(xb[0:n, 0:256], b[0:n, 0:256], PC[0:n, 0:256], ALU.add)
        softplus_parts(t1[0:n, 0:256], xb[0:n, 0:256], n, 256, t2[0:n, 0:256], t2[0:n, 256:512])
        k.ts(t2[0:n, 0:256], xb[0:n, 0:256], 0.0, ALU.min)
        k.tt(lf[0:n, 0:256], t2[0:n, 0:256], t1[0:n, 0:256], ALU.subtract)
        k.ts(lf[0:n, 0:256], lf[0:n, 0:256], 1.0 / 16.0, ALU.mult)
        return lf

    def gla_batch(l):
        n = NS
        q_ = proj[0:n, 0:256]
        k_ = proj[0:n, 256:512]
        v_ = proj[0:n, 512:1024]
        gz = proj[0:n, 1024:1536]
        lf = gla_lf(n)
        LF = lf[0:n, 0:256]
        eG, enG, qg, kg, t0 = G[5][0:n, 0:256], G[6][0:n, 0:256], G[7][0:n, 0:256], G[8][0:n, 0:256], G[9][0:n, 0:256]
        k.act(eG, LF, AF.Exp)
        k.act(enG, LF, AF.Exp, scale=-1.0)
        k.stt(qg, q_, 0.125, eG, ALU.mult, ALU.mult)
        k.tt(kg, k_, enG, ALU.mult)
        k.tt(t0, qg, kg, ALU.mult)
        Aqk = sm[0:n, 72:76]
        k.red(Aqk, v3(t0, 4))
        MLq = (G[12], G[15])
        b = k.bank()
        for hp in range(2):
            k.tr(b[:, hp * 128:hp * 128 + n], qg[:, hp * 128:(hp + 1) * 128], cid(n))
        for h2 in range(2):
            pr0, pr1 = h2 * 64, h2 * 64 + 64
            o4 = MLq[h2][pr0:pr1, :].rearrange("p (a s m) -> p a s m", a=2, s=16)[:, :, 0:n, 0:n]
            i4 = v3(b[pr0:pr1, 0:256], 2)[:, :, 0:n]
            k.tt(o4, bc(i4.unsqueeze(2), [64, 2, n, n]), i16(pr0, pr1, [64, 2, n, n], 1), ALU.mult)
        b = k.bank()
        for hp in range(2):
            k.tr(b[:, hp * 128:hp * 128 + n], lf[0:n, hp * 128:(hp + 1) * 128], cid(n))
        EGB = G[10]
        k.act(v3(EGB[:, 0:256], 2)[:, :, 0:n], v3(b[:, 0:256], 2)[:, :, 0:n], AF.Exp)
        bQS = k.bank()
        k.memset(bQS[0:n, :], 0.0)
        for s_ in range(n):
            st = v3(stage[s_ % 2][:, 0:256], 2)
            k.dma(st, st_gla[l, s_].rearrange("(hp h2) k v -> (h2 k) hp v", h2=2))
            for h in range(4):
                lt = MLq[h % 2][:, :].rearrange("p (a s m) -> p a s m", a=2, s=16)[:, h // 2, s_, 0:n]
                k.mm(bQS[0:n, h * 128:(h + 1) * 128], lt, st[:, h // 2, :], start=False, stop=True, skip=True)
        o_ = G[11]
        k.tt(v3(o_[0:n, :], 4), v3(v_, 4), bc(Aqk.unsqueeze(2), [n, 4, 128]), ALU.mult)
        k.tt(o_[0:n, :], o_[0:n, :], bQS[0:n, :], ALU.add)
        for s_ in range(n):
            st = v3(stage[s_ % 2][:, 0:256], 2)
            k.dma(st, st_gla[l, s_].rearrange("(hp h2) k v -> (h2 k) hp v", h2=2))
            vm = G[0 + s_ % 2]
            k.ts(vm[0:n, :], v_, CM[0:n, C_ID, s_:s_ + 1], ALU.mult)
            b = k.bank()
            for hp in range(2):
                k.mm(b[:, hp * 256:(hp + 1) * 256], k_[:, hp * 128:(hp + 1) * 128], vm[0:n, hp * 256:(hp + 1) * 256])
            so = G[2 + s_ % 2]
            for hp in range(2):
                for h2 in range(2):
                    pr = slice(h2 * 64, h2 * 64 + 64)
                    k.stt(v3(so[:, 0:256], 2)[pr, hp, :], st[pr, hp, :], v3(EGB[:, 0:256], 2)[pr, hp, s_:s_ + 1],
                          b[pr, hp * 256 + h2 * 128: hp * 256 + (h2 + 1) * 128], ALU.mult, ALU.add)
            k.dma(s_gla[l, s_].rearrange("(hp h2) k v -> (h2 k) hp v", h2=2), v3(so[:, 0:256], 2), q="pool")
        rms_gate_store(sbatch, 4, 128, PC[0:n, 256:384], gz, 512, EPS, src=o_)

    def gla_chunk(l, seg, q_, k_, v_, gz, lf, LF):
        kind, idx, n, row = seg
        bG = k.bank()
        k.mm(bG[0:n, 0:256], cmat(C_TRIU, n, n), LF)
        Gc = G[5][0:n, 0:256]
        k.cp(Gc, bG[0:n, 0:256])
        bT = k.bank()
        k.mm(bT[0:n, 0:256], cmat(C_ONES, n, n), LF)
        edk = G[6][0:n, 0:256]
        k.tt(edk, bT[0:n, 0:256], Gc, ALU.subtract)
        k.act(edk, edk, AF.Exp)
        bE = k.bank()
        for hp in range(2):
            k.mm(bE[:, 2 * hp:2 * hp + 2], lf[0:n, hp * 128:(hp + 1) * 128], cmat(C_ONES, n, 2))
        eGl = sm2[:, 8:10]
        k.act(eGl, v3(bE[:, 0:4], 2)[:, :, 0], AF.Exp)
        eG = G[7][0:n, 0:256]
        enG = G[8][0:n, 0:256]
        k.act(eG, Gc, AF.Exp)
        k.act(enG, Gc, AF.Exp, scale=-1.0)
        qg = G[9][0:n, 0:256]
        kg = G[10][0:n, 0:256]
        kd = G[11][0:n, 0:256]
        k.stt(qg, q_, 0.125, eG, ALU.mult, ALU.mult)
        k.tt(kg, k_, enG, ALU.mult)
        k.tt(kd, k_, edk, ALU.mult)
        QGTm, KGT = (G[12], G[15]), G[13]
        for src, dst in ((qg, None), (kg, KGT)):
            b = k.bank()
            for hp in range(2):
                k.tr(b[:, hp * 128:hp * 128 + n], src[:, hp * 128:(hp + 1) * 128], cid(n))
            if dst is None:
                for h2 in range(2):
                    pr = slice(h2 * 64, h2 * 64 + 64)
                    k.cp(v3(QGTm[h2][pr, 0:256], 2)[:, :, 0:n], v3(b[pr, 0:256], 2)[:, :, 0:n])
            else:
                k.cp(v3(dst[:, 0:256], 2)[:, :, 0:n], v3(b[:, 0:256], 2)[:, :, 0:n])

        def m3(t):
            return v3(t[0:n, :], 4)[:, :, 0:n]

        bA = k.bank()
        for h in range(4):
            hp = h // 2
            k.mm(m3(bA)[:, h, :], v3(KGT[:, 0:256], 2)[:, hp, 0:n], v3(QGTm[h % 2][:, 0:256], 2)[:, hp, 0:n])
        AT = G[14]
        k.tt(m3(AT), m3(bA), bc(CM[0:n, C_TRIU:C_TRIU + 1, 0:n], [n, 4, n]), ALU.mult)
        for h in range(4):
            hp = h // 2
            k.mm(PSO[0:n, h * 128:(h + 1) * 128], v3(QGTm[h % 2][:, 0:256], 2)[:, hp, 0:n], S_gla[:, hp, :], start=True, stop=False)
            k.mm(PSO[0:n, h * 128:(h + 1) * 128], m3(AT)[:, h, :], v_[:, h * 128:(h + 1) * 128], start=False, stop=True)
        for hp in range(2):
            bS = k.bank()
            k.mm(bS[:, 0:256], kd[:, hp * 128:(hp + 1) * 128], v_[:, hp * 256:(hp + 1) * 256])
            for h2 in range(2):
                pr = slice(h2 * 64, h2 * 64 + 64)
                k.stt(S_gla[pr, hp, :], S_gla[pr, hp, :], eGl[pr, hp:hp + 1], bS[pr, h2 * 128:(h2 + 1) * 128], ALU.mult, ALU.add)
        rms_gate_store(seg, 4, 128, PC[0:n, 256:384], gz, 512, EPS, otile=G[14])

    RWC = dict(mu=0, w0=1664, a0=2176, kk=2688, ka=3200, rk=3712, lw=4224, lb=4736)

    def rw_phase(l):
        if k.preloaded != (l, "rw"):
            load_weights(l, RW_C0, RW_W, 0)
        bload(PC[:, 0:1664], rw_mu[l:l + 1, :], 1664)
        for nm, src in (("w0", rw_w0), ("a0", rw_a0), ("kk", rw_k_k), ("ka", rw_k_a), ("rk", rw_r_k),
                        ("lw", rw_ln_w), ("lb", rw_ln_b)):
            bload(PC[:, RWC[nm]:RWC[nm] + 512], src[l:l + 1, :], 512)
        k.dma(wsm[0:64, 0:512], rw_w2[l])
        k.dma(wsm[0:64, 512:1024], rw_a2[l])
        extract_mod("prompt", 128, 0, 2 * D)
        k.memset(srow, 0.0)
        k.memset(M_rw[:], 0.0)
        MT = G[26]
        def pre(seg):
            idx = seg[1]
            k.dma(srow, st_rs[l, idx:idx + 1, :])
            k.dma(v3(MT[0:64, :], 8), st_rw[l, idx].rearrange("h v k -> v h k"))
            b = k.bank()
            for hp in range(4):
                k.tr(b[:, hp * 64:(hp + 1) * 64], MT[0:64, hp * 128:(hp + 1) * 128], cid(64))
            k.cp(M_rw[:], v3(b[:, 0:256], 4))

        def post(seg):
            kind, idx, n, row = seg
            last = (kind == "s") or idx == NT - 1
            if last:
                b = k.bank()
                for hp in range(4):
                    k.tr(b[0:64, hp * 128:(hp + 1) * 128], M_rw[:, hp, :], cid(128))
                k.cp(MT[0:64, :], b[0:64, :])
                dst = s_rw[l, idx] if kind == "s" else p_rw[l]
                k.dma(dst.rearrange("h v k -> v h k"), v3(MT[0:64, :], 8), q="pool")
                dst = s_rs[l, idx:idx + 1, :] if kind == "s" else p_rs[l:l + 1, :]
                k.dma(dst, proj[n - 1:n, 0:1664], q="pool")
            else:
                k.dma(srow, proj[n - 1:n, 0:1664])

        def _nl():
            load_weights(l, 0, D, 0, rows_kc=12, src=w_out)
            k.preloaded = (l, "out")
        k.next_load = _nl
        mixer_pass(l, RW_W, pre, rw_segment, post, batch_core=rw_batch)

    def rw_segment(l, seg):
        kind, idx, n, row = seg
        xm = cq
        for c0 in range(0, 1664, 512):
            wd = min(512, 1664 - c0)
            b = k.bank()
            if n > 1:
                k.mm(b[0:n, 0:wd], cmat(C_SH1, n, n), proj[0:n, c0:c0 + wd], start=True, stop=False)
            k.mm(b[0:n, 0:wd], cmat(C_E0, 1, n), srow[0:1, c0:c0 + wd], start=(n == 1), stop=True)
            g = G[0]
            k.tt(g[0:n, 0:wd], b[0:n, 0:wd], proj[0:n, c0:c0 + wd], ALU.subtract)
            k.tt(g[0:n, 0:wd], g[0:n, 0:wd], PC[0:n, c0:c0 + wd], ALU.mult)
            k.tt(xm[0:n, c0:c0 + wd], g[0:n, 0:wd], proj[0:n, c0:c0 + wd], ALU.add)
        rw_chunk(l, seg, *rw_prep(n))

    def rw_prep(n):
        xm = cq
        r_ = xm[0:n, 0:512]
        k_ = xm[0:n, 512:1024]
        v_ = xm[0:n, 1024:1536]
        rz = proj[0:n, 1664:2176]

        def pc(nm):
            return PC[0:n, RWC[nm]:RWC[nm] + 512]

        tw = G[0]
        k.act(tw[0:n, 0:64], xm[0:n, 1536:1600], AF.Tanh)
        b = k.bank()
        k.tr(b[0:64, 0:n], tw[0:n, 0:64], cid(n))
        k.tr(b[0:64, 128:128 + n], xm[0:n, 1600:1664], cid(n))
        loT = G[1]
        k.cp(loT[0:64, 0:256], b[0:64, 0:256])
        bw = k.bank()
        k.mm(bw[0:n, :], loT[0:64, 0:n], wsm[0:64, 0:512])
        LW = G[2]
        k.tt(LW[0:n, :], bw[0:n, :], pc("w0"), ALU.add)
        k.act(LW[0:n, :], LW[0:n, :], AF.Sigmoid)
        k.ts(LW[0:n, :], LW[0:n, :], -float(np.exp(-0.5)), ALU.mult)
        ba = k.bank()
        k.mm(ba[0:n, :], loT[0:64, 128:128 + n], wsm[0:64, 512:1024])
        At = G[3]
        k.tt(At[0:n, :], ba[0:n, :], pc("a0"), ALU.add)
        k.act(At[0:n, :], At[0:n, :], AF.Sigmoid)
        KK, K2, Bt, t0 = G[4], G[5], G[6], G[7]
        k.tt(KK[0:n, :], k_, pc("kk"), ALU.mult)
        k.tt(t0[0:n, :], KK[0:n, :], KK[0:n, :], ALU.mult)
        k.red(sm[0:n, 64:72], v3(t0[0:n, :], 8))
        k.act(sm[0:n, 72:80], sm[0:n, 64:72], AF.Ln, bias=EPS)
        k.act(sm[0:n, 80:88], sm[0:n, 72:80], AF.Exp, scale=-0.5)
        k.tt(v3(KK[0:n, :], 8), v3(KK[0:n, :], 8), bc(sm[0:n, 80:88].unsqueeze(2), [n, 8, 64]), ALU.mult)
        k.stt(t0[0:n, :], At[0:n, :], -1.0, pc("ka"), ALU.add, ALU.mult)
        k.stt(K2[0:n, :], t0[0:n, :], 1.0, k_, ALU.add, ALU.mult)
        k.tt(Bt[0:n, :], KK[0:n, :], At[0:n, :], ALU.mult)
        k.tt(t0[0:n, :], r_, K2[0:n, :], ALU.mult)
        k.tt(t0[0:n, :], t0[0:n, :], pc("rk"), ALU.mult)
        bsum = sm[0:n, 88:96]
        k.red(bsum, v3(t0[0:n, :], 8))
        return r_, k_, v_, rz, LW, At, KK, K2, Bt, bsum, pc

    def rw_chunk(l, seg, r_, k_, v_, rz, LW, At, KK, K2, Bt, bsum, pc):
        kind, idx, n, row = seg
        t0 = G[7]
        bG = k.bank()
        k.mm(bG[0:n, :], cmat(C_TRIU, n, n), LW[0:n, :])
        Gc = G[8]
        k.cp(Gc[0:n, :], bG[0:n, :])
        bT = k.bank()
        k.mm(bT[0:n, :], cmat(C_ONES, n, n), LW[0:n, :])
        edk = G[9]
        k.tt(edk[0:n, :], bT[0:n, :], Gc[0:n, :], ALU.subtract)
        k.act(edk[0:n, :], edk[0:n, :], AF.Exp)
        bE = k.bank()
        for hp in range(4):
            k.mm(bE[:, 2 * hp:2 * hp + 2], LW[0:n, hp * 128:(hp + 1) * 128], cmat(C_ONES, n, 2))
        eGl = sm2[:, 12:16]
        k.act(eGl, v3(bE[:, 0:8], 4)[:, :, 0], AF.Exp)
        eG, enG, eGx = G[10], G[11], G[7]
        k.act(eG[0:n, :], Gc[0:n, :], AF.Exp)
        k.act(enG[0:n, :], Gc[0:n, :], AF.Exp, scale=-1.0)
        k.tt(eGx[0:n, :], Gc[0:n, :], LW[0:n, :], ALU.subtract)
        k.act(eGx[0:n, :], eGx[0:n, :], AF.Exp)
        kkg, rg, bg, k2g, bd, k2d = G[12], G[13], G[14], G[15], G[16], G[17]
        k.tt(kkg[0:n, :], KK[0:n, :], eGx[0:n, :], ALU.mult)
        k.tt(rg[0:n, :], r_, eG[0:n, :], ALU.mult)
        k.tt(bg[0:n, :], Bt[0:n, :], enG[0:n, :], ALU.mult)
        k.tt(k2g[0:n, :], K2[0:n, :], enG[0:n, :], ALU.mult)
        k.tt(bd[0:n, :], Bt[0:n, :], edk[0:n, :], ALU.mult)
        k.tt(k2d[0:n, :], K2[0:n, :], edk[0:n, :], ALU.mult)
        bgT, k2gT, kkgTm, rgTm = G[0], G[1], (G[2], G[3]), (G[10], G[11])
        for src, dst in ((kkg, kkgTm), (rg, rgTm), (bg, bgT), (k2g, k2gT)):
            b = k.bank()
            for hp in range(4):
                k.tr(b[:, hp * 128:hp * 128 + n], src[0:n, hp * 128:(hp + 1) * 128], cid(n))
            if isinstance(dst, tuple):
                for h2 in range(2):
                    pr = slice(h2 * 64, h2 * 64 + 64)
                    po = slice((1 - h2) * 64, (1 - h2) * 64 + 64)
                    k.cp(v3(dst[h2][pr, :], 4)[:, :, 0:n], v3(b[pr, :], 4)[:, :, 0:n])
                    k.memset(v3(dst[h2][po, :], 4)[:, :, 0:n], 0.0)
            else:
                k.cp(v3(dst[:, :], 4)[:, :, 0:n], v3(b[:, :], 4)[:, :, 0:n])

        def fT(t, h):
            if isinstance(t, tuple):
                t = t[h % 2]
            return v3(t[:, :], 4)[:, h // 2, 0:n]

        def m3(t):
            return v3(t[0:n, :], 4)[:, :, 0:n]

        def mk(i):
            return bc(CM[0:n, i:i + 1, 0:n], [n, 4, n])

        for hb_ in range(2):
            heads = [4 * hb_ + i for i in range(4)]
            bN, bNT, bAk, bRb, bRk = k.bank(), k.bank(), k.bank(), k.bank(), k.bank()
            for i, h in enumerate(heads):
                k.mm(m3(bN)[:, i, :], fT(kkgTm, h), fT(bgT, h))
                k.mm(m3(bNT)[:, i, :], fT(bgT, h), fT(kkgTm, h))
                k.mm(m3(bAk)[:, i, :], fT(k2gT, h), fT(kkgTm, h))
                k.mm(m3(bRb)[:, i, :], fT(bgT, h), fT(rgTm, h))
                k.mm(m3(bRk)[:, i, :], fT(k2gT, h), fT(rgTm, h))
            AkT, RbT, RkT = G[4], G[5], G[6]
            k.tt(m3(AkT), m3(bAk), mk(C_TRIUS), ALU.mult)
            k.tt(m3(RbT), m3(bRb), mk(C_TRIU), ALU.mult)
            k.tt(m3(RkT), m3(bRk), mk(C_TRIU), ALU.mult)
            if n > 1:
                Nbd, NbdT, Loff = G[7], G[8], G[9]
                k.tt(m3(Nbd), m3(bN), mk(C_NTRIL64), ALU.mult)
                k.tt(m3(Loff), m3(bN), mk(C_LOWLEFT), ALU.mult)
                k.tt(m3(NbdT), m3(bNT), mk(C_NTRIU64), ALU.mult)
                X = neumann(n, m3(Nbd), m3(NbdT), m3(Loff), G[18:24])
            else:
                X = neumann(n, None, None, None, G[18:24])
            bR = k.bank()
            for i, h in enumerate(heads):
                k.mm(bR[0:n, i * 64:(i + 1) * 64], fT(kkgTm, h), M_rw[:, h // 2, :], start=True, stop=False)
                k.mm(bR[0:n, i * 64:(i + 1) * 64], m3(AkT)[:, i, :], v_[:, h * 64:(h + 1) * 64], start=False, stop=True)
            R0 = G[24]
            k.ts(R0[0:n, 0:256], bR[0:n, 0:256], -1.0, ALU.mult)
            bU = k.bank()
            for i, h in enumerate(heads):
                k.mm(bU[0:n, i * 64:(i + 1) * 64], X[:, i, :], R0[0:n, i * 64:(i + 1) * 64])
            U = G[25]
            k.cp(U[0:n, 0:256], bU[0:n, 0:256])
            for i, h in enumerate(heads):
                o_ = PSO[0:n, h * 64:(h + 1) * 64]
                k.mm(o_, fT(rgTm, h), M_rw[:, h // 2, :], start=True, stop=False)
                k.mm(o_, m3(RbT)[:, i, :], U[0:n, i * 64:(i + 1) * 64], start=False, stop=False)
                k.mm(o_, m3(RkT)[:, i, :], v_[:, h * 64:(h + 1) * 64], start=False, stop=True)
            for j in range(2):
                hp = 2 * hb_ + j
                bM = k.bank()
                k.mm(bM[:, 0:128], bd[0:n, hp * 128:(hp + 1) * 128], U[0:n, j * 128:(j + 1) * 128], start=True, stop=False)
                k.mm(bM[:, 0:128], k2d[0:n, hp * 128:(hp + 1) * 128], v_[:, hp * 128:(hp + 1) * 128], start=False, stop=True)
                for h2 in range(2):
                    pr = slice(h2 * 64, h2 * 64 + 64)
                    k.stt(M_rw[pr, hp, :], M_rw[pr, hp, :], eGl[pr, hp:hp + 1], bM[pr, h2 * 64:(h2 + 1) * 64], ALU.mult, ALU.add)
        rw_out(n, row, None, v_, rz, bsum, pc)

    def rw_batch(l):
        n = NS
        xm = cq
        for j, c0 in enumerate(range(0, 1664, 512)):
            wd = min(512, 1664 - c0)
            st = stage[j % 2][0:n, (j // 2) * 512:(j // 2) * 512 + wd]
            k.dma(st, st_rs[l, 0:n, c0:c0 + wd])
            g = G[0]
            k.tt(g[0:n, 0:wd], st, proj[0:n, c0:c0 + wd], ALU.subtract)
            k.tt(g[0:n, 0:wd], g[0:n, 0:wd], PC[0:n, c0:c0 + wd], ALU.mult)
            k.tt(xm[0:n, c0:c0 + wd], g[0:n, 0:wd], proj[0:n, c0:c0 + wd], ALU.add)
        k.dma(s_rs[l, 0:n, :], proj[0:n, 0:1664], q="pool")
        r_, k_, v_, rz, LW, At, KK, K2, Bt, bsum, pc = rw_prep(n)
        t0, eG, rg = G[7], G[8], G[9]
        k.act(eG[0:n, :], LW[0:n, :], AF.Exp)
        k.tt(rg[0:n, :], r_, eG[0:n, :], ALU.mult)
        rb, rk2 = sm2[0:n, 16:24], sm2[0:n, 24:32]
        k.tt(t0[0:n, :], r_, Bt[0:n, :], ALU.mult)
        k.red(rb, v3(t0[0:n, :], 8))
        k.tt(t0[0:n, :], r_, K2[0:n, :], ALU.mult)
        k.red(rk2, v3(t0[0:n, :], 8))
        MLkk = ((G[10], G[11]), (G[12], G[13]))
        MLrg = ((G[14], G[15]), (G[16], G[17]))

        def mlv(t):
            return t[:, :].rearrange("p (a s m) -> p a s m", a=2, s=16)

        for src, ML in ((KK, MLkk), (rg, MLrg)):
            b = k.bank()
            for hp in range(4):
                k.tr(b[:, hp * 128:hp * 128 + n], src[0:n, hp * 128:(hp + 1) * 128], cid(n))
            for h2 in range(2):
                p0, p1 = h2 * 64, h2 * 64 + 64
                q0, q1 = (1 - h2) * 64, (1 - h2) * 64 + 64
                for hq in range(2):
                    o4 = mlv(ML[h2][hq])[p0:p1, :, 0:n, 0:n]
                    i4 = v3(b[p0:p1, :], 4)[:, 2 * hq:2 * hq + 2, 0:n]
                    k.tt(o4, bc(i4.unsqueeze(2), [64, 2, n, n]), i16(p0, p1, [64, 2, n, n], 1), ALU.mult)
                    k.memset(ML[h2][hq][q0:q1, :], 0.0)
        b = k.bank()
        for hp in range(4):
            k.tr(b[:, hp * 128:hp * 128 + n], LW[0:n, hp * 128:(hp + 1) * 128], cid(n))
        WT = G[18]
        k.act(v3(WT[:, :], 4)[:, :, 0:n], v3(b[:, :], 4)[:, :, 0:n], AF.Exp)
        MT = G[26]
        Mst = (G[19], G[20])

        def load_state(s_):
            k.dma(v3(MT[0:64, :], 8), st_rw[l, s_].rearrange("h v k -> v h k"))
            b_ = k.bank()
            for hp in range(4):
                k.tr(b_[:, hp * 64:(hp + 1) * 64], MT[0:64, hp * 128:(hp + 1) * 128], cid(64))
            Ms = v3(Mst[s_ % 2][:, 0:256], 4)
            k.cp(Ms, v3(b_[:, 0:256], 4))
            return Ms

        k.nrot = 6
        bKM, bRM = k.banks[6], k.banks[7]
        k.memset(bKM[0:n, :], 0.0)
        k.memset(bRM[0:n, :], 0.0)
        for s_ in range(n):
            Ms = load_state(s_)
            for h in range(8):
                hp, h2 = h // 2, h % 2
                for ML, bk in ((MLkk, bKM), (MLrg, bRM)):
                    lt = mlv(ML[h2][hp // 2])[:, hp % 2, s_, 0:n]
                    k.mm(bk[0:n, h * 64:(h + 1) * 64], lt, Ms[:, hp, :], start=False, stop=True, skip=True)
        U, o_, t1 = G[21], G[22], G[23]
        k.ts(U[0:n, :], bKM[0:n, :], -1.0, ALU.mult)
        k.tt(v3(o_[0:n, :], 8), v3(U[0:n, :], 8), bc(rb.unsqueeze(2), [n, 8, 64]), ALU.mult)
        k.tt(v3(t1[0:n, :], 8), v3(v_, 8), bc(rk2.unsqueeze(2), [n, 8, 64]), ALU.mult)
        k.tt(o_[0:n, :], o_[0:n, :], t1[0:n, :], ALU.add)
        k.tt(o_[0:n, :], o_[0:n, :], bRM[0:n, :], ALU.add)
        k.nrot = 7
        um, vm, Mo = G[23], G[24], G[25]
        for s_ in range(n):
            Ms = load_state(s_)
            k.ts(um[0:n, :], U[0:n, :], CM[0:n, C_ID, s_:s_ + 1], ALU.mult)
            k.ts(vm[0:n, :], v_, CM[0:n, C_ID, s_:s_ + 1], ALU.mult)
            b = k.bank()
            for hp in range(4):
                cs = slice(hp * 128, (hp + 1) * 128)
                k.mm(b[:, cs], Bt[0:n, cs], um[0:n, cs], start=True, stop=False)
                k.mm(b[:, cs], K2[0:n, cs], vm[0:n, cs], start=False, stop=True)
            Mo3 = v3(Mo[:, 0:256], 4)
            for hp in range(4):
                for h2 in range(2):
                    pr = slice(h2 * 64, h2 * 64 + 64)
                    k.stt(Mo3[pr, hp, :], Ms[pr, hp, :], v3(WT[:, :], 4)[pr, hp, s_:s_ + 1],
                          b[pr, hp * 128 + h2 * 64: hp * 128 + (h2 + 1) * 64], ALU.mult, ALU.add)
            b2 = k.bank()
            for hp in range(4):
                k.tr(b2[0:64, hp * 128:(hp + 1) * 128], Mo3[:, hp, :], cid(128))
            so = stage[s_ % 2][0:64, 0:512]
            k.cp(so, b2[0:64, :])
            k.dma(s_rw[l, s_].rearrange("h v k -> v h k"), v3(so, 8), q="pool")
        rw_out(n, SEQ, o_, v_, rz, bsum, pc)

    def rw_out(n, row, src, v_, rz, bsum, pc):
        run_pre_tail()
        on, xc, sq, zs = G[0], G[25], G[2], G[3]
        if src is None:
            k.cp(on[0:n, :], PSO[0:n, :])
        else:
            on = src
        k.red(sm[0:n, 96:104], v3(on[0:n, :], 8))
        k.ts(sm[0:n, 96:104], sm[0:n, 96:104], 1.0 / 64.0, ALU.mult)
        k.tt(v3(xc[0:n, :], 8), v3(on[0:n, :], 8), bc(sm[0:n, 96:104].unsqueeze(2), [n, 8, 64]), ALU.subtract)
        k.tt(sq[0:n, :], xc[0:n, :], xc[0:n, :], ALU.mult)
        k.red(sm[0:n, 104:112], v3(sq[0:n, :], 8))
        k.act(sm[0:n, 112:120], sm[0:n, 104:112], AF.Ln, bias=64e-5, scale=1.0 / 64.0)
        k.act(sm[0:n, 120:128], sm[0:n, 112:120], AF.Exp, scale=-0.5)
        k.tt(v3(xc[0:n, :], 8), v3(xc[0:n, :], 8), bc(sm[0:n, 120:128].unsqueeze(2), [n, 8, 64]), ALU.mult)
        k.tt(xc[0:n, :], xc[0:n, :], pc("lw"), ALU.mult)
        k.tt(xc[0:n, :], xc[0:n, :], pc("lb"), ALU.add)
        k.tt(v3(sq[0:n, :], 8), v3(v_, 8), bc(bsum.unsqueeze(2), [n, 8, 64]), ALU.mult)
        k.tt(xc[0:n, :], xc[0:n, :], sq[0:n, :], ALU.add)
        k.act(zs[0:n, :], rz, AF.Silu)
        k.tt(xc[0:n, :], xc[0:n, :], zs[0:n, :], ALU.mult)
        k.dma(omix[row:row + n, 1024:1536], xc[0:n, :], q="pool")
        run_post_tail()

    def out_phase(l):
        if k.preloaded != (l, "out"):
            load_weights(l, 0, D, 0, rows_kc=12, src=w_out)
        last_layer = (l == LAYERS - 1)
        if last_layer:
            bload(PC[:, 0:D], final_norm_g[0:1, :], D)
        extract_mod("prompt", 128, 2 * D, D)
        for si, seg in enumerate(psegs + [sbatch]):
            kind, idx, n, row = seg
            xt = XB[si % 2]
            seg_mod(seg, 2 * D, D)
            k.dma(xt[0:n, :], xsrc(l, seg))
            k.dma(stage[0][0:n, 0:1024], omix[row:row + n, 0:1024])
            k.dma(stage[1][0:n, 0:512], omix[row:row + n, 1024:1536])
            k.cp(ob[0:n, 0:1024], stage[0][0:n, 0:1024], eng="act")
            k.cp(ob[0:n, 1024:1536], stage[1][0:n, 0:512], eng="dve")
            for grp, (k0, k1) in enumerate(((0, 8), (8, 12))):
                b = k.bank()
                bb = b[:, :].bitcast(BF16)
                for kc in range(k0, k1):
                    k.tr(bb[:, (kc - k0) * 128:(kc - k0) * 128 + n], ob[0:n, kc * 128:(kc + 1) * 128], identb[0:n, 0:n])
                k.cp(oT[:, k0:k1, 0:n], v3(bb, 8)[:, 0:k1 - k0, 0:n])
            for c in range(2):
                b = k.bank()
                for kc in range(12):
                    k.mm(b[0:n, :], oT[:, kc, 0:n], W[:, kc * D + c * 512: kc * D + (c + 1) * 512],
                         start=(kc == 0), stop=(kc == 11), rkey=f"W{kc}")
                g = G[c]
                k.tt(g[0:n, :], b[0:n, :], modseg[0:n, c * 512:(c + 1) * 512], ALU.mult)
                k.tt(xt[0:n, c * 512:(c + 1) * 512], xt[0:n, c * 512:(c + 1) * 512], g[0:n, :], ALU.add)
            if not last_layer:
                k.dma(xmid[row:row + n, :], xt[0:n, :], q="pool")
            else:
                yt = (cq, PB[0])[si % 2][0:n, 0:D]
                k.memset(sm[0:n, 0:1], 0.0)
                k.e("act", "activation", out=yt, in_=xt[0:n, :], func=AF.Square, accum_out=sm[0:n, 0:1])
                k.act(sm[0:n, 1:2], sm[0:n, 0:1], AF.Ln, bias=EPS, scale=1.0 / D)
                k.act(sm[0:n, 2:3], sm[0:n, 1:2], AF.Exp, scale=-0.5)
                k.stt(yt, xt[0:n, :], sm[0:n, 2:3], PC[0:n, 0:D], ALU.mult, ALU.mult)
                dst = y_p[idx * 128:(idx + 1) * 128, :] if kind == "p" else y_s[0:NS, :]
                k.dma(dst, yt, q="pool")

    for l in range(LAYERS):
        layer_mod(l)
        if "dn" in PHASES:
            dn_phase(l)
        if "gla" in PHASES:
            gla_phase(l)
        if "rw" in PHASES:
            rw_phase(l)
        if "out" in PHASES:
            out_phase(l)
    k.S.finish()
    return nc, k


_WNAMES = ["norm_g", "ada_w", "ada_b", "w_in", "dn_conv_w", "dn_a_log", "dn_dt_bias", "dn_norm_g", "gla_wf",
           "gla_bf", "gla_norm_g", "rw_mu", "rw_w0", "rw_w2", "rw_a0", "rw_a2", "rw_k_k", "rw_k_a", "rw_r_k",
           "rw_ln_w", "rw_ln_b", "w_out", "final_norm_g"]


def make_in_maps(inputs, SEQ, NS=NSAMP):
    f = lambda a: np.ascontiguousarray(np.asarray(a, dtype=np.float32))
    shared = {nm: f(inputs[nm]) for nm in _WNAMES}
    shared["rw_r_k"] = shared["rw_r_k"].reshape(2, 512)
    shared["final_norm_g"] = shared["final_norm_g"].reshape(1, D)
    shared["cm"] = make_consts(NS)
    maps = []
    for c in range(NCORES):
        sl = slice(c * NS, (c + 1) * NS)
        m = dict(shared)
        m["xp"] = f(inputs["x_prompt"][c, :SEQ])
        m["xs"] = f(inputs["x_sample"][sl, 0])
        m["call"] = f(np.concatenate([inputs["c_sample"][sl], inputs["c_prompt"][c:c + 1]], axis=0))
        m["st_conv"] = f(inputs["state_dn_conv"][:, sl])
        m["st_dn"] = f(inputs["state_dn"][:, sl])
        m["st_gla"] = f(inputs["state_gla"][:, sl])
        m["st_rs"] = f(inputs["state_rwkv_shift"][:, sl])
        m["st_rw"] = f(inputs["state_rwkv"][:, sl])
        maps.append(m)
    return maps


def gather(res, SEQ):
    R = res.results
    cat = lambda nm, ax: np.concatenate([np.asarray(r[nm]) for r in R], axis=ax)
    stk = lambda nm: np.stack([np.asarray(r[nm]) for r in R], axis=1)
    y_p = np.stack([np.asarray(r["y_p"]) for r in R], axis=0)
    y_s = cat("y_s", 0)[:, None, :]
    return (y_p.astype(np.float32), y_s.astype(np.float32), stk("p_conv"), stk("p_dn"), stk("p_gla"), stk("p_rs"),
            stk("p_rw"), cat("s_conv", 1), cat("s_dn", 1), cat("s_gla", 1), cat("s_rs", 1), cat("s_rw", 1))


_CACHE = {}


def kernel(**inputs):
    SEQ = int(np.asarray(inputs["x_prompt"]).shape[1])
    if SEQ not in _CACHE:
        _CACHE[SEQ] = build(SEQ=SEQ)[0]
    nc = _CACHE[SEQ]
    maps = make_in_maps(inputs, SEQ)
    res = run_bass_kernel_spmd(nc, maps, core_ids=list(range(NCORES)))
    return gather(res, SEQ)
```

```python
import numpy as np
import concourse.bass as bass
import concourse.mybir as mybir
from concourse.bass_utils import run_bass_kernel_spmd

F32 = mybir.dt.float32
BF16 = mybir.dt.bfloat16
AF = mybir.ActivationFunctionType
ALU = mybir.AluOpType
AX = mybir.AxisListType

D = 1024
NSAMP = 16
NCORES = 8
EPS = 1e-6
ENGS = ("pe", "dve", "act", "pool", "sp")
N_DMA_SEMS = 32
SEM_LIMIT = 30000
USE_R = True
BATCH_S = True
SAME_ENGINE_SYNC = True

(C_ID, C_ONES, C_TRIU, C_SH1, C_SH2, C_SH3, C_BD1, C_BD2, C_BD3, C_SELB128, C_A1, C_SELB1, C_E0, C_SELP,
 C_TRIUS, C_NTRIL64, C_NTRIU64, C_LOWLEFT, C_MLOW, C_MUP) = range(20)
C_I16 = 20
NCM = 22
NCMR = 14
F32R = mybir.dt.float32r


def make_consts(NS=NSAMP):
    cm = np.zeros((128, NCM, 128), np.float32)
    i = np.arange(128)[:, None]
    j = np.arange(128)[None, :]
    same = (i // 64) == (j // 64)
    cm[:, C_ID] = (i == j)
    cm[:, C_ONES] = 1.0
    cm[:, C_TRIU] = (i <= j)
    cm[:, C_TRIUS] = (i < j)
    cm[:, C_NTRIL64] = -1.0 * ((i > j) & same)
    cm[:, C_NTRIU64] = -1.0 * ((i < j) & same)
    cm[:, C_LOWLEFT] = ((i >= 64) & (j < 64))
    cm[:, C_MLOW] = np.where(i >= j, 0.0, -1e30)
    cm[:, C_MUP] = np.where(j >= i, 0.0, -1e30)
    cm[:, C_SH1] = (i == j - 1)
    cm[:, C_SH2] = (i == j - 2)
    cm[:, C_SH3] = (i == j - 3)
    for d, c in ((1, C_BD1), (2, C_BD2), (3, C_BD3)):
        cm[:, c] = ((i == 3 + j - d) & (i < 3) & (j < d))
    cm[:, C_SELB128] = ((i == 125 + j) & (j < 3))
    cm[:, C_A1] = ((i == j + 1) & (i < 3) & (j < 2))
    cm[0, C_SELB1, 2] = 1.0
    cm[0, C_E0, 0] = 1.0
    cm[NS, C_SELP, :] = 1.0
    cm[:, C_I16:C_I16 + 2, :] = np.eye(16, dtype=np.float32).reshape(1, 2, 128)
    return cm


class Buf:
    __slots__ = ("last_w", "reads", "excl")

    def __init__(self, excl):
        self.last_w = None
        self.reads = []
        self.excl = excl


class Sched:
    def __init__(self, nc):
        self.nc = nc
        self.streams = {e: [] for e in ENGS}
        self.gen = {e: 0 for e in ENGS}
        self.sems = {}
        for e in ENGS:
            self.sems[(e, 0)] = nc.alloc_semaphore(name=f"s_{e}0")
        self.cnt = {e: 0 for e in ENGS}
        self.dsems = [nc.alloc_semaphore(name=f"s_dma{i}") for i in range(N_DMA_SEMS)]
        self.dcnt = [0] * N_DMA_SEMS
        self.dnext = 0
        self.dnext2 = 0
        self.dnext3 = 0
        self.seen = {e: {} for e in ENGS}
        self.bufs = {}
        self.n_instr = 0
        self.n_wait = 0

    def buf(self, key):
        b = self.bufs.get(key)
        if b is None:
            b = self.bufs[key] = Buf(key.startswith("ps"))
        return b

    def _sem(self, key):
        return self.dsems[key] if isinstance(key, int) else self.sems[key]

    def _deps(self, eng, rb, wb):
        need = {}

        def add(ev):
            if ev is None:
                return
            k, v = ev
            if not isinstance(k, int) and k[0] == eng and (eng == "pe" or not SAME_ENGINE_SYNC):
                return
            if need.get(k, 0) < v:
                need[k] = v

        for b in rb:
            add(b.last_w)
            if b.excl:
                for r in b.reads:
                    add(r)
        for b in wb:
            add(b.last_w)
            for r in b.reads:
                add(r)
        out = []
        seen = self.seen[eng]
        for k, v in need.items():
            if seen.get(k, 0) < v:
                seen[k] = v
                out.append((self._sem(k), v))
        return out

    def _commit(self, ev, rb, wb):
        for b in rb:
            if b.excl:
                b.last_w = ev
                b.reads = []
            else:
                b.reads.append(ev)
                if len(b.reads) > 48:
                    best = {}
                    for k, v in b.reads:
                        if best.get(k, 0) < v:
                            best[k] = v
                    b.reads = list(best.items())
        for b in wb:
            b.last_w = ev
            b.reads = []

    def op(self, eng, fn, reads, writes, desc=None):
        if not hasattr(self, "meta"):
            self.meta = {e: [] for e in ENGS}
        rb = [self.buf(k) for k in reads]
        wb = [self.buf(k) for k in writes]
        wl = self._deps(eng, rb, wb)
        if self.cnt[eng] >= SEM_LIMIT:
            self.gen[eng] += 1
            self.cnt[eng] = 0
            self.sems[(eng, self.gen[eng])] = self.nc.alloc_semaphore(name=f"s_{eng}{self.gen[eng]}")
        self.cnt[eng] += 1
        key = (eng, self.gen[eng])
        ev = (key, self.cnt[eng])
        sem = self.sems[key]
        self.n_wait += len(wl)
        self.n_instr += 1
        self.meta[eng].append((len(wl), desc))

        def emit(e, fn=fn, wl=wl, sem=sem):
            for s, v in wl:
                e.wait_ge(s, v)
            fn(e).then_inc(sem, 1)

        self.streams[eng].append(emit)
        self._commit(ev, rb, wb)

    def dma(self, q, out, in_, reads, writes):
        rb = [self.buf(k) for k in reads]
        wb = [self.buf(k) for k in writes]
        wl = self._deps(q, rb, wb)
        if q == "sp":
            j = self.dnext
            self.dnext = (self.dnext + 1) % 16
        elif q == "pool":
            j = 16 + self.dnext2
            self.dnext2 = (self.dnext2 + 1) % 8
        else:
            j = 24 + self.dnext3
            self.dnext3 = (self.dnext3 + 1) % 8
        prev = self.dcnt[j]
        if prev > 0 and self.seen[q].get(j, 0) < prev:
            self.seen[q][j] = prev
            wl.append((self.dsems[j], prev))
        self.dcnt[j] += 16
        ev = (j, self.dcnt[j])
        dsem = self.dsems[j]
        self.n_wait += len(wl)
        self.n_instr += 1

        def emit(e, wl=wl, dsem=dsem, out=out, in_=in_):
            for s, v in wl:
                e.wait_ge(s, v)
            e.dma_start(out=out, in_=in_).then_inc(dsem, 16)

        self.streams[q].append(emit)
        self._commit(ev, rb, wb)

    def finish(self):
        wl = [(self.dsems[j], self.dcnt[j]) for j in range(N_DMA_SEMS) if self.dcnt[j] > 0]
        for e in ENGS:
            if e not in ("sp",) and self.cnt[e] > 0:
                wl.append((self.sems[(e, self.gen[e])], self.cnt[e]))

        def emit(e, wl=wl):
            for s, v in wl:
                e.wait_ge(s, v)

        self.streams["sp"].append(emit)
        streams = self.streams
        with self.nc.Block() as block:
            @block.tensor
            def _(e):
                for f in streams["pe"]:
                    f(e)

            @block.vector
            def _(e):
                for f in streams["dve"]:
                    f(e)

            @block.scalar
            def _(e):
                for f in streams["act"]:
                    f(e)

            @block.gpsimd
            def _(e):
                for f in streams["pool"]:
                    f(e)

            @block.sync
            def _(e):
                for f in streams["sp"]:
                    f(e)


def _isap(v):
    return hasattr(v, "tensor") and hasattr(v, "ap")


class K:
    def __init__(self, nc):
        self.nc = nc
        self.S = Sched(nc)
        self.banks = [nc.alloc_psum_tensor(f"ps{i}", [128, 512], F32) for i in range(8)]
        self.nb = 0
        self.nrot = 7
        self.ncp = 0
        self.R = set()

    def bank(self):
        b = self.banks[self.nb % self.nrot]
        self.nb += 1
        return b

    def e(self, eng, method, _keys=None, **kw):
        reads, writes = [], []
        for name, val in list(kw.items()):
            if _isap(val):
                if _keys and name in _keys:
                    kk = _keys[name]
                    (writes if name in ("out", "accum_out", "ap") else reads).extend(kk if isinstance(kk, list) else [kk])
                elif name in ("out", "accum_out", "ap"):
                    writes.append(val.name)
                    if val.name in self.R and val.dtype == F32 and name != "ap":
                        kw[name] = val.bitcast(F32R)
                else:
                    reads.append(val.name)
        desc = method + " " + " ".join(f"{n_}={tuple(v_.shape)}:{v_.dtype}@{v_.name}" for n_, v_ in kw.items() if _isap(v_))
        self.S.op(eng, lambda e: getattr(e, method)(**kw), reads, writes, desc)

    def dma(self, out, in_, q="sp", wkey=None):
        self.S.dma(q, out, in_, [in_.name], [wkey if wkey is not None else out.name])

    def mm(self, out, lhsT, rhs, start=True, stop=True, skip=False, rkey=None):
        if rkey is not None:
            self.e("pe", "matmul", _keys={"rhs": rkey}, out=out, lhsT=lhsT, rhs=rhs, start=start, stop=stop)
            return
        if (lhsT.name in self.R and rhs.name in self.R and lhsT.dtype == F32 and rhs.dtype == F32
                and lhsT.shape[0] >= 2 and lhsT.shape[-1] >= 2 and rhs.shape[-1] >= 2):
            lhsT = lhsT.bitcast(F32R)
            rhs = rhs.bitcast(F32R)
        if skip:
            self.e("pe", "matmul", out=out, lhsT=lhsT, rhs=rhs, start=start, stop=stop, skip_group_check=True)
        else:
            self.e("pe", "matmul", out=out, lhsT=lhsT, rhs=rhs, start=start, stop=stop)

    def tr(self, out, in_, ident):
        self.e("pe", "transpose", out=out, in_=in_, identity=ident)

    def tt(self, out, a, b, op, eng="dve"):
        self.e(eng, "tensor_tensor", out=out, in0=a, in1=b, op=op)

    def ts(self, out, a, s1, op0, s2=None, op1=None, eng="dve"):
        if op1 is None:
            self.e(eng, "tensor_scalar", out=out, in0=a, scalar1=s1, scalar2=None, op0=op0)
        else:
            self.e(eng, "tensor_scalar", out=out, in0=a, scalar1=s1, scalar2=s2, op0=op0, op1=op1)

    def stt(self, out, a, s, b, op0, op1, eng="dve"):
        self.e(eng, "scalar_tensor_tensor", out=out, in0=a, scalar=s, in1=b, op0=op0, op1=op1)

    def red(self, out, in_, op=ALU.add):
        self.e("dve", "tensor_reduce", out=out, in_=in_, axis=AX.X, op=op)

    def recip(self, out, in_):
        self.e("dve", "reciprocal", out=out, in_=in_)

    def act(self, out, in_, func, bias=None, scale=None):
        kw = dict(out=out, in_=in_, func=func)
        if bias is not None:
            kw["bias"] = bias
        if scale is not None:
            kw["scale"] = scale
        self.e("act", "activation", **kw)

    def cp(self, out, in_, eng=None, okey=None, ikey=None):
        if eng is None:
            eng = ("dve", "act")[self.ncp % 2]
            self.ncp += 1
        kk = {"out": okey} if okey is not None else None
        if ikey is not None:
            kk = dict(kk or {}, in_=ikey)
        if eng == "act":
            self.e("act", "copy", _keys=kk, out=out, in_=in_)
        else:
            self.e(eng, "tensor_copy", _keys=kk, out=out, in_=in_)

    def memset(self, ap, val, eng="dve"):
        self.e(eng, "memset", ap=ap, constant=val)


class Ref:
    def __init__(self, t):
        self.t = t

    def __getitem__(self, key):
        return self.t[key]


def v3(ap, a):
    return ap.rearrange("p (a b) -> p a b", a=a)


def bc(ap, shape):
    return ap.to_broadcast(list(shape))


DN_C0, DN_W = 0, 2056
GLA_C0, GLA_W = 2056, 1552
RW_C0, RW_W = 3608, 2176
NG = 27


def build(SEQ=2048, NS=NSAMP, LAYERS=2, PHASES=("dn", "gla", "rw", "out")):
    nc = bass.Bass("TRN2", target_bir_lowering=False)
    NT = SEQ // 128
    NTOK = SEQ + NS

    def din(name, shape):
        return nc.dram_tensor(name, list(shape), F32, kind="ExternalInput").ap()

    def dout(name, shape):
        return nc.dram_tensor(name, list(shape), F32, kind="ExternalOutput").ap()

    xp = din("xp", [SEQ, D]); xs = din("xs", [NS, D]); call = din("call", [NS + 1, D])
    st_conv = din("st_conv", [2, NS, 3, 1536]); st_dn = din("st_dn", [2, NS, 4, 128, 128])
    st_gla = din("st_gla", [2, NS, 4, 64, 128]); st_rs = din("st_rs", [2, NS, 1664])
    st_rw = din("st_rw", [2, NS, 8, 64, 64])
    cmd = din("cm", [128, NCM, 128])
    norm_g = din("norm_g", [2, D]); ada_w = din("ada_w", [2, D, 3 * D]); ada_b = din("ada_b", [2, 3 * D])
    w_in = din("w_in", [2, D, 5784]); dn_conv_w = din("dn_conv_w", [2, 4, 1536])
    dn_a_log = din("dn_a_log", [2, 4]); dn_dt_bias = din("dn_dt_bias", [2, 4]); dn_norm_g = din("dn_norm_g", [2, 128])
    gla_wf = din("gla_wf", [2, 16, 256]); gla_bf = din("gla_bf", [2, 256]); gla_norm_g = din("gla_norm_g", [2, 128])
    rw_mu = din("rw_mu", [2, 1664]); rw_w0 = din("rw_w0", [2, 512]); rw_w2 = din("rw_w2", [2, 64, 512])
    rw_a0 = din("rw_a0", [2, 512]); rw_a2 = din("rw_a2", [2, 64, 512]); rw_k_k = din("rw_k_k", [2, 512])
    rw_k_a = din("rw_k_a", [2, 512]); rw_r_k = din("rw_r_k", [2, 512]); rw_ln_w = din("rw_ln_w", [2, 512])
    rw_ln_b = din("rw_ln_b", [2, 512]); w_out = din("w_out", [2, 1536, D]); final_norm_g = din("final_norm_g", [1, D])

    y_p = dout("y_p", [SEQ, D]); y_s = dout("y_s", [NS, D])
    p_conv = dout("p_conv", [2, 3, 1536]); p_dn = dout("p_dn", [2, 4, 128, 128]); p_gla = dout("p_gla", [2, 4, 64, 128])
    p_rs = dout("p_rs", [2, 1664]); p_rw = dout("p_rw", [2, 8, 64, 64])
    s_conv = dout("s_conv", [2, NS, 3, 1536]); s_dn = dout("s_dn", [2, NS, 4, 128, 128])
    s_gla = dout("s_gla", [2, NS, 4, 64, 128]); s_rs = dout("s_rs", [2, NS, 1664]); s_rw = dout("s_rw", [2, NS, 8, 64, 64])

    xmid = nc.dram_tensor("xmid", [NTOK, D], F32).ap()
    omix = nc.dram_tensor("omix", [NTOK, 1536], F32).ap()
    projs = nc.dram_tensor("projs", [NS, RW_W], F32).ap()
    modall = nc.dram_tensor("modall", [NS + 1, 3 * D], F32).ap()
    hts = nc.dram_tensor("hts", [NT + 1, 128, 8 * 128], BF16).ap()

    k = K(nc)
    sb = nc.alloc_sbuf_tensor

    CM = sb("CM", [128, NCM, 128], F32)
    identb = sb("identb", [128, 128], BF16)
    W = sb("W", [128, 8 * RW_W], BF16)
    stage = [sb(f"stage{i}", [128, 1088], F32) for i in range(2)]
    modseg = sb("modseg", [128, 2 * D], F32)
    PC = sb("PC", [128, 6912], F32)
    x_t = sb("x_t", [128, D], F32)
    hb = sb("hb", [128, D], BF16)
    HB = [sb("hT", [128, 8, 128], BF16), sb("hT1", [128, 8, 128], BF16)]
    PB = [sb("proj", [128, RW_W], F32), sb("proj1", [128, RW_W], F32)]
    hT = Ref(HB[0])
    proj = Ref(PB[0])
    k.pre_tail = None
    k.post_tail = None
    k.next_load = None
    k.preloaded = None
    cq = sb("cq", [128, 1664], F32)
    cT = sb("cT", [128, 8, NS + 1], F32)
    TLS = sb("TLS", [3, 1664], F32)
    TL = TLS[0:3, 0:1536]
    srow = TLS[0:1, 0:1664]
    S_dn = sb("S_dn", [128, 4, 128], F32)
    S_gla = sb("S_gla", [128, 2, 128], F32)
    M_rw = sb("M_rw", [128, 4, 64], F32)
    sm = sb("sm", [128, 128], F32)
    sm2 = sb("sm2", [128, 64], F32)
    wsm = sb("wsm", [64, 1024], F32)
    ob = modseg[:, 1024:1792].bitcast(BF16)
    XB = [x_t, sb("x_t2", [128, D], F32)]
    oT = sb("oT", [128, 12, 128], BF16)
    G = [sb(f"g{i}", [128, 512], F32) for i in range(NG)]
    PSO = k.banks[7]

    CMR = sb("CMR", [128, NCMR, 128], F32)
    if USE_R:
        k.R.update(["CMR", "cq", "proj", "proj1", "S_dn", "S_gla", "M_rw"] + [f"g{i}" for i in range(26)])

    def cmat(i, r, c):
        return CMR[0:r, i, 0:c] if USE_R else CM[0:r, i, 0:c]

    def cid(n):
        return CM[0:n, C_ID, 0:n]

    k.dma(CM[:], cmd)
    k.cp(identb[:], CM[:, C_ID, :], eng="dve")
    k.cp(CMR[:], CM[:, 0:NCMR, :], eng="dve")
    k.dma(x_t[0:NS + 1, :], call)
    k.act(x_t[0:NS + 1, :], x_t[0:NS + 1, :], AF.Silu)
    b_ = k.bank()
    for kc in range(8):
        k.tr(b_[:, kc * 32:kc * 32 + NS + 1], x_t[0:NS + 1, kc * 128:(kc + 1) * 128], cid(NS + 1))
    k.cp(cT[:], v3(b_[:, 0:256], 8)[:, :, 0:NS + 1], eng="dve")

    psegs = [("p", t, 128, t * 128) for t in range(NT)]
    ssegs = [("s", s, 1, SEQ + s) for s in range(NS)]
    sbatch = ("sb", 0, NS, SEQ)

    def xsrc(l, seg):
        kind, idx, n, row = seg
        if l == 0:
            return xp[idx * 128:(idx + 1) * 128, :] if kind == "p" else xs[idx:idx + n, :]
        return xmid[row:row + n, :]

    def front_into(l, seg, wcols, cache, slot, defer=False):
        save = (proj.t, hT.t)
        proj.t, hT.t = PB[slot], HB[slot]
        r = front(l, seg, wcols, cache, defer)
        proj.t, hT.t = save
        return r

    def run_post_tail():
        f = k.post_tail
        k.post_tail = None
        if f is not None:
            f()

    def run_pre_tail():
        f = k.pre_tail
        k.pre_tail = None
        if f is not None:
            k.post_tail = f()

    def mixer_pass(l, wcols, pre, core, post, batch_core=None):
        cache = None
        if "dn" in PHASES:
            cache = "store" if wcols == DN_W else "load"
        front_into(l, psegs[0], wcols, cache, 0)
        for t, seg in enumerate(psegs):
            proj.t, hT.t = PB[t % 2], HB[t % 2]
            nxt = psegs[t + 1] if t + 1 < NT else sbatch
            k.pre_tail = (lambda nxt=nxt, t=t: front_into(l, nxt, wcols, cache, (t + 1) % 2, defer=True))
            core(l, seg)
            run_pre_tail()
            run_post_tail()
            post(seg)
        proj.t, hT.t = PB[NT % 2], HB[NT % 2]
        if k.next_load is not None:
            k.next_load()
            k.next_load = None
        batch_core(l)

    def bload(dst, src_row, width):
        k.dma(dst, bc(src_row, [128, width]))

    def load_weights(l, c0, width, dst_off, rows_kc=8, src=None):
        src = w_in if src is None else src
        k.e("dve", "memset", _keys={"ap": [f"W{j}" for j in range(12)]}, ap=W[:, 0:2], constant=0.0)
        for kc in range(rows_kc):
            half = -(-width // 2)
            for a in (0, half):
                wd = min(half, width - a)
                dst = W[:, dst_off + kc * width + a: dst_off + kc * width + a + wd]
                k.dma(dst, src[l, kc * 128:(kc + 1) * 128, c0 + a:c0 + a + wd], q="pool", wkey=f"W{kc}")

    def extract_mod(lhsT, n, c0, ncols):
        if lhsT == "prompt":
            k.dma(modseg[0:n, 0:ncols], bc(modall[NS:NS + 1, c0:c0 + ncols], [n, ncols]))
        else:
            k.dma(modseg[0:n, 0:ncols], modall[0:n, c0:c0 + ncols])

    def seg_mod(seg, c0, ncols):
        kind, idx, n, row = seg
        if kind == "sb":
            extract_mod("batch", NS, c0, ncols)

    def front(l, seg, wcols, cache=None, defer=False):
        kind, idx, n, row = seg
        slot = hts[idx if kind == "p" else NT].rearrange("p (a b) -> p a b", a=8)[:, :, 0:n]
        if cache == "load":
            k.dma(hT[:, :, 0:n], slot)
        else:
            front_h(l, seg)
            if cache == "store":
                k.dma(slot, hT[:, :, 0:n], q="pool")
        c = 0
        evacs = []
        pt = proj.t
        while c < wcols:
            wd = min(512, wcols - c)
            b = k.bank()
            for kc in range(8):
                k.mm(b[0:n, 0:wd], hT[:, kc, 0:n], W[:, kc * wcols + c: kc * wcols + c + wd],
                     start=(kc == 0), stop=(kc == 7), rkey=f"W{kc}")
            evacs.append((pt[0:n, c:c + wd], b[0:n, 0:wd]))
            c += wd

        def do_evacs():
            for o_, i_ in evacs:
                k.cp(o_, i_)

        if defer:
            return do_evacs
        do_evacs()
        return None

    def front_h(l, seg):
        kind, idx, n, row = seg
        seg_mod(seg, 0, 2 * D)
        k.dma(x_t[0:n, :], xsrc(l, seg))
        hf = cq[0:n, 0:D]
        k.memset(sm[0:n, 0:1], 0.0)
        k.e("act", "activation", out=hf, in_=x_t[0:n, :], func=AF.Square, accum_out=sm[0:n, 0:1])
        k.act(sm[0:n, 1:2], sm[0:n, 0:1], AF.Ln, bias=EPS, scale=1.0 / D)
        k.act(sm[0:n, 2:3], sm[0:n, 1:2], AF.Exp, scale=-0.5)
        k.stt(hf, x_t[0:n, :], sm[0:n, 2:3], modseg[0:n, D:2 * D], ALU.mult, ALU.mult)
        k.tt(hb[0:n, :], hf, modseg[0:n, 0:D], ALU.add)
        b = k.bank()
        bb = b[:, :].bitcast(BF16)
        for kc in range(8):
            k.tr(bb[:, kc * 128:kc * 128 + n], hb[0:n, kc * 128:(kc + 1) * 128], identb[0:n, 0:n])
        k.cp(hT[:, :, 0:n], v3(bb, 8)[:, :, 0:n], eng="dve")

    def rms_gate_store(seg, nh, hd, gain, gate_in, col0, eps, src=None, otile=None, zs_pre=None):
        kind, idx, n, row = seg
        run_pre_tail()
        o3 = v3(PSO[0:n, :], nh)
        sq, on, zs = G[0], (G[1] if otile is None else otile), G[2]
        if src is None:
            k.cp(on[0:n, :], PSO[0:n, :])
        else:
            on = src
        k.tt(sq[0:n, :], on[0:n, :], on[0:n, :], ALU.mult)
        k.red(sm[0:n, 16:16 + nh], v3(sq[0:n, :], nh))
        k.act(sm[0:n, 24:24 + nh], sm[0:n, 16:16 + nh], AF.Ln, bias=eps, scale=1.0 / hd)
        k.act(sm[0:n, 32:32 + nh], sm[0:n, 24:24 + nh], AF.Exp, scale=-0.5)
        k.tt(v3(on[0:n, :], nh), v3(on[0:n, :], nh), bc(sm[0:n, 32:32 + nh].unsqueeze(2), [n, nh, hd]), ALU.mult)
        k.tt(v3(on[0:n, :], nh), v3(on[0:n, :], nh), bc(gain.unsqueeze(1), [n, nh, hd]), ALU.mult)
        if zs_pre is None:
            k.act(zs[0:n, :], gate_in, AF.Silu)
        else:
            zs = zs_pre
        k.tt(on[0:n, :], on[0:n, :], zs[0:n, :], ALU.mult)
        k.dma(omix[row:row + n, col0:col0 + 512], on[0:n, :], q="pool")
        run_post_tail()

    def neumann(n, Nbd, NbdT, Loff, tiles):
        idb = bc(CM[0:n, C_ID:C_ID + 1, 0:n], [n, 4, n])
        Xa, Xb, Pa, Pb, Qa, Qb = tiles

        def t3(t):
            return v3(t[0:n, :], 4)[:, :, 0:n]

        if n == 1:
            k.memset(t3(Xa), 1.0)
            return t3(Xa)
        X = t3(Xa)
        k.tt(X, NbdT, idb, ALU.add)
        P, Pt = Nbd, NbdT
        spare = [Xb, Pa, Pb, Qa, Qb]
        cur = {"X": Xa}
        pp = [Pa, Pb]
        qq = [Qa, Qb]
        xx = [Xb, Xa]
        for lvl in range(1, 6):
            bP = k.bank()
            for h in range(4):
                k.mm(v3(bP[0:n, :], 4)[:, h, 0:n], Pt[:, h, :], P[:, h, :])
            Pn = t3(pp[lvl % 2])
            k.cp(Pn, v3(bP[0:n, :], 4)[:, :, 0:n])
            if lvl < 5:
                bQ = k.bank()
                for h in range(4):
                    k.mm(v3(bQ[0:n, :], 4)[:, h, 0:n], P[:, h, :], Pt[:, h, :])
                Qn = t3(qq[lvl % 2])
                k.cp(Qn, v3(bQ[0:n, :], 4)[:, :, 0:n])
            bX = k.bank()
            for h in range(4):
                k.mm(v3(bX[0:n, :], 4)[:, h, 0:n], Pn[:, h, :], X[:, h, :])
            Xn = t3(xx[(lvl - 1) % 2])
            k.tt(Xn, X, v3(bX[0:n, :], 4)[:, :, 0:n], ALU.add)
            X = Xn
            P = Pn
            if lvl < 5:
                Pt = Qn
        bT = k.bank()
        for h in range(4):
            k.tr(v3(bT[0:n, :], 4)[:, h, 0:n], X[:, h, :], cid(n))
        Tbd = t3(Pa)
        k.cp(Tbd, v3(bT[0:n, :], 4)[:, :, 0:n])
        bY = k.bank()
        for h in range(4):
            k.mm(v3(bY[0:n, :], 4)[:, h, 0:n], Loff[:, h, :], X[:, h, :])
        Y = t3(Pb)
        k.cp(Y, v3(bY[0:n, :], 4)[:, :, 0:n])
        bZ = k.bank()
        for h in range(4):
            k.mm(v3(bZ[0:n, :], 4)[:, h, 0:n], Tbd[:, h, :], Y[:, h, :])
        Xf = t3(Qa)
        k.tt(Xf, X, v3(bZ[0:n, :], 4)[:, :, 0:n], ALU.subtract)
        return Xf

    def layer_mod(l):
        gi = 0
        for cc in range(6):
            b = k.bank()
            for kc in range(8):
                g = stage[gi % 2][:, (gi // 2 % 2) * 512:(gi // 2 % 2) * 512 + 512]
                gi += 1
                k.dma(g, ada_w[l, kc * 128:(kc + 1) * 128, cc * 512:(cc + 1) * 512])
                k.mm(b[0:NS + 1, :], cT[:, kc, :], g, start=(kc == 0), stop=(kc == 7))
            gb = hT[0:NS + 1, :, :].rearrange("p a b -> p (a b)").bitcast(F32)
            mo = hb[0:NS + 1, :].bitcast(F32)
            k.dma(gb, bc(ada_b[l:l + 1, cc * 512:(cc + 1) * 512], [NS + 1, 512]))
            k.tt(mo, b[0:NS + 1, :], gb, ALU.add)
            if cc in (2, 3):
                k.dma(x_t[0:NS + 1, 0:512], bc(norm_g[l:l + 1, (cc - 2) * 512:(cc - 1) * 512], [NS + 1, 512]))
                k.stt(mo, mo, 1.0, x_t[0:NS + 1, 0:512], ALU.add, ALU.mult)
            k.dma(modall[:, cc * 512:(cc + 1) * 512], mo, q="pool")

    def softplus_parts(out, x, n, w, t1, t2):
        k.act(t1, x, AF.Abs)
        k.act(t2, t1, AF.Exp, scale=-1.0)
        k.act(out, t2, AF.Ln, bias=1.0)

    def dn_phase(l):
        load_weights(l, DN_C0, DN_W, 0)
        for d_ in range(4):
            bload(PC[:, d_ * 1536:(d_ + 1) * 1536], dn_conv_w[l, 3 - d_:4 - d_, :], 1536)
        bload(PC[:, 6144:6272], dn_norm_g[l:l + 1, :], 128)
        bload(PC[:, 6272:6276], dn_a_log[l:l + 1, :], 4)
        bload(PC[:, 6276:6280], dn_dt_bias[l:l + 1, :], 4)
        k.act(PC[:, 6280:6284], PC[:, 6272:6276], AF.Exp)
        k.ts(PC[:, 6280:6284], PC[:, 6280:6284], -1.0, ALU.mult)
        extract_mod("prompt", 128, 0, 2 * D)
        k.memset(TL, 0.0)
        k.memset(S_dn[:], 0.0)
        def pre(seg):
            k.dma(TL, st_conv[l, seg[1]])
            k.dma(v3(stage[0][:, 0:512], 4), st_dn[l, seg[1]].rearrange("h k v -> k h v"))
            k.cp(S_dn[:], v3(stage[0][:, 0:512], 4), eng="dve")

        def post(seg):
            kind, idx, n, row = seg
            if kind == "s":
                k.dma(s_conv[l, idx], TL, q="pool")
                k.dma(s_dn[l, idx].rearrange("h k v -> k h v"), S_dn[:], q="pool")
            elif idx == NT - 1:
                k.dma(p_conv[l], TL, q="pool")
                k.dma(p_dn[l].rearrange("h k v -> k h v"), S_dn[:], q="pool")

        def _nl():
            load_weights(l, GLA_C0, GLA_W, 0)
            k.preloaded = (l, "gla")
        k.next_load = _nl
        mixer_pass(l, DN_W, pre, dn_segment, post, batch_core=dn_batch)

    def dn_segment(l, seg):
        kind, idx, n, row = seg
        qkv = proj[0:n, 0:1536]
        for c in range(3):
            cs = slice(c * 512, (c + 1) * 512)
            b = k.bank()
            ops = []
            for d_ in range(4):
                if n > d_:
                    g = G[d_]
                    k.tt(g[0:n, :], proj[0:n, cs], PC[0:n, d_ * 1536 + c * 512: d_ * 1536 + (c + 1) * 512], ALU.mult)
                    ops.append((cmat((C_ID, C_SH1, C_SH2, C_SH3)[d_], n, n), g[0:n, :]))
            for d_ in range(1, 4):
                g = G[3 + d_]
                k.tt(g[0:3, :], TL[0:3, cs], PC[0:3, d_ * 1536 + c * 512: d_ * 1536 + (c + 1) * 512], ALU.mult)
                ops.append((cmat((C_BD1, C_BD2, C_BD3)[d_ - 1], 3, n), g[0:3, :]))
            for i, (lt, rh) in enumerate(ops):
                k.mm(b[0:n, :], lt, rh, start=(i == 0), stop=(i == len(ops) - 1))
            k.act(cq[0:n, cs], b[0:n, :], AF.Silu)
        for c in range(3):
            cs = slice(c * 512, (c + 1) * 512)
            b = k.bank()
            if n == 128:
                k.mm(b[0:3, :], cmat(C_SELB128, 128, 3), proj[0:n, cs])
            else:
                k.mm(b[0:3, :], cmat(C_A1, 3, 3), TL[0:3, cs], start=True, stop=False)
                k.mm(b[0:3, :], cmat(C_SELB1, 1, 3), proj[0:1, cs], start=False, stop=True)
            k.cp(TL[0:3, cs], b[0:3, :])
        k.act(G[23][0:n, :], proj[0:n, 1536:2048], AF.Silu)
        k.act(sm[0:n, 28:32], proj[0:n, 2048:2052], AF.Sigmoid)
        qn, kn, vv, beta, gg = dn_prep(n, beta_done=True)
        dn_chunk(l, seg, qn, kn, vv, beta, gg)

    def dn_prep(n, beta_done=False):
        sq = G[0]
        for hlf in range(2):
            k.tt(sq[0:n, :], cq[0:n, hlf * 512:(hlf + 1) * 512], cq[0:n, hlf * 512:(hlf + 1) * 512], ALU.mult)
            k.red(sm[0:n, 4 + hlf * 4:8 + hlf * 4], v3(sq[0:n, :], 4))
        k.act(sm[0:n, 12:20], sm[0:n, 4:12], AF.Ln, bias=EPS)
        k.act(sm[0:n, 20:28], sm[0:n, 12:20], AF.Exp, scale=-0.5)
        k.ts(sm[0:n, 20:24], sm[0:n, 20:24], 128.0 ** -0.5, ALU.mult)
        qk3 = v3(cq[0:n, 0:1024], 8)
        k.tt(qk3, qk3, bc(sm[0:n, 20:28].unsqueeze(2), [n, 8, 128]), ALU.mult)
        qn = cq[0:n, 0:512]
        kn = cq[0:n, 512:1024]
        vv = cq[0:n, 1024:1536]
        beta = sm[0:n, 28:32]
        if not beta_done:
            k.act(beta, proj[0:n, 2048:2052], AF.Sigmoid)
        tt_ = sm[0:n, 32:36]
        k.tt(tt_, proj[0:n, 2052:2056], PC[0:n, 6276:6280], ALU.add)
        softplus_parts(sm[0:n, 36:40], tt_, n, 4, sm[0:n, 40:44], sm[0:n, 44:48])
        k.ts(sm[0:n, 40:44], tt_, 0.0, ALU.max)
        k.tt(sm[0:n, 40:44], sm[0:n, 40:44], sm[0:n, 36:40], ALU.add)
        gg = sm[0:n, 48:52]
        k.tt(gg, sm[0:n, 40:44], PC[0:n, 6280:6284], ALU.mult)
        return qn, kn, vv, beta, gg

    def i16(p0, p1, shape, axis):
        a = CM[p0:p1, C_I16:C_I16 + 2, :].rearrange("p a (b c) -> p (a b) c", c=16)[:, 0:NS, 0:NS]
        return bc(a.unsqueeze(axis), shape)

    def dn_batch(l):
        n = NS
        for c in range(3):
            cs = slice(c * 512, (c + 1) * 512)
            acc, tmp = G[0], G[1]
            k.tt(acc[0:n, :], proj[0:n, cs], PC[0:n, c * 512:(c + 1) * 512], ALU.mult)
            for d_ in range(1, 4):
                st = stage[d_ % 2][0:n, (d_ // 2) * 512:(d_ // 2) * 512 + 512]
                k.dma(st, st_conv[l, :, 3 - d_, cs])
                k.tt(tmp[0:n, :], st, PC[0:n, d_ * 1536 + c * 512: d_ * 1536 + (c + 1) * 512], ALU.mult)
                k.tt(acc[0:n, :], acc[0:n, :], tmp[0:n, :], ALU.add)
            k.act(cq[0:n, cs], acc[0:n, :], AF.Silu)
        k.dma(s_conv[l, :, 0:2, :], st_conv[l, :, 1:3, :], q="pool")
        k.dma(s_conv[l, :, 2, :], proj[0:n, 0:1536], q="pool")
        qn, kn, vv, beta, gg = dn_prep(n)
        eG = sm[0:n, 60:64]
        k.act(eG, gg, AF.Exp)
        beG = sm[0:n, 68:72]
        k.tt(beG, beta, eG, ALU.mult)

        def h3(t):
            return v3(t[0:n, :], 4)

        def bcs(s_):
            return bc(s_.unsqueeze(2), [n, 4, 128])

        kbg, vb, qg, t0 = G[0], G[1], G[2], G[3]
        k.tt(h3(kbg), v3(kn, 4), bcs(beG), ALU.mult)
        k.tt(h3(vb), v3(vv, 4), bcs(beta), ALU.mult)
        k.tt(h3(qg), v3(qn, 4), bcs(eG), ALU.mult)
        k.tt(t0[0:n, :], qn, kn, ALU.mult)
        Aqk = sm[0:n, 72:76]
        k.red(Aqk, h3(t0))
        MLk, MLq = (G[4], G[5]), (G[6], G[7])
        for src, ML in ((kbg, MLk), (qg, MLq)):
            b = k.bank()
            for h in range(4):
                k.tr(b[:, h * 128:h * 128 + n], src[0:n, h * 128:(h + 1) * 128], cid(n))
            for hp in range(2):
                o4 = ML[hp][:, :].rearrange("p (a s m) -> p a s m", a=2, s=16)[:, :, 0:n, 0:n]
                i4 = v3(b[:, :], 4)[:, 2 * hp:2 * hp + 2, 0:n]
                k.tt(o4, bc(i4.unsqueeze(2), [128, 2, n, n]), i16(0, 128, [128, 2, n, n], 1), ALU.mult)
        bKS, bQS = k.bank(), k.bank()
        k.memset(bKS[0:n, :], 0.0)
        k.memset(bQS[0:n, :], 0.0)
        for s_ in range(n):
            st = v3(stage[s_ % 2][:, 0:512], 4)
            k.dma(st, st_dn[l, s_].rearrange("h k v -> k h v"))
            for h in range(4):
                for ML, bk in ((MLk, bKS), (MLq, bQS)):
                    lt = ML[h // 2][:, :].rearrange("p (a s m) -> p a s m", a=2, s=16)[:, h % 2, s_, 0:n]
                    k.mm(bk[0:n, h * 128:(h + 1) * 128], lt, st[:, h, :], start=False, stop=True, skip=True)
        vnew, o_ = G[8], G[9]
        k.tt(vnew[0:n, :], vb[0:n, :], bKS[0:n, :], ALU.subtract)
        k.tt(h3(o_), h3(vnew), bcs(Aqk), ALU.mult)
        k.tt(o_[0:n, :], o_[0:n, :], bQS[0:n, :], ALU.add)
        t1 = G[10]
        k.tt(t1[0:n, 0:4 * n].rearrange("p (s h) -> p s h", h=4), bc(eG.unsqueeze(1), [n, n, 4]),
             bc(CM[0:n, C_ID, 0:n].unsqueeze(2), [n, n, 4]), ALU.mult)
        b = k.bank()
        k.mm(b[:, 0:4 * n], cmat(C_ONES, n, 128), t1[0:n, 0:4 * n])
        EGB = G[11]
        k.cp(EGB[:, 0:4 * n], b[:, 0:4 * n])
        for s_ in range(n):
            st = v3(stage[s_ % 2][:, 0:512], 4)
            k.dma(st, st_dn[l, s_].rearrange("h k v -> k h v"))
            vm = G[12 + s_ % 2]
            k.ts(vm[0:n, :], vnew[0:n, :], CM[0:n, C_ID, s_:s_ + 1], ALU.mult)
            b = k.bank()
            for h in range(4):
                k.mm(b[:, h * 128:(h + 1) * 128], kn[:, h * 128:(h + 1) * 128], vm[0:n, h * 128:(h + 1) * 128])
            so = G[14 + s_ % 2]
            k.tt(v3(so[:, :], 4), st, bc(EGB[:, s_ * 4:(s_ + 1) * 4].unsqueeze(2), [128, 4, 128]), ALU.mult)
            k.tt(so[:, :], so[:, :], b[:, :], ALU.add)
            k.dma(s_dn[l, s_].rearrange("h k v -> k h v"), v3(so[:, :], 4), q="pool")
        rms_gate_store(sbatch, 4, 128, PC[0:n, 6144:6272], proj[0:n, 1536:2048], 0, EPS, src=o_)

    def dn_chunk(l, seg, qn, kn, vv, beta, gg):
        kind, idx, n, row = seg
        b = k.bank()
        k.mm(b[0:n, 0:4], cmat(C_TRIU, n, n), gg)
        Gc = sm[0:n, 52:56]
        k.cp(Gc, b[0:n, 0:4], eng="dve")
        nG = sm[0:n, 56:60]
        k.ts(nG, Gc, -1.0, ALU.mult)
        b = k.bank()
        k.mm(b[:, 0:4], cmat(C_ONES, n, 128), gg)
        GT = sm2[:, 0:4]
        k.cp(GT, b[:, 0:4], eng="dve")
        eGl = sm2[:, 4:8]
        k.act(eGl, GT, AF.Exp)
        eG = sm[0:n, 60:64]
        k.act(eG, Gc, AF.Exp)
        edk = sm[0:n, 64:68]
        k.tt(edk, GT[0:n, :], Gc, ALU.subtract)
        k.act(edk, edk, AF.Exp)
        beG = sm[0:n, 68:72]
        k.tt(beG, beta, eG, ALU.mult)

        def h3(t):
            return v3(t[0:n, :], 4)

        def bcs(s):
            return bc(s.unsqueeze(2), [n, 4, 128])

        kbg, vb, kd, qg = G[0], G[1], G[2], G[3]
        k.tt(h3(kbg), v3(kn, 4), bcs(beG), ALU.mult)
        k.tt(h3(vb), v3(vv, 4), bcs(beta), ALU.mult)
        k.tt(h3(kd), v3(kn, 4), bcs(edk), ALU.mult)
        k.tt(h3(qg), v3(qn, 4), bcs(eG), ALU.mult)
        KT, QT, QGT = G[4], G[5], G[6]
        for src, dst in ((kn, KT), (qn, QT), (qg[0:n, :], QGT)):
            b = k.bank()
            for h in range(4):
                k.tr(b[:, h * 128:h * 128 + n], src[:, h * 128:(h + 1) * 128], cid(n))
            k.cp(v3(dst[:, :], 4)[:, :, 0:n], v3(b[:, :], 4)[:, :, 0:n])

        def f3(t):
            return v3(t[:, :], 4)[:, :, 0:n]

        def m3(t):
            return v3(t[0:n, :], 4)[:, :, 0:n]

        dg = G[7]
        k.tt(m3(dg), bc(CM[0:n, C_ID:C_ID + 1, 0:n], [n, 4, n]), bc(Gc.unsqueeze(2), [n, 4, n]), ALU.mult)
        bR = k.bank()
        for h in range(4):
            k.mm(m3(bR)[:, h, :], cmat(C_ONES, n, n), m3(dg)[:, h, :])
        t1, t2, Dm, DTm = G[8], G[9], G[10], G[11]
        k.stt(m3(t1), m3(bR), -1.0, bc(CM[0:n, C_MLOW:C_MLOW + 1, 0:n], [n, 4, n]), ALU.mult, ALU.add)
        k.tt(m3(t2), m3(bR), bc(CM[0:n, C_MUP:C_MUP + 1, 0:n], [n, 4, n]), ALU.add)
        for h in range(4):
            k.act(m3(Dm)[:, h, :], m3(t1)[:, h, :], AF.Exp, bias=Gc[:, h:h + 1])
            k.act(m3(DTm)[:, h, :], m3(t2)[:, h, :], AF.Exp, bias=nG[:, h:h + 1])
        bK = k.bank()
        bA = k.bank()
        for h in range(4):
            k.mm(m3(bK)[:, h, :], f3(KT)[:, h, :], f3(KT)[:, h, :])
            k.mm(m3(bA)[:, h, :], f3(KT)[:, h, :], f3(QT)[:, h, :])
        KKD, AT = G[12], G[13]
        k.tt(m3(KKD), m3(bK), m3(Dm), ALU.mult)
        k.tt(m3(KKD), m3(KKD), bc(beta.unsqueeze(2), [n, 4, n]), ALU.mult)
        k.tt(m3(AT), m3(bA), m3(DTm), ALU.mult)
        if n > 1:
            Nbd, NbdT, Loff = G[14], G[15], G[16]
            k.tt(m3(Nbd), m3(KKD), bc(CM[0:n, C_NTRIL64:C_NTRIL64 + 1, 0:n], [n, 4, n]), ALU.mult)
            k.tt(m3(Loff), m3(KKD), bc(CM[0:n, C_LOWLEFT:C_LOWLEFT + 1, 0:n], [n, 4, n]), ALU.mult)
            b = k.bank()
            for h in range(4):
                k.tr(m3(b)[:, h, :], m3(Nbd)[:, h, :], cid(n))
            k.cp(m3(NbdT), m3(b))
            X = neumann(n, m3(Nbd), m3(NbdT), m3(Loff), G[17:23])
        else:
            X = neumann(n, None, None, None, G[17:23])
        bW = k.bank()
        for h in range(4):
            k.mm(f3(bW)[:, h, :], h3(kbg)[:, h, :], X[:, h, :])
        nWT = G[7]
        k.ts(f3(nWT), f3(bW), -1.0, ALU.mult)
        bV = k.bank()
        for h in range(4):
            k.mm(bV[0:n, h * 128:(h + 1) * 128], X[:, h, :], h3(vb)[:, h, :], start=True, stop=False)
            k.mm(bV[0:n, h * 128:(h + 1) * 128], f3(nWT)[:, h, :], S_dn[:, h, :], start=False, stop=True)
        VN = G[8]
        k.cp(VN[0:n, :], bV[0:n, :])
        for h in range(4):
            k.mm(PSO[0:n, h * 128:(h + 1) * 128], f3(QGT)[:, h, :], S_dn[:, h, :], start=True, stop=False)
            k.mm(PSO[0:n, h * 128:(h + 1) * 128], m3(AT)[:, h, :], h3(VN)[:, h, :], start=False, stop=True)
        bS = k.bank()
        for h in range(4):
            k.mm(bS[:, h * 128:(h + 1) * 128], h3(kd)[:, h, :], h3(VN)[:, h, :])
        for h in range(4):
            k.stt(S_dn[:, h, :], S_dn[:, h, :], eGl[:, h:h + 1], bS[:, h * 128:(h + 1) * 128], ALU.mult, ALU.add)
        rms_gate_store(seg, 4, 128, PC[0:n, 6144:6272], proj[0:n, 1536:2048], 0, EPS, otile=G[22], zs_pre=G[23])

    def gla_phase(l):
        if k.preloaded != (l, "gla"):
            load_weights(l, GLA_C0, GLA_W, 0)
        bload(PC[:, 0:256], gla_bf[l:l + 1, :], 256)
        bload(PC[:, 256:384], gla_norm_g[l:l + 1, :], 128)
        k.dma(wsm[0:16, 0:256], gla_wf[l])
        extract_mod("prompt", 128, 0, 2 * D)
        k.memset(S_gla[:], 0.0)
        k.memset(G[12][:, :], 0.0)
        k.memset(G[15][:, :], 0.0)
        def pre(seg):
            k.dma(v3(stage[0][:, 0:256], 2), st_gla[l, seg[1]].rearrange("(hp h2) k v -> (h2 k) hp v", h2=2))
            k.cp(S_gla[:], v3(stage[0][:, 0:256], 2), eng="dve")

        def post(seg):
            kind, idx, n, row = seg
            if kind == "s":
                k.dma(s_gla[l, idx].rearrange("(hp h2) k v -> (h2 k) hp v", h2=2), S_gla[:], q="pool")
            elif idx == NT - 1:
                k.dma(p_gla[l].rearrange("(hp h2) k v -> (h2 k) hp v", h2=2), S_gla[:], q="pool")

        def _nl():
            load_weights(l, RW_C0, RW_W, 0)
            k.preloaded = (l, "rw")
        k.next_load = _nl
        mixer_pass(l, GLA_W, pre, gla_segment, post, batch_core=gla_batch)

    def gla_segment(l, seg):
        kind, idx, n, row = seg
        q_ = proj[0:n, 0:256]
        k_ = proj[0:n, 256:512]
        v_ = proj[0:n, 512:1024]
        gz = proj[0:n, 1024:1536]
        lf = gla_lf(n)
        LF = lf[0:n, 0:256]
        gla_chunk(l, seg, q_, k_, v_, gz, lf, LF)

    def gla_lf(n):
        glo = proj[0:n, 1536:1552]
        b = k.bank()
        k.tr(b[0:16, 0:n], glo, cid(n))
        gloT = G[0]
        k.cp(gloT[0:16, 0:n], b[0:16, 0:n])
        b = k.bank()
        k.mm(b[0:n, 0:256], gloT[0:16, 0:n], wsm[0:16, 0:256])
        xb, t1, t2, lf = G[1], G[2], G[3], G[4]
        k.tt(xb[0:n, 0:256], b[0:n, 0:256], PC[0:n, 0:256], ALU.add)
        softplus_parts(t1[0:n, 0:256], xb[0:n, 0:256], n, 256, t2[0:n, 0:256], t2[0:n, 256:512])
        k.ts(t2[0:n, 0:256], xb[0:n, 0:256], 0.0, ALU.min)
        k.tt(lf[0:n, 0:256], t2[0:n, 0:256], t1[0:n, 0:256], ALU.subtract)
        k.ts(lf[0:n, 0:256], lf[0:n, 0:256], 1.0 / 16.0, ALU.mult)
        return lf

    def gla_batch(l):
        n = NS
        q_ = proj[0:n, 0:256]
        k_ = proj[0:n, 256:512]
        v_ = proj[0:n, 512:1024]
        gz = proj[0:n, 1024:1536]
        lf = gla_lf(n)
        LF = lf[0:n, 0:256]
        eG, enG, qg, kg, t0 = G[5][0:n, 0:256], G[6][0:n, 0:256], G[7][0:n, 0:256], G[8][0:n, 0:256], G[9][0:n, 0:256]
        k.act(eG, LF, AF.Exp)
        k.act(enG, LF, AF.Exp, scale=-1.0)
        k.stt(qg, q_, 0.125, eG, ALU.mult, ALU.mult)
        k.tt(kg, k_, enG, ALU.mult)
        k.tt(t0, qg, kg, ALU.mult)
        Aqk = sm[0:n, 72:76]
        k.red(Aqk, v3(t0, 4))
        MLq = (G[12], G[15])
        b = k.bank()
        for hp in range(2):
            k.tr(b[:, hp * 128:hp * 128 + n], qg[:, hp * 128:(hp + 1) * 128], cid(n))
        for h2 in range(2):
            pr0, pr1 = h2 * 64, h2 * 64 + 64
            o4 = MLq[h2][pr0:pr1, :].rearrange("p (a s m) -> p a s m", a=2, s=16)[:, :, 0:n, 0:n]
            i4 = v3(b[pr0:pr1, 0:256], 2)[:, :, 0:n]
            k.tt(o4, bc(i4.unsqueeze(2), [64, 2, n, n]), i16(pr0, pr1, [64, 2, n, n], 1), ALU.mult)
        b = k.bank()
        for hp in range(2):
            k.tr(b[:, hp * 128:hp * 128 + n], lf[0:n, hp * 128:(hp + 1) * 128], cid(n))
        EGB = G[10]
        k.act(v3(EGB[:, 0:256], 2)[:, :, 0:n], v3(b[:, 0:256], 2)[:, :, 0:n], AF.Exp)
        bQS = k.bank()
        k.memset(bQS[0:n, :], 0.0)
        for s_ in range(n):
            st = v3(stage[s_ % 2][:, 0:256], 2)
            k.dma(st, st_gla[l, s_].rearrange("(hp h2) k v -> (h2 k) hp v", h2=2))
            for h in range(4):
                lt = MLq[h % 2][:, :].rearrange("p (a s m) -> p a s m", a=2, s=16)[:, h // 2, s_, 0:n]
                k.mm(bQS[0:n, h * 128:(h + 1) * 128], lt, st[:, h // 2, :], start=False, stop=True, skip=True)
        o_ = G[11]
        k.tt(v3(o_[0:n, :], 4), v3(v_, 4), bc(Aqk.unsqueeze(2), [n, 4, 128]), ALU.mult)
        k.tt(o_[0:n, :], o_[0:n, :], bQS[0:n, :], ALU.add)
        for s_ in range(n):
            st = v3(stage[s_ % 2][:, 0:256], 2)
            k.dma(st, st_gla[l, s_].rearrange("(hp h2) k v -> (h2 k) hp v", h2=2))
            vm = G[0 + s_ % 2]
            k.ts(vm[0:n, :], v_, CM[0:n, C_ID, s_:s_ + 1], ALU.mult)
            b = k.bank()
            for hp in range(2):
                k.mm(b[:, hp * 256:(hp + 1) * 256], k_[:, hp * 128:(hp + 1) * 128], vm[0:n, hp * 256:(hp + 1) * 256])
            so = G[2 + s_ % 2]
            for hp in range(2):
                for h2 in range(2):
                    pr = slice(h2 * 64, h2 * 64 + 64)
                    k.stt(v3(so[:, 0:256], 2)[pr, hp, :], st[pr, hp, :], v3(EGB[:, 0:256], 2)[pr, hp, s_:s_ + 1],
                          b[pr, hp * 256 + h2 * 128: hp * 256 + (h2 + 1) * 128], ALU.mult, ALU.add)
            k.dma(s_gla[l, s_].rearrange("(hp h2) k v -> (h2 k) hp v", h2=2), v3(so[:, 0:256], 2), q="pool")
        rms_gate_store(sbatch, 4, 128, PC[0:n, 256:384], gz, 512, EPS, src=o_)

    def gla_chunk(l, seg, q_, k_, v_, gz, lf, LF):
        kind, idx, n, row = seg
        bG = k.bank()
        k.mm(bG[0:n, 0:256], cmat(C_TRIU, n, n), LF)
        Gc = G[5][0:n, 0:256]
        k.cp(Gc, bG[0:n, 0:256])
        bT = k.bank()
        k.mm(bT[0:n, 0:256], cmat(C_ONES, n, n), LF)
        edk = G[6][0:n, 0:256]
        k.tt(edk, bT[0:n, 0:256], Gc, ALU.subtract)
        k.act(edk, edk, AF.Exp)
        bE = k.bank()
        for hp in range(2):
            k.mm(bE[:, 2 * hp:2 * hp + 2], lf[0:n, hp * 128:(hp + 1) * 128], cmat(C_ONES, n, 2))
        eGl = sm2[:, 8:10]
        k.act(eGl, v3(bE[:, 0:4], 2)[:, :, 0], AF.Exp)
        eG = G[7][0:n, 0:256]
        enG = G[8][0:n, 0:256]
        k.act(eG, Gc, AF.Exp)
        k.act(enG, Gc, AF.Exp, scale=-1.0)
        qg = G[9][0:n, 0:256]
        kg = G[10][0:n, 0:256]
        kd = G[11][0:n, 0:256]
        k.stt(qg, q_, 0.125, eG, ALU.mult, ALU.mult)
        k.tt(kg, k_, enG, ALU.mult)
        k.tt(kd, k_, edk, ALU.mult)
        QGTm, KGT = (G[12], G[15]), G[13]
        for src, dst in ((qg, None), (kg, KGT)):
            b = k.bank()
            for hp in range(2):
                k.tr(b[:, hp * 128:hp * 128 + n], src[:, hp * 128:(hp + 1) * 128], cid(n))
            if dst is None:
                for h2 in range(2):
                    pr = slice(h2 * 64, h2 * 64 + 64)
                    k.cp(v3(QGTm[h2][pr, 0:256], 2)[:, :, 0:n], v3(b[pr, 0:256], 2)[:, :, 0:n])
            else:
                k.cp(v3(dst[:, 0:256], 2)[:, :, 0:n], v3(b[:, 0:256], 2)[:, :, 0:n])

        def m3(t):
            return v3(t[0:n, :], 4)[:, :, 0:n]

        bA = k.bank()
        for h in range(4):
            hp = h // 2
            k.mm(m3(bA)[:, h, :], v3(KGT[:, 0:256], 2)[:, hp, 0:n], v3(QGTm[h % 2][:, 0:256], 2)[:, hp, 0:n])
        AT = G[14]
        k.tt(m3(AT), m3(bA), bc(CM[0:n, C_TRIU:C_TRIU + 1, 0:n], [n, 4, n]), ALU.mult)
        for h in range(4):
            hp = h // 2
            k.mm(PSO[0:n, h * 128:(h + 1) * 128], v3(QGTm[h % 2][:, 0:256], 2)[:, hp, 0:n], S_gla[:, hp, :], start=True, stop=False)
            k.mm(PSO[0:n, h * 128:(h + 1) * 128], m3(AT)[:, h, :], v_[:, h * 128:(h + 1) * 128], start=False, stop=True)
        for hp in range(2):
            bS = k.bank()
            k.mm(bS[:, 0:256], kd[:, hp * 128:(hp + 1) * 128], v_[:, hp * 256:(hp + 1) * 256])
            for h2 in range(2):
                pr = slice(h2 * 64, h2 * 64 + 64)
                k.stt(S_gla[pr, hp, :], S_gla[pr, hp, :], eGl[pr, hp:hp + 1], bS[pr, h2 * 128:(h2 + 1) * 128], ALU.mult, ALU.add)
        rms_gate_store(seg, 4, 128, PC[0:n, 256:384], gz, 512, EPS, otile=G[14])

    RWC = dict(mu=0, w0=1664, a0=2176, kk=2688, ka=3200, rk=3712, lw=4224, lb=4736)

    def rw_phase(l):
        if k.preloaded != (l, "rw"):
            load_weights(l, RW_C0, RW_W, 0)
        bload(PC[:, 0:1664], rw_mu[l:l + 1, :], 1664)
        for nm, src in (("w0", rw_w0), ("a0", rw_a0), ("kk", rw_k_k), ("ka", rw_k_a), ("rk", rw_r_k),
                        ("lw", rw_ln_w), ("lb", rw_ln_b)):
            bload(PC[:, RWC[nm]:RWC[nm] + 512], src[l:l + 1, :], 512)
        k.dma(wsm[0:64, 0:512], rw_w2[l])
        k.dma(wsm[0:64, 512:1024], rw_a2[l])
        extract_mod("prompt", 128, 0, 2 * D)
        k.memset(srow, 0.0)
        k.memset(M_rw[:], 0.0)
        MT = G[26]
        def pre(seg):
            idx = seg[1]
            k.dma(srow, st_rs[l, idx:idx + 1, :])
            k.dma(v3(MT[0:64, :], 8), st_rw[l, idx].rearrange("h v k -> v h k"))
            b = k.bank()
            for hp in range(4):
                k.tr(b[:, hp * 64:(hp + 1) * 64], MT[0:64, hp * 128:(hp + 1) * 128], cid(64))
            k.cp(M_rw[:], v3(b[:, 0:256], 4))

        def post(seg):
            kind, idx, n, row = seg
            last = (kind == "s") or idx == NT - 1
            if last:
                b = k.bank()
                for hp in range(4):
                    k.tr(b[0:64, hp * 128:(hp + 1) * 128], M_rw[:, hp, :], cid(128))
                k.cp(MT[0:64, :], b[0:64, :])
                dst = s_rw[l, idx] if kind == "s" else p_rw[l]
                k.dma(dst.rearrange("h v k -> v h k"), v3(MT[0:64, :], 8), q="pool")
                dst = s_rs[l, idx:idx + 1, :] if kind == "s" else p_rs[l:l + 1, :]
                k.dma(dst, proj[n - 1:n, 0:1664], q="pool")
            else:
                k.dma(srow, proj[n - 1:n, 0:1664])

        def _nl():
            load_weights(l, 0, D, 0, rows_kc=12, src=w_out)
            k.preloaded = (l, "out")
        k.next_load = _nl
        mixer_pass(l, RW_W, pre, rw_segment, post, batch_core=rw_batch)

    def rw_segment(l, seg):
        kind, idx, n, row = seg
        xm = cq
        for c0 in range(0, 1664, 512):
            wd = min(512, 1664 - c0)
            b = k.bank()
            if n > 1:
                k.mm(b[0:n, 0:wd], cmat(C_SH1, n, n), proj[0:n, c0:c0 + wd], start=True, stop=False)
            k.mm(b[0:n, 0:wd], cmat(C_E0, 1, n), srow[0:1, c0:c0 + wd], start=(n == 1), stop=True)
            g = G[0]
            k.tt(g[0:n, 0:wd], b[0:n, 0:wd], proj[0:n, c0:c0 + wd], ALU.subtract)
            k.tt(g[0:n, 0:wd], g[0:n, 0:wd], PC[0:n, c0:c0 + wd], ALU.mult)
            k.tt(xm[0:n, c0:c0 + wd], g[0:n, 0:wd], proj[0:n, c0:c0 + wd], ALU.add)
        rw_chunk(l, seg, *rw_prep(n))

    def rw_prep(n):
        xm = cq
        r_ = xm[0:n, 0:512]
        k_ = xm[0:n, 512:1024]
        v_ = xm[0:n, 1024:1536]
        rz = proj[0:n, 1664:2176]

        def pc(nm):
            return PC[0:n, RWC[nm]:RWC[nm] + 512]

        tw = G[0]
        k.act(tw[0:n, 0:64], xm[0:n, 1536:1600], AF.Tanh)
        b = k.bank()
        k.tr(b[0:64, 0:n], tw[0:n, 0:64], cid(n))
        k.tr(b[0:64, 128:128 + n], xm[0:n, 1600:1664], cid(n))
        loT = G[1]
        k.cp(loT[0:64, 0:256], b[0:64, 0:256])
        bw = k.bank()
        k.mm(bw[0:n, :], loT[0:64, 0:n], wsm[0:64, 0:512])
        LW = G[2]
        k.tt(LW[0:n, :], bw[0:n, :], pc("w0"), ALU.add)
        k.act(LW[0:n, :], LW[0:n, :], AF.Sigmoid)
        k.ts(LW[0:n, :], LW[0:n, :], -float(np.exp(-0.5)), ALU.mult)
        ba = k.bank()
        k.mm(ba[0:n, :], loT[0:64, 128:128 + n], wsm[0:64, 512:1024])
        At = G[3]
        k.tt(At[0:n, :], ba[0:n, :], pc("a0"), ALU.add)
        k.act(At[0:n, :], At[0:n, :], AF.Sigmoid)
        KK, K2, Bt, t0 = G[4], G[5], G[6], G[7]
        k.tt(KK[0:n, :], k_, pc("kk"), ALU.mult)
        k.tt(t0[0:n, :], KK[0:n, :], KK[0:n, :], ALU.mult)
        k.red(sm[0:n, 64:72], v3(t0[0:n, :], 8))
        k.act(sm[0:n, 72:80], sm[0:n, 64:72], AF.Ln, bias=EPS)
        k.act(sm[0:n, 80:88], sm[0:n, 72:80], AF.Exp, scale=-0.5)
        k.tt(v3(KK[0:n, :], 8), v3(KK[0:n, :], 8), bc(sm[0:n, 80:88].unsqueeze(2), [n, 8, 64]), ALU.mult)
        k.stt(t0[0:n, :], At[0:n, :], -1.0, pc("ka"), ALU.add, ALU.mult)
        k.stt(K2[0:n, :], t0[0:n, :], 1.0, k_, ALU.add, ALU.mult)
        k.tt(Bt[0:n, :], KK[0:n, :], At[0:n, :], ALU.mult)
        k.tt(t0[0:n, :], r_, K2[0:n, :], ALU.mult)
        k.tt(t0[0:n, :], t0[0:n, :], pc("rk"), ALU.mult)
        bsum = sm[0:n, 88:96]
        k.red(bsum, v3(t0[0:n, :], 8))
        return r_, k_, v_, rz, LW, At, KK, K2, Bt, bsum, pc

    def rw_chunk(l, seg, r_, k_, v_, rz, LW, At, KK, K2, Bt, bsum, pc):
        kind, idx, n, row = seg
        t0 = G[7]
        bG = k.bank()
        k.mm(bG[0:n, :], cmat(C_TRIU, n, n), LW[0:n, :])
        Gc = G[8]
        k.cp(Gc[0:n, :], bG[0:n, :])
        bT = k.bank()
        k.mm(bT[0:n, :], cmat(C_ONES, n, n), LW[0:n, :])
        edk = G[9]
        k.tt(edk[0:n, :], bT[0:n, :], Gc[0:n, :], ALU.subtract)
        k.act(edk[0:n, :], edk[0:n, :], AF.Exp)
        bE = k.bank()
        for hp in range(4):
            k.mm(bE[:, 2 * hp:2 * hp + 2], LW[0:n, hp * 128:(hp + 1) * 128], cmat(C_ONES, n, 2))
        eGl = sm2[:, 12:16]
        k.act(eGl, v3(bE[:, 0:8], 4)[:, :, 0], AF.Exp)
        eG, enG, eGx = G[10], G[11], G[7]
        k.act(eG[0:n, :], Gc[0:n, :], AF.Exp)
        k.act(enG[0:n, :], Gc[0:n, :], AF.Exp, scale=-1.0)
        k.tt(eGx[0:n, :], Gc[0:n, :], LW[0:n, :], ALU.subtract)
        k.act(eGx[0:n, :], eGx[0:n, :], AF.Exp)
        kkg, rg, bg, k2g, bd, k2d = G[12], G[13], G[14], G[15], G[16], G[17]
        k.tt(kkg[0:n, :], KK[0:n, :], eGx[0:n, :], ALU.mult)
        k.tt(rg[0:n, :], r_, eG[0:n, :], ALU.mult)
        k.tt(bg[0:n, :], Bt[0:n, :], enG[0:n, :], ALU.mult)
        k.tt(k2g[0:n, :], K2[0:n, :], enG[0:n, :], ALU.mult)
        k.tt(bd[0:n, :], Bt[0:n, :], edk[0:n, :], ALU.mult)
        k.tt(k2d[0:n, :], K2[0:n, :], edk[0:n, :], ALU.mult)
        bgT, k2gT, kkgTm, rgTm = G[0], G[1], (G[2], G[3]), (G[10], G[11])
        for src, dst in ((kkg, kkgTm), (rg, rgTm), (bg, bgT), (k2g, k2gT)):
            b = k.bank()
            for hp in range(4):
                k.tr(b[:, hp * 128:hp * 128 + n], src[0:n, hp * 128:(hp + 1) * 128], cid(n))
            if isinstance(dst, tuple):
                for h2 in range(2):
                    pr = slice(h2 * 64, h2 * 64 + 64)
                    po = slice((1 - h2) * 64, (1 - h2) * 64 + 64)
                    k.cp(v3(dst[h2][pr, :], 4)[:, :, 0:n], v3(b[pr, :], 4)[:, :, 0:n])
                    k.memset(v3(dst[h2][po, :], 4)[:, :, 0:n], 0.0)
            else:
                k.cp(v3(dst[:, :], 4)[:, :, 0:n], v3(b[:, :], 4)[:, :, 0:n])

        def fT(t, h):
            if isinstance(t, tuple):
                t = t[h % 2]
            return v3(t[:, :], 4)[:, h // 2, 0:n]

        def m3(t):
            return v3(t[0:n, :], 4)[:, :, 0:n]

        def mk(i):
            return bc(CM[0:n, i:i + 1, 0:n], [n, 4, n])

        for hb_ in range(2):
            heads = [4 * hb_ + i for i in range(4)]
            bN, bNT, bAk, bRb, bRk = k.bank(), k.bank(), k.bank(), k.bank(), k.bank()
            for i, h in enumerate(heads):
                k.mm(m3(bN)[:, i, :], fT(kkgTm, h), fT(bgT, h))
                k.mm(m3(bNT)[:, i, :], fT(bgT, h), fT(kkgTm, h))
                k.mm(m3(bAk)[:, i, :], fT(k2gT, h), fT(kkgTm, h))
                k.mm(m3(bRb)[:, i, :], fT(bgT, h), fT(rgTm, h))
                k.mm(m3(bRk)[:, i, :], fT(k2gT, h), fT(rgTm, h))
            AkT, RbT, RkT = G[4], G[5], G[6]
            k.tt(m3(AkT), m3(bAk), mk(C_TRIUS), ALU.mult)
            k.tt(m3(RbT), m3(bRb), mk(C_TRIU), ALU.mult)
            k.tt(m3(RkT), m3(bRk), mk(C_TRIU), ALU.mult)
            if n > 1:
                Nbd, NbdT, Loff = G[7], G[8], G[9]
                k.tt(m3(Nbd), m3(bN), mk(C_NTRIL64), ALU.mult)
                k.tt(m3(Loff), m3(bN), mk(C_LOWLEFT), ALU.mult)
                k.tt(m3(NbdT), m3(bNT), mk(C_NTRIU64), ALU.mult)
                X = neumann(n, m3(Nbd), m3(NbdT), m3(Loff), G[18:24])
            else:
                X = neumann(n, None, None, None, G[18:24])
            bR = k.bank()
            for i, h in enumerate(heads):
                k.mm(bR[0:n, i * 64:(i + 1) * 64], fT(kkgTm, h), M_rw[:, h // 2, :], start=True, stop=False)
                k.mm(bR[0:n, i * 64:(i + 1) * 64], m3(AkT)[:, i, :], v_[:, h * 64:(h + 1) * 64], start=False, stop=True)
            R0 = G[24]
            k.ts(R0[0:n, 0:256], bR[0:n, 0:256], -1.0, ALU.mult)
            bU = k.bank()
            for i, h in enumerate(heads):
                k.mm(bU[0:n, i * 64:(i + 1) * 64], X[:, i, :], R0[0:n, i * 64:(i + 1) * 64])
            U = G[25]
            k.cp(U[0:n, 0:256], bU[0:n, 0:256])
            for i, h in enumerate(heads):
                o_ = PSO[0:n, h * 64:(h + 1) * 64]
                k.mm(o_, fT(rgTm, h), M_rw[:, h // 2, :], start=True, stop=False)
                k.mm(o_, m3(RbT)[:, i, :], U[0:n, i * 64:(i + 1) * 64], start=False, stop=False)
                k.mm(o_, m3(RkT)[:, i, :], v_[:, h * 64:(h + 1) * 64], start=False, stop=True)
            for j in range(2):
                hp = 2 * hb_ + j
                bM = k.bank()
                k.mm(bM[:, 0:128], bd[0:n, hp * 128:(hp + 1) * 128], U[0:n, j * 128:(j + 1) * 128], start=True, stop=False)
                k.mm(bM[:, 0:128], k2d[0:n, hp * 128:(hp + 1) * 128], v_[:, hp * 128:(hp + 1) * 128], start=False, stop=True)
                for h2 in range(2):
                    pr = slice(h2 * 64, h2 * 64 + 64)
                    k.stt(M_rw[pr, hp, :], M_rw[pr, hp, :], eGl[pr, hp:hp + 1], bM[pr, h2 * 64:(h2 + 1) * 64], ALU.mult, ALU.add)
        rw_out(n, row, None, v_, rz, bsum, pc)

    def rw_batch(l):
        n = NS
        xm = cq
        for j, c0 in enumerate(range(0, 1664, 512)):
            wd = min(512, 1664 - c0)
            st = stage[j % 2][0:n, (j // 2) * 512:(j // 2) * 512 + wd]
            k.dma(st, st_rs[l, 0:n, c0:c0 + wd])
            g = G[0]
            k.tt(g[0:n, 0:wd], st, proj[0:n, c0:c0 + wd], ALU.subtract)
            k.tt(g[0:n, 0:wd], g[0:n, 0:wd], PC[0:n, c0:c0 + wd], ALU.mult)
            k.tt(xm[0:n, c0:c0 + wd], g[0:n, 0:wd], proj[0:n, c0:c0 + wd], ALU.add)
        k.dma(s_rs[l, 0:n, :], proj[0:n, 0:1664], q="pool")
        r_, k_, v_, rz, LW, At, KK, K2, Bt, bsum, pc = rw_prep(n)
        t0, eG, rg = G[7], G[8], G[9]
        k.act(eG[0:n, :], LW[0:n, :], AF.Exp)
        k.tt(rg[0:n, :], r_, eG[0:n, :], ALU.mult)
        rb, rk2 = sm2[0:n, 16:24], sm2[0:n, 24:32]
        k.tt(t0[0:n, :], r_, Bt[0:n, :], ALU.mult)
        k.red(rb, v3(t0[0:n, :], 8))
        k.tt(t0[0:n, :], r_, K2[0:n, :], ALU.mult)
        k.red(rk2, v3(t0[0:n, :], 8))
        MLkk = ((G[10], G[11]), (G[12], G[13]))
        MLrg = ((G[14], G[15]), (G[16], G[17]))

        def mlv(t):
            return t[:, :].rearrange("p (a s m) -> p a s m", a=2, s=16)

        for src, ML in ((KK, MLkk), (rg, MLrg)):
            b = k.bank()
            for hp in range(4):
                k.tr(b[:, hp * 128:hp * 128 + n], src[0:n, hp * 128:(hp + 1) * 128], cid(n))
            for h2 in range(2):
                p0, p1 = h2 * 64, h2 * 64 + 64
                q0, q1 = (1 - h2) * 64, (1 - h2) * 64 + 64
                for hq in range(2):
                    o4 = mlv(ML[h2][hq])[p0:p1, :, 0:n, 0:n]
                    i4 = v3(b[p0:p1, :], 4)[:, 2 * hq:2 * hq + 2, 0:n]
                    k.tt(o4, bc(i4.unsqueeze(2), [64, 2, n, n]), i16(p0, p1, [64, 2, n, n], 1), ALU.mult)
                    k.memset(ML[h2][hq][q0:q1, :], 0.0)
        b = k.bank()
        for hp in range(4):
            k.tr(b[:, hp * 128:hp * 128 + n], LW[0:n, hp * 128:(hp + 1) * 128], cid(n))
        WT = G[18]
        k.act(v3(WT[:, :], 4)[:, :, 0:n], v3(b[:, :], 4)[:, :, 0:n], AF.Exp)
        MT = G[26]
        Mst = (G[19], G[20])

        def load_state(s_):
            k.dma(v3(MT[0:64, :], 8), st_rw[l, s_].rearrange("h v k -> v h k"))
            b_ = k.bank()
            for hp in range(4):
                k.tr(b_[:, hp * 64:(hp + 1) * 64], MT[0:64, hp * 128:(hp + 1) * 128], cid(64))
            Ms = v3(Mst[s_ % 2][:, 0:256], 4)
            k.cp(Ms, v3(b_[:, 0:256], 4))
            return Ms

        k.nrot = 6
        bKM, bRM = k.banks[6], k.banks[7]
        k.memset(bKM[0:n, :], 0.0)
        k.memset(bRM[0:n, :], 0.0)
        for s_ in range(n):
            Ms = load_state(s_)
            for h in range(8):
                hp, h2 = h // 2, h % 2
                for ML, bk in ((MLkk, bKM), (MLrg, bRM)):
                    lt = mlv(ML[h2][hp // 2])[:, hp % 2, s_, 0:n]
                    k.mm(bk[0:n, h * 64:(h + 1) * 64], lt, Ms[:, hp, :], start=False, stop=True, skip=True)
        U, o_, t1 = G[21], G[22], G[23]
        k.ts(U[0:n, :], bKM[0:n, :], -1.0, ALU.mult)
        k.tt(v3(o_[0:n, :], 8), v3(U[0:n, :], 8), bc(rb.unsqueeze(2), [n, 8, 64]), ALU.mult)
        k.tt(v3(t1[0:n, :], 8), v3(v_, 8), bc(rk2.unsqueeze(2), [n, 8, 64]), ALU.mult)
        k.tt(o_[0:n, :], o_[0:n, :], t1[0:n, :], ALU.add)
        k.tt(o_[0:n, :], o_[0:n, :], bRM[0:n, :], ALU.add)
        k.nrot = 7
        um, vm, Mo = G[23], G[24], G[25]
        for s_ in range(n):
            Ms = load_state(s_)
            k.ts(um[0:n, :], U[0:n, :], CM[0:n, C_ID, s_:s_ + 1], ALU.mult)
            k.ts(vm[0:n, :], v_, CM[0:n, C_ID, s_:s_ + 1], ALU.mult)
            b = k.bank()
            for hp in range(4):
                cs = slice(hp * 128, (hp + 1) * 128)
                k.mm(b[:, cs], Bt[0:n, cs], um[0:n, cs], start=True, stop=False)
                k.mm(b[:, cs], K2[0:n, cs], vm[0:n, cs], start=False, stop=True)
            Mo3 = v3(Mo[:, 0:256], 4)
            for hp in range(4):
                for h2 in range(2):
                    pr = slice(h2 * 64, h2 * 64 + 64)
                    k.stt(Mo3[pr, hp, :], Ms[pr, hp, :], v3(WT[:, :], 4)[pr, hp, s_:s_ + 1],
                          b[pr, hp * 128 + h2 * 64: hp * 128 + (h2 + 1) * 64], ALU.mult, ALU.add)
            b2 = k.bank()
            for hp in range(4):
                k.tr(b2[0:64, hp * 128:(hp + 1) * 128], Mo3[:, hp, :], cid(128))
            so = stage[s_ % 2][0:64, 0:512]
            k.cp(so, b2[0:64, :])
            k.dma(s_rw[l, s_].rearrange("h v k -> v h k"), v3(so, 8), q="pool")
        rw_out(n, SEQ, o_, v_, rz, bsum, pc)

    def rw_out(n, row, src, v_, rz, bsum, pc):
        run_pre_tail()
        on, xc, sq, zs = G[0], G[25], G[2], G[3]
        if src is None:
            k.cp(on[0:n, :], PSO[0:n, :])
        else:
            on = src
        k.red(sm[0:n, 96:104], v3(on[0:n, :], 8))
        k.ts(sm[0:n, 96:104], sm[0:n, 96:104], 1.0 / 64.0, ALU.mult)
        k.tt(v3(xc[0:n, :], 8), v3(on[0:n, :], 8), bc(sm[0:n, 96:104].unsqueeze(2), [n, 8, 64]), ALU.subtract)
        k.tt(sq[0:n, :], xc[0:n, :], xc[0:n, :], ALU.mult)
        k.red(sm[0:n, 104:112], v3(sq[0:n, :], 8))
        k.act(sm[0:n, 112:120], sm[0:n, 104:112], AF.Ln, bias=64e-5, scale=1.0 / 64.0)
        k.act(sm[0:n, 120:128], sm[0:n, 112:120], AF.Exp, scale=-0.5)
        k.tt(v3(xc[0:n, :], 8), v3(xc[0:n, :], 8), bc(sm[0:n, 120:128].unsqueeze(2), [n, 8, 64]), ALU.mult)
        k.tt(xc[0:n, :], xc[0:n, :], pc("lw"), ALU.mult)
        k.tt(xc[0:n, :], xc[0:n, :], pc("lb"), ALU.add)
        k.tt(v3(sq[0:n, :], 8), v3(v_, 8), bc(bsum.unsqueeze(2), [n, 8, 64]), ALU.mult)
        k.tt(xc[0:n, :], xc[0:n, :], sq[0:n, :], ALU.add)
        k.act(zs[0:n, :], rz, AF.Silu)
        k.tt(xc[0:n, :], xc[0:n, :], zs[0:n, :], ALU.mult)
        k.dma(omix[row:row + n, 1024:1536], xc[0:n, :], q="pool")
        run_post_tail()

    def out_phase(l):
        if k.preloaded != (l, "out"):
            load_weights(l, 0, D, 0, rows_kc=12, src=w_out)
        last_layer = (l == LAYERS - 1)
        if last_layer:
            bload(PC[:, 0:D], final_norm_g[0:1, :], D)
        extract_mod("prompt", 128, 2 * D, D)
        for si, seg in enumerate(psegs + [sbatch]):
            kind, idx, n, row = seg
            xt = XB[si % 2]
            seg_mod(seg, 2 * D, D)
            k.dma(xt[0:n, :], xsrc(l, seg))
            k.dma(stage[0][0:n, 0:1024], omix[row:row + n, 0:1024])
            k.dma(stage[1][0:n, 0:512], omix[row:row + n, 1024:1536])
            k.cp(ob[0:n, 0:1024], stage[0][0:n, 0:1024], eng="act")
            k.cp(ob[0:n, 1024:1536], stage[1][0:n, 0:512], eng="dve")
            for grp, (k0, k1) in enumerate(((0, 8), (8, 12))):
                b = k.bank()
                bb = b[:, :].bitcast(BF16)
                for kc in range(k0, k1):
                    k.tr(bb[:, (kc - k0) * 128:(kc - k0) * 128 + n], ob[0:n, kc * 128:(kc + 1) * 128], identb[0:n, 0:n])
                k.cp(oT[:, k0:k1, 0:n], v3(bb, 8)[:, 0:k1 - k0, 0:n])
            for c in range(2):
                b = k.bank()
                for kc in range(12):
                    k.mm(b[0:n, :], oT[:, kc, 0:n], W[:, kc * D + c * 512: kc * D + (c + 1) * 512],
                         start=(kc == 0), stop=(kc == 11), rkey=f"W{kc}")
                g = G[c]
                k.tt(g[0:n, :], b[0:n, :], modseg[0:n, c * 512:(c + 1) * 512], ALU.mult)
                k.tt(xt[0:n, c * 512:(c + 1) * 512], xt[0:n, c * 512:(c + 1) * 512], g[0:n, :], ALU.add)
            if not last_layer:
                k.dma(xmid[row:row + n, :], xt[0:n, :], q="pool")
            else:
                yt = (cq, PB[0])[si % 2][0:n, 0:D]
                k.memset(sm[0:n, 0:1], 0.0)
                k.e("act", "activation", out=yt, in_=xt[0:n, :], func=AF.Square, accum_out=sm[0:n, 0:1])
                k.act(sm[0:n, 1:2], sm[0:n, 0:1], AF.Ln, bias=EPS, scale=1.0 / D)
                k.act(sm[0:n, 2:3], sm[0:n, 1:2], AF.Exp, scale=-0.5)
                k.stt(yt, xt[0:n, :], sm[0:n, 2:3], PC[0:n, 0:D], ALU.mult, ALU.mult)
                dst = y_p[idx * 128:(idx + 1) * 128, :] if kind == "p" else y_s[0:NS, :]
                k.dma(dst, yt, q="pool")

    for l in range(LAYERS):
        layer_mod(l)
        if "dn" in PHASES:
            dn_phase(l)
        if "gla" in PHASES:
            gla_phase(l)
        if "rw" in PHASES:
            rw_phase(l)
        if "out" in PHASES:
            out_phase(l)
    k.S.finish()
    return nc, k


_WNAMES = ["norm_g", "ada_w", "ada_b", "w_in", "dn_conv_w", "dn_a_log", "dn_dt_bias", "dn_norm_g", "gla_wf",
           "gla_bf", "gla_norm_g", "rw_mu", "rw_w0", "rw_w2", "rw_a0", "rw_a2", "rw_k_k", "rw_k_a", "rw_r_k",
           "rw_ln_w", "rw_ln_b", "w_out", "final_norm_g"]


def make_in_maps(inputs, SEQ, NS=NSAMP):
    f = lambda a: np.ascontiguousarray(np.asarray(a, dtype=np.float32))
    shared = {nm: f(inputs[nm]) for nm in _WNAMES}
    shared["rw_r_k"] = shared["rw_r_k"].reshape(2, 512)
    shared["final_norm_g"] = shared["final_norm_g"].reshape(1, D)
    shared["cm"] = make_consts(NS)
    maps = []
    for c in range(NCORES):
        sl = slice(c * NS, (c + 1) * NS)
        m = dict(shared)
        m["xp"] = f(inputs["x_prompt"][c, :SEQ])
        m["xs"] = f(inputs["x_sample"][sl, 0])
        m["call"] = f(np.concatenate([inputs["c_sample"][sl], inputs["c_prompt"][c:c + 1]], axis=0))
        m["st_conv"] = f(inputs["state_dn_conv"][:, sl])
        m["st_dn"] = f(inputs["state_dn"][:, sl])
        m["st_gla"] = f(inputs["state_gla"][:, sl])
        m["st_rs"] = f(inputs["state_rwkv_shift"][:, sl])
        m["st_rw"] = f(inputs["state_rwkv"][:, sl])
        maps.append(m)
    return maps


def gather(res, SEQ):
    R = res.results
    cat = lambda nm, ax: np.concatenate([np.asarray(r[nm]) for r in R], axis=ax)
    stk = lambda nm: np.stack([np.asarray(r[nm]) for r in R], axis=1)
    y_p = np.stack([np.asarray(r["y_p"]) for r in R], axis=0)
    y_s = cat("y_s", 0)[:, None, :]
    return (y_p.astype(np.float32), y_s.astype(np.float32), stk("p_conv"), stk("p_dn"), stk("p_gla"), stk("p_rs"),
            stk("p_rw"), cat("s_conv", 1), cat("s_dn", 1), cat("s_gla", 1), cat("s_rs", 1), cat("s_rw", 1))


_CACHE = {}


def kernel(**inputs):
    SEQ = int(np.asarray(inputs["x_prompt"]).shape[1])
    if SEQ not in _CACHE:
        _CACHE[SEQ] = build(SEQ=SEQ)[0]
    nc = _CACHE[SEQ]
    maps = make_in_maps(inputs, SEQ)
    res = run_bass_kernel_spmd(nc, maps, core_ids=list(range(NCORES)))
    return gather(res, SEQ)
```
